# Optimizing a Trainium2 kernel written in Bass

```python
import math
import jax, jax.numpy as jnp
from jax import lax
import numpy as np

D_MODEL = 1024
BATCH = 32
SEQ = 2048
DEPTH = 4
DEC_BATCH = 32
DEC_SEQ = 32
PAST_LEN = 1024

CHUNK = 64
Q_BLOCK = 128
N_EVEN = (DEPTH + 1) // 2
N_ODD = DEPTH // 2
EPS = 1e-6
L2_EPS = 1e-6
NEG_INF = -1e30

MLA_HEADS = 8
MLA_NOPE = 64
MLA_ROPE = 32
MLA_V = 64
MLA_Q_RANK = 256
MLA_KV_RANK = 128
MLA_SCALE = (MLA_NOPE + MLA_ROPE) ** -0.5
ROPE_THETA = 10000.0
MLA_COLS = MLA_Q_RANK + MLA_KV_RANK + MLA_ROPE

RWKV_HEADS = 8
RWKV_HEAD = 64
RWKV_DIM = RWKV_HEADS * RWKV_HEAD
RWKV_DECAY_LORA = 64
RWKV_A_LORA = 64
RWKV_GATE_LORA = 128
RWKV_COLS = 3 * RWKV_DIM + RWKV_DECAY_LORA + RWKV_A_LORA + RWKV_GATE_LORA
RWKV_LN_EPS = 64e-5

EVEN_IN = MLA_COLS + RWKV_COLS
EVEN_MIX = MLA_HEADS * MLA_V + RWKV_DIM

POOL_GROUPS = 4
POOL_GROUP_DIM = 128
POOL_DIM = POOL_GROUPS * POOL_GROUP_DIM
POOL_WINDOWS = (2, 4, 8, 16)
POOL_HIST = 15

GDN_HEADS = 4
GDN_DK = 128
GDN_DV = 128
GDN_CONV = 4
GDN_QKV = GDN_HEADS * (2 * GDN_DK + GDN_DV)
GDN_DIM = GDN_HEADS * GDN_DV

ODD_IN = POOL_DIM + GDN_QKV + GDN_DIM + 2 * GDN_HEADS
ODD_MIX = POOL_DIM + GDN_DIM

D_FF = 2816
FFN_CONV = 3

kernel_name = 'hybrid_streaming_mla_rwkv7_pool_gdn_step'


def rms_norm(x, gain, eps=EPS):
    xf = x.astype(jnp.float32)
    y = xf * lax.rsqrt(jnp.mean(xf * xf, axis=-1, keepdims=True) + eps)
    return (y * gain.astype(jnp.float32)).astype(x.dtype)


def l2_normalize(x, eps=L2_EPS):
    xf = x.astype(jnp.float32)
    return (xf * lax.rsqrt(jnp.sum(xf * xf, axis=-1, keepdims=True) + eps)).astype(x.dtype)


def head_layer_norm(y, gain, bias, eps=RWKV_LN_EPS):
    B, T, H, N = y.shape
    yf = y.astype(jnp.float32)
    mu = jnp.mean(yf, axis=-1, keepdims=True)
    var = jnp.mean(jnp.square(yf - mu), axis=-1, keepdims=True)
    yn = ((yf - mu) * lax.rsqrt(var + eps)).reshape(B, T, H * N)
    return (yn * gain.astype(jnp.float32) + bias.astype(jnp.float32)).astype(y.dtype)


def modulate(x, gain, shift, scale):
    return rms_norm(x, gain) * (1.0 + scale[:, None, :]) + shift[:, None, :]


def rope_tables(pos):
    inv = 1.0 / (ROPE_THETA ** (jnp.arange(0, MLA_ROPE, 2, dtype=jnp.float32) / MLA_ROPE))
    ang = pos.astype(jnp.float32)[:, None] * inv[None, :]
    return jnp.cos(ang), jnp.sin(ang)


def apply_rope(x, cos, sin):
    x1, x2 = jnp.split(x.astype(jnp.float32), 2, axis=-1)
    return jnp.concatenate([x1 * cos - x2 * sin, x1 * sin + x2 * cos], axis=-1).astype(x.dtype)


def causal_dwconv(u, hist, w):
    T = u.shape[1]
    pad = jnp.concatenate([hist.astype(u.dtype), u], axis=1)
    out = w[0] * pad[:, 0:T]
    for i in range(1, w.shape[0]):
        out = out + w[i] * pad[:, i:i + T]
    return out, pad[:, T:]


def mla_attend(q_nope, q_pe, k_nope, k_pe, v, mask):
    s = (jnp.einsum('bqhd,bkhd->bhqk', q_nope, k_nope, preferred_element_type=jnp.float32)
         + jnp.einsum('bqhr,bkr->bhqk', q_pe, k_pe, preferred_element_type=jnp.float32)) * MLA_SCALE
    if mask is not None:
        s = jnp.where(mask, s, NEG_INF)
    p = jax.nn.softmax(s, axis=-1)
    return jnp.einsum('bhqk,bkhd->bqhd', p.astype(v.dtype), v)


def rwkv7_scan(S0, r, decay, k, v, kk, a):
    f32 = jnp.float32
    xs = tuple(jnp.moveaxis(t.astype(f32), 1, 0) for t in (r, decay, k, v, kk, a))

    def step(S, inp):
        r_t, w_t, k_t, v_t, kk_t, a_t = inp
        sa = jnp.einsum('bhvk,bhk->bhv', S, -kk_t)
        S = (S * w_t[:, :, None, :] + sa[..., None] * (kk_t * a_t)[:, :, None, :]
             + v_t[..., None] * k_t[:, :, None, :])
        return S, jnp.einsum('bhvk,bhk->bhv', S, r_t)

    S, ys = lax.scan(step, S0.astype(f32), xs)
    return jnp.moveaxis(ys, 0, 1).astype(r.dtype), S.astype(S0.dtype)


def gated_delta(S0, q, k, v, beta, g):
    f32 = jnp.float32
    B, T, H, _ = q.shape
    C = T if T <= CHUNK else CHUNK
    n = T // C

    def to_chunks(t):
        t = t.astype(f32).reshape((B, n, C) + t.shape[2:])
        return jnp.moveaxis(jnp.moveaxis(t, 1, 0), 3, 2)

    idx = jnp.arange(C)
    incl = idx[:, None] >= idx[None, :]
    strict = idx[:, None] > idx[None, :]
    eye = jnp.eye(C, dtype=f32)

    def step(S, inp):
        qc, kc, vc, bc, gc = inp
        G = jnp.cumsum(gc, axis=-1)
        diff = G[..., :, None] - G[..., None, :]
        dec_incl = jnp.where(incl, jnp.exp(jnp.where(incl, diff, 0.0)), 0.0)
        dec_strict = jnp.where(strict, dec_incl, 0.0)
        A = bc[..., None] * jnp.einsum('bhik,bhjk->bhij', kc, kc) * dec_strict
        eG = jnp.exp(G)
        rhs = bc[..., None] * (vc - eG[..., None] * jnp.einsum('bhck,bhkv->bhcv', kc, S))
        U = lax.linalg.triangular_solve(A + eye, rhs, left_side=True, lower=True, unit_diagonal=True)
        qk = jnp.einsum('bhik,bhjk->bhij', qc, kc) * dec_incl
        o = eG[..., None] * jnp.einsum('bhck,bhkv->bhcv', qc, S) + jnp.einsum('bhij,bhjv->bhiv', qk, U)
        decay_last = jnp.exp(G[..., -1:] - G)
        S = (jnp.exp(G[..., -1])[..., None, None] * S
             + jnp.einsum('bhck,bhcv->bhkv', kc * decay_last[..., None], U))
        return S, o

    S, o = lax.scan(step, S0.astype(f32), tuple(to_chunks(t) for t in (q, k, v, beta, g)))
    o = jnp.swapaxes(jnp.moveaxis(o, 0, 1), 2, 3).reshape(B, T, H, v.shape[-1])
    return o.astype(v.dtype), S.astype(S0.dtype)


def pool_mix(u, hist, pos0, pool_w, pool_scale):
    B, T, _ = u.shape
    f32 = jnp.float32
    xp = jnp.concatenate([hist.astype(u.dtype), u], axis=1)
    cs = jnp.concatenate([jnp.zeros((B, 1, POOL_DIM), f32), jnp.cumsum(xp.astype(f32), axis=1)], axis=1)
    end = cs[:, POOL_HIST + 1:]
    pos = pos0 + jnp.arange(T, dtype=jnp.int32)
    means = []
    for gi, w in enumerate(POOL_WINDOWS):
        sl = slice(gi * POOL_GROUP_DIM, (gi + 1) * POOL_GROUP_DIM)
        start = cs[:, POOL_HIST + 1 - w:POOL_HIST + 1 - w + T, sl]
        cnt = jnp.minimum(w, pos + 1).astype(f32)[None, :, None]
        means.append((end[..., sl] - start) / cnt)
    mean = jnp.stack(means, axis=2)
    diff = (mean - u.astype(f32).reshape(B, T, POOL_GROUPS, POOL_GROUP_DIM)).astype(u.dtype)
    y = jnp.einsum('btgc,gcd->btgd', diff, pool_w).reshape(B, T, POOL_DIM) * pool_scale
    return y, xp[:, T:]


def even_mixer(h, pos0, past_ckv, past_kpe, shift_state, wkv_state, W, i):
    B, T, _ = h.shape
    p = h @ W['even_w_in'][i]
    q_lat, kv_lat, kr_raw, rw = jnp.split(p, [MLA_Q_RANK, MLA_Q_RANK + MLA_KV_RANK, MLA_COLS], axis=-1)

    cos, sin = rope_tables(pos0 + jnp.arange(T, dtype=jnp.int32))
    q = (rms_norm(q_lat, W['mla_g_qlat'][i]) @ W['mla_w_uq'][i]).reshape(B, T, MLA_HEADS, MLA_NOPE + MLA_ROPE)
    q_nope = rms_norm(q[..., :MLA_NOPE], W['mla_g_qn'][i])
    q_pe = apply_rope(rms_norm(q[..., MLA_NOPE:], W['mla_g_qr'][i]), cos[:, None, :], sin[:, None, :])
    ckv_new = rms_norm(kv_lat, W['mla_g_kvlat'][i])
    kpe_new = apply_rope(rms_norm(kr_raw, W['mla_g_kr'][i]), cos, sin)
    if past_ckv is None:
        ckv_all, kpe_all = ckv_new, kpe_new
    else:
        ckv_all = jnp.concatenate([past_ckv.astype(ckv_new.dtype), ckv_new], axis=1)
        kpe_all = jnp.concatenate([past_kpe.astype(kpe_new.dtype), kpe_new], axis=1)
    L = ckv_all.shape[1]
    kv = (ckv_all @ W['mla_w_ukv'][i]).reshape(B, L, MLA_HEADS, MLA_NOPE + MLA_V)
    k_nope = rms_norm(kv[..., :MLA_NOPE], W['mla_g_kn'][i])
    v_mla = kv[..., MLA_NOPE:]
    if past_ckv is None:
        nb = T // Q_BLOCK
        qn_b = q_nope.reshape(B, nb, Q_BLOCK, MLA_HEADS, MLA_NOPE).transpose(1, 0, 2, 3, 4)
        qp_b = q_pe.reshape(B, nb, Q_BLOCK, MLA_HEADS, MLA_ROPE).transpose(1, 0, 2, 3, 4)
        k_chunk = jnp.arange(L) // CHUNK

        def block(args):
            qn_i, qp_i, bi = args
            q_chunk = (bi * Q_BLOCK + jnp.arange(Q_BLOCK)) // CHUNK
            mask = k_chunk[None, :] <= q_chunk[:, None]
            return mla_attend(qn_i, qp_i, k_nope, kpe_all, v_mla, mask)

        o_mla = lax.map(block, (qn_b, qp_b, jnp.arange(nb)))
        o_mla = o_mla.transpose(1, 0, 2, 3, 4).reshape(B, T, MLA_HEADS * MLA_V)
    else:
        o_mla = mla_attend(q_nope, q_pe, k_nope, kpe_all, v_mla, None).reshape(B, T, MLA_HEADS * MLA_V)

    prev = jnp.concatenate([shift_state[:, None, :].astype(rw.dtype), rw[:, :-1]], axis=1)
    xm = rw + (prev - rw) * W['rwkv_mu'][i]
    c0 = 3 * RWKV_DIM
    r, k, v, dw, da, dg = jnp.split(
        xm, [RWKV_DIM, 2 * RWKV_DIM, c0, c0 + RWKV_DECAY_LORA, c0 + RWKV_DECAY_LORA + RWKV_A_LORA], axis=-1)
    w_log = -jax.nn.softplus(-(W['rwkv_w0'][i] + jnp.tanh(dw) @ W['rwkv_w2'][i])) - 0.5
    a = jax.nn.sigmoid(W['rwkv_a0'][i] + da @ W['rwkv_a2'][i])
    g = jax.nn.sigmoid(dg) @ W['rwkv_g2'][i]

    def heads(t):
        return t.reshape(B, T, RWKV_HEADS, RWKV_HEAD)

    kk = l2_normalize(heads(k * W['rwkv_k_k'][i]))
    k = k * (1.0 + (a - 1.0) * W['rwkv_k_a'][i])
    r_h, k_h, v_h, a_h = heads(r), heads(k), heads(v), heads(a)
    decay = jnp.exp(-jnp.exp(heads(w_log).astype(jnp.float32)))
    y, wkv_new = rwkv7_scan(wkv_state, r_h, decay, k_h, v_h, kk, a_h)
    y = head_layer_norm(y, W['rwkv_lnx_g'][i], W['rwkv_lnx_b'][i])
    bonus = jnp.sum(r_h * k_h * W['rwkv_r_k'][i], axis=-1, keepdims=True) * v_h
    y = (y + bonus.reshape(B, T, RWKV_DIM)) * g

    out = jnp.concatenate([o_mla, y], axis=-1) @ W['even_w_out'][i]
    return out, ckv_new, kpe_new, rw[:, -1], wkv_new


def odd_mixer(h, pos0, pool_hist, conv_hist, gdn_state, W, i):
    B, T, _ = h.shape
    p = h @ W['odd_w_in'][i]
    c1 = POOL_DIM + GDN_QKV
    u, qkv, z, b, a = jnp.split(p, [POOL_DIM, c1, c1 + GDN_DIM, c1 + GDN_DIM + GDN_HEADS], axis=-1)

    y_pool, pool_new = pool_mix(u, pool_hist, pos0, W['pool_w'][i], W['pool_scale'][i])

    qkv_c, conv_new = causal_dwconv(qkv, conv_hist, W['gdn_conv_w'][i])
    qkv_c = jax.nn.silu(qkv_c)
    q, k, v = jnp.split(qkv_c, [GDN_HEADS * GDN_DK, 2 * GDN_HEADS * GDN_DK], axis=-1)
    q = l2_normalize(q.reshape(B, T, GDN_HEADS, GDN_DK)) * (GDN_DK ** -0.5)
    k = l2_normalize(k.reshape(B, T, GDN_HEADS, GDN_DK))
    v = v.reshape(B, T, GDN_HEADS, GDN_DV)
    beta = jax.nn.sigmoid(b.astype(jnp.float32))
    g = -jnp.exp(W['gdn_a_log'][i].astype(jnp.float32)) * jax.nn.softplus(
        a.astype(jnp.float32) + W['gdn_dt_bias'][i])
    o, gdn_new = gated_delta(gdn_state, q, k, v, beta, g)
    o = rms_norm(o, W['gdn_o_g'][i]) * jax.nn.silu(z.reshape(B, T, GDN_HEADS, GDN_DV))

    out = jnp.concatenate([y_pool, o.reshape(B, T, GDN_DIM)], axis=-1) @ W['odd_w_out'][i]
    return out, pool_new, conv_new, gdn_new


def conv_ffn(h, conv_hist, w_gate, w_up, conv_w, w_down):
    a, hist_new = causal_dwconv(h @ w_gate, conv_hist, conv_w)
    return (jax.nn.silu(a) * (h @ w_up)) @ w_down, hist_new


def trunk(x, c, pos0, caches, W):
    B = x.shape[0]
    dt = x.dtype
    if caches is None:
        ckv_c = kpe_c = None
        shift_c = jnp.zeros((N_EVEN, B, RWKV_COLS), dt)
        wkv_c = jnp.zeros((N_EVEN, B, RWKV_HEADS, RWKV_HEAD, RWKV_HEAD), dt)
        pool_c = jnp.zeros((N_ODD, B, POOL_HIST, POOL_DIM), dt)
        conv_c = jnp.zeros((N_ODD, B, GDN_CONV - 1, GDN_QKV), dt)
        gdn_c = jnp.zeros((N_ODD, B, GDN_HEADS, GDN_DK, GDN_DV), dt)
        ffn_c = jnp.zeros((DEPTH, B, FFN_CONV - 1, D_FF), dt)
    else:
        ckv_c, kpe_c, shift_c, wkv_c, pool_c, conv_c, gdn_c, ffn_c = caches
    mod = jnp.einsum('bd,lde->lbe', jax.nn.silu(c), W['ada_w']) + W['ada_b'][:, None, :]
    o_ckv, o_kpe, o_shift, o_wkv, o_pool, o_conv, o_gdn, o_ffn = [], [], [], [], [], [], [], []
    for layer in range(DEPTH):
        sh_m, sc_m, gt_m, sh_f, sc_f, gt_f = jnp.split(mod[layer], 6, axis=-1)
        h = modulate(x, W['norm_mix_g'][layer], sh_m, sc_m)
        i = layer // 2
        if layer % 2 == 0:
            y, ckv, kpe, sh, wkv = even_mixer(
                h, pos0, None if ckv_c is None else ckv_c[i], None if kpe_c is None else kpe_c[i],
                shift_c[i], wkv_c[i], W, i)
            o_ckv.append(ckv)
            o_kpe.append(kpe)
            o_shift.append(sh)
            o_wkv.append(wkv)
        else:
            y, pl, cv, gs = odd_mixer(h, pos0, pool_c[i], conv_c[i], gdn_c[i], W, i)
            o_pool.append(pl)
            o_conv.append(cv)
            o_gdn.append(gs)
        x = x + gt_m[:, None, :] * y
        h = modulate(x, W['norm_ffn_g'][layer], sh_f, sc_f)
        f, fc = conv_ffn(h, ffn_c[layer], W['ffn_w_gate'][layer], W['ffn_w_up'][layer],
                         W['ffn_conv_w'][layer], W['ffn_w_down'][layer])
        o_ffn.append(fc)
        x = x + gt_f[:, None, :] * f
    st = jnp.stack
    return x, (st(o_ckv), st(o_kpe), st(o_shift), st(o_wkv), st(o_pool), st(o_conv), st(o_gdn), st(o_ffn))


def setup_inputs(seed: int = 0) -> dict:
    key = jax.random.key(seed)
    ks = iter(jax.random.split(key, 80))
    f32 = jnp.float32

    def nrm(shape, scale=1.0):
        return jax.random.normal(next(ks), shape, f32) * scale

    def unif(shape, lo, hi):
        return jax.random.uniform(next(ks), shape, f32, lo, hi)

    def gain(shape, s=0.05):
        return 1.0 + nrm(shape, s)

    D = D_MODEL
    inp = {}
    inp['x_prompt'] = nrm((BATCH, SEQ, D))
    inp['x_sample'] = nrm((DEC_BATCH, DEC_SEQ, D))
    inp['cache_mla_ckv'] = nrm((N_EVEN, DEC_BATCH, PAST_LEN, MLA_KV_RANK))
    inp['cache_mla_kpe'] = nrm((N_EVEN, DEC_BATCH, PAST_LEN, MLA_ROPE))
    inp['state_rwkv_shift'] = nrm((N_EVEN, DEC_BATCH, RWKV_COLS))
    inp['state_rwkv_wkv'] = nrm((N_EVEN, DEC_BATCH, RWKV_HEADS, RWKV_HEAD, RWKV_HEAD), 0.3)
    inp['state_pool'] = nrm((N_ODD, DEC_BATCH, POOL_HIST, POOL_DIM))
    inp['state_gdn_conv'] = nrm((N_ODD, DEC_BATCH, GDN_CONV - 1, GDN_QKV))
    inp['state_gdn'] = nrm((N_ODD, DEC_BATCH, GDN_HEADS, GDN_DK, GDN_DV), 0.1)
    inp['state_ffn_conv'] = nrm((DEPTH, DEC_BATCH, FFN_CONV - 1, D_FF))
    inp['c_prompt'] = nrm((BATCH, D))
    inp['c_sample'] = nrm((DEC_BATCH, D))
    inp['ada_w'] = nrm((DEPTH, D, 6 * D), 0.5 * D ** -0.5)
    inp['ada_b'] = nrm((DEPTH, 6 * D), 0.01)
    inp['norm_mix_g'] = gain((DEPTH, D))
    inp['norm_ffn_g'] = gain((DEPTH, D))
    inp['even_w_in'] = nrm((N_EVEN, D, EVEN_IN), D ** -0.5)
    inp['mla_g_qlat'] = gain((N_EVEN, MLA_Q_RANK))
    inp['mla_g_kvlat'] = gain((N_EVEN, MLA_KV_RANK))
    inp['mla_w_uq'] = nrm((N_EVEN, MLA_Q_RANK, MLA_HEADS * (MLA_NOPE + MLA_ROPE)), MLA_Q_RANK ** -0.5)
    inp['mla_w_ukv'] = nrm((N_EVEN, MLA_KV_RANK, MLA_HEADS * (MLA_NOPE + MLA_V)), MLA_KV_RANK ** -0.5)
    inp['mla_g_qn'] = gain((N_EVEN, MLA_NOPE))
    inp['mla_g_qr'] = gain((N_EVEN, MLA_ROPE))
    inp['mla_g_kn'] = gain((N_EVEN, MLA_NOPE))
    inp['mla_g_kr'] = gain((N_EVEN, MLA_ROPE))
    inp['rwkv_mu'] = unif((N_EVEN, RWKV_COLS), 0.0, 1.0)
    inp['rwkv_w0'] = unif((N_EVEN, RWKV_DIM), -6.0, 0.0)
    inp['rwkv_w2'] = nrm((N_EVEN, RWKV_DECAY_LORA, RWKV_DIM), 0.5 * RWKV_DECAY_LORA ** -0.5)
    inp['rwkv_a0'] = nrm((N_EVEN, RWKV_DIM), 0.5)
    inp['rwkv_a2'] = nrm((N_EVEN, RWKV_A_LORA, RWKV_DIM), 0.5 * RWKV_A_LORA ** -0.5)
    inp['rwkv_g2'] = nrm((N_EVEN, RWKV_GATE_LORA, RWKV_DIM), RWKV_GATE_LORA ** -0.5)
    inp['rwkv_k_k'] = 0.85 + nrm((N_EVEN, RWKV_DIM), 0.05)
    inp['rwkv_k_a'] = gain((N_EVEN, RWKV_DIM))
    inp['rwkv_r_k'] = nrm((N_EVEN, RWKV_HEADS, RWKV_HEAD), 0.1)
    inp['rwkv_lnx_g'] = gain((N_EVEN, RWKV_DIM))
    inp['rwkv_lnx_b'] = nrm((N_EVEN, RWKV_DIM), 0.01)
    inp['even_w_out'] = nrm((N_EVEN, EVEN_MIX, D), EVEN_MIX ** -0.5)
    inp['odd_w_in'] = nrm((N_ODD, D, ODD_IN), D ** -0.5)
    inp['pool_w'] = nrm((N_ODD, POOL_GROUPS, POOL_GROUP_DIM, POOL_GROUP_DIM), POOL_GROUP_DIM ** -0.5)
    inp['pool_scale'] = gain((N_ODD, POOL_DIM), 0.1)
    inp['gdn_conv_w'] = nrm((N_ODD, GDN_CONV, GDN_QKV), GDN_CONV ** -0.5)
    inp['gdn_a_log'] = jnp.log(unif((N_ODD, GDN_HEADS), 1.0, 16.0))
    dt0 = jnp.exp(unif((N_ODD, GDN_HEADS), math.log(1e-3), math.log(1e-1)))
    inp['gdn_dt_bias'] = dt0 + jnp.log(-jnp.expm1(-dt0))
    inp['gdn_o_g'] = gain((N_ODD, GDN_DV))
    inp['odd_w_out'] = nrm((N_ODD, ODD_MIX, D), ODD_MIX ** -0.5)
    inp['ffn_w_gate'] = nrm((DEPTH, D, D_FF), D ** -0.5)
    inp['ffn_w_up'] = nrm((DEPTH, D, D_FF), D ** -0.5)
    inp['ffn_conv_w'] = nrm((DEPTH, FFN_CONV, D_FF), FFN_CONV ** -0.5)
    inp['ffn_w_down'] = nrm((DEPTH, D_FF, D), D_FF ** -0.5)
    return inp


def reference(x_prompt, x_sample, cache_mla_ckv, cache_mla_kpe, state_rwkv_shift, state_rwkv_wkv,
              state_pool, state_gdn_conv, state_gdn, state_ffn_conv, c_prompt, c_sample,
              ada_w, ada_b, norm_mix_g, norm_ffn_g,
              even_w_in, mla_g_qlat, mla_g_kvlat, mla_w_uq, mla_w_ukv, mla_g_qn, mla_g_qr, mla_g_kn, mla_g_kr,
              rwkv_mu, rwkv_w0, rwkv_w2, rwkv_a0, rwkv_a2, rwkv_g2, rwkv_k_k, rwkv_k_a, rwkv_r_k,
              rwkv_lnx_g, rwkv_lnx_b, even_w_out,
              odd_w_in, pool_w, pool_scale, gdn_conv_w, gdn_a_log, gdn_dt_bias, gdn_o_g, odd_w_out,
              ffn_w_gate, ffn_w_up, ffn_conv_w, ffn_w_down):
    W = {
        'ada_w': ada_w, 'ada_b': ada_b, 'norm_mix_g': norm_mix_g, 'norm_ffn_g': norm_ffn_g,
        'even_w_in': even_w_in, 'mla_g_qlat': mla_g_qlat, 'mla_g_kvlat': mla_g_kvlat,
        'mla_w_uq': mla_w_uq, 'mla_w_ukv': mla_w_ukv, 'mla_g_qn': mla_g_qn, 'mla_g_qr': mla_g_qr,
        'mla_g_kn': mla_g_kn, 'mla_g_kr': mla_g_kr,
        'rwkv_mu': rwkv_mu, 'rwkv_w0': rwkv_w0, 'rwkv_w2': rwkv_w2, 'rwkv_a0': rwkv_a0, 'rwkv_a2': rwkv_a2,
        'rwkv_g2': rwkv_g2, 'rwkv_k_k': rwkv_k_k, 'rwkv_k_a': rwkv_k_a, 'rwkv_r_k': rwkv_r_k,
        'rwkv_lnx_g': rwkv_lnx_g, 'rwkv_lnx_b': rwkv_lnx_b, 'even_w_out': even_w_out,
        'odd_w_in': odd_w_in, 'pool_w': pool_w, 'pool_scale': pool_scale, 'gdn_conv_w': gdn_conv_w,
        'gdn_a_log': gdn_a_log, 'gdn_dt_bias': gdn_dt_bias, 'gdn_o_g': gdn_o_g, 'odd_w_out': odd_w_out,
        'ffn_w_gate': ffn_w_gate, 'ffn_w_up': ffn_w_up, 'ffn_conv_w': ffn_conv_w, 'ffn_w_down': ffn_w_down,
    }
    y_prompt, p_states = trunk(x_prompt, c_prompt, 0, None, W)
    p_mla_ckv, p_mla_kpe, p_rwkv_shift, p_rwkv_wkv, p_pool, p_gdn_conv, p_gdn, p_ffn_conv = p_states
    past_len = cache_mla_ckv.shape[2]
    caches = (cache_mla_ckv, cache_mla_kpe, state_rwkv_shift, state_rwkv_wkv,
              state_pool, state_gdn_conv, state_gdn, state_ffn_conv)
    y_sample, s_states = trunk(x_sample, c_sample, past_len, caches, W)
    s_mla_ckv, s_mla_kpe, s_rwkv_shift, s_rwkv_wkv, s_pool, s_gdn_conv, s_gdn, s_ffn_conv = s_states
    return (y_prompt, y_sample,
            p_mla_ckv, p_mla_kpe, p_rwkv_shift, p_rwkv_wkv, p_pool, p_gdn_conv, p_gdn, p_ffn_conv,
            s_mla_ckv, s_mla_kpe, s_rwkv_shift, s_rwkv_wkv, s_pool, s_gdn_conv, s_gdn, s_ffn_conv)
```

```python
from contextlib import ExitStack
import os
import numpy as np
import concourse.bass as bass
import concourse.mybir as mybir
from concourse.bass_utils import run_bass_kernel_spmd

F32 = mybir.dt.float32
BF16 = mybir.dt.bfloat16
I32 = mybir.dt.int32
ALU = mybir.AluOpType
AF = mybir.ActivationFunctionType
AX = mybir.AxisListType

ENGS = ("pe", "act", "dve", "pool", "sp")
N_DMA_SEMS = 8


class Res:
    __slots__ = ("name", "excl", "last_w", "readers")

    def __init__(self, name, excl=False):
        self.name = name
        self.excl = excl
        self.last_w = None
        self.readers = {}

    def add_reader(self, op):
        self.readers[id(op)] = op


class View:
    __slots__ = ("ap", "res")

    def __init__(self, ap, res):
        self.ap = ap
        self.res = res if isinstance(res, (list, tuple)) else [res]

    def __getitem__(self, idx):
        return View(self.ap[idx], self.res)

    def rearrange(self, *a, **k):
        return View(self.ap.rearrange(*a, **k), self.res)

    def bitcast(self, dt):
        return View(self.ap.bitcast(dt), self.res)

    @property
    def shape(self):
        return self.ap.shape

    def bc3(self, axis, n):
        sh = list(self.ap.shape)
        sh.insert(axis, n)
        return View(self.ap.unsqueeze(axis).to_broadcast(sh), self.res)


class Tile(View):
    pass


class Op:
    __slots__ = ("eng", "fn", "deps", "is_dma", "signal", "cnt", "dsem", "dval", "idx", "tag", "epoch", "odeps", "dur", "succ", "fin", "nun")

    def __init__(self, eng, fn, is_dma, tag=""):
        self.eng = eng
        self.fn = fn
        self.deps = []
        self.is_dma = is_dma
        self.signal = False
        self.cnt = 0
        self.dsem = None
        self.dval = 0
        self.idx = -1
        self.tag = tag
        self.epoch = 0
        self.odeps = []
        self.dur = 0.3
        self.succ = []
        self.fin = None
        self.nun = 0


class Prog:
    def __init__(self, nc):
        self.nc = nc
        self.es = ExitStack()
        self.ops = []
        self.final_ops = []
        self.n_res = 0
        self.bar_op = None
        self.bar_idx = 0
        self.epoch = 0

    def sbuf(self, name, shape, dt=F32):
        self.n_res += 1
        name = f"sb_{name}_{self.n_res}"
        t = self.es.enter_context(self.nc.sbuf_tensor(name, list(shape), dt))
        return Tile(t[:], Res(name))

    def psum(self, name, shape, dt=F32):
        self.n_res += 1
        name = f"ps_{name}_{self.n_res}"
        t = self.es.enter_context(self.nc.psum_tensor(name, list(shape), dt))
        return Tile(t[:], Res(name, excl=True))

    def dram(self, name, shape, dt=F32, kind="Internal"):
        t = self.nc.dram_tensor(name, list(shape), dt, kind=kind)
        return Tile(t.ap(), Res(name))

    def scope(self):
        prog = self

        class _S:
            def __enter__(s2):
                s2.saved = prog.es
                prog.es = ExitStack()
                return s2

            def __exit__(s2, *a):
                prog.es.close()
                prog.es = s2.saved
                prog.barrier()
                return False
        return _S()

    def barrier(self):
        last = {}
        dmas = []
        for op in self.ops[self.bar_idx:]:
            if op.is_dma:
                dmas.append(op)
            else:
                last[op.eng] = op
        op = Op("sp", lambda e: e.nop(), False, "barrier")
        op.deps = list(last.values()) + dmas + ([self.bar_op] if self.bar_op else [])
        self.epoch += 1
        op.epoch = self.epoch
        op.idx = len(self.ops)
        self.ops.append(op)
        self.bar_op = op
        self.bar_idx = len(self.ops)

    def _record(self, op, reads, writes):
        deps = []
        for r in reads:
            if r.excl:
                writes = list(writes) + [r]
                continue
            if r.last_w is not None:
                deps.append(r.last_w)
            r.add_reader(op)
        for w in writes:
            if w.last_w is not None:
                deps.append(w.last_w)
            deps.extend(w.readers.values())
            w.last_w = op
            w.readers = {}
        seen = set()
        if self.bar_op is not None:
            deps.append(self.bar_op)
        op.epoch = self.epoch
        for d in deps:
            if d is op or id(d) in seen:
                continue
            if d.idx < self.bar_idx and d is not self.bar_op:
                continue
            op.odeps.append(d)
            seen.add(id(d))
            if (not d.is_dma) and d.eng == "pe" and op.eng == "pe" and not op.is_dma:
                continue
            op.deps.append(d)
        op.idx = len(self.ops)
        self.ops.append(op)
        return op

    def call(self, eng, meth, *args, tag="", **kw):
        if eng == "pool" and meth == "tensor_scalar" and kw.get("scalar2", 0) is None:
            kw = dict(kw); kw["scalar2"] = 0.0; kw["op1"] = ALU.add
        reads, writes = [], []
        a2 = list(args)
        for i, a in enumerate(a2):
            if isinstance(a, View):
                (writes if (i == 0 and "out" not in kw) else reads).extend(a.res)
                a2[i] = a.ap
        k2 = dict(kw)
        for k, a in kw.items():
            if isinstance(a, View):
                (writes if k in ("out", "accum_out") else reads).extend(a.res)
                k2[k] = a.ap

        def fn(e, a2=a2, k2=k2, meth=meth):
            return getattr(e, meth)(*a2, **k2)

        op = Op(eng, fn, False, tag or meth)
        oap = k2.get("out", a2[0] if a2 else None)
        free = 1
        try:
            for d_ in oap.shape[1:]:
                free *= int(d_)
        except Exception:
            free = 64
        if eng == "pe":
            op.dur = 0.11 + free / 2400.0
        elif eng == "act":
            op.dur = 0.2 + free / 1200.0
        elif eng == "dve":
            op.dur = (0.1 + free * 0.0065) if meth == "reciprocal" else (0.08 + free / 960.0)
        else:
            op.dur = 0.25 + free / 500.0
        return self._record(op, reads, writes)

    def dma(self, eng, out, in_, final=False, **kw):
        reads = list(in_.res) if isinstance(in_, View) else []
        writes = list(out.res) if isinstance(out, View) else []
        oap = out.ap if isinstance(out, View) else out
        iap = in_.ap if isinstance(in_, View) else in_

        def fn(e, oap=oap, iap=iap, kw=kw):
            return e.dma_start(out=oap, in_=iap, **kw)

        op_ = Op(eng, fn, True, "dma")
        nbytes = 4
        try:
            for d_ in oap.shape:
                nbytes *= int(d_)
        except Exception:
            nbytes = 65536
        op_.dur = 2.0 + nbytes / 150000.0
        op = self._record(op_, reads, writes)
        if final:
            self.final_ops.append(op)
        return op

    def __getattr__(self, name):
        if name in ENGS:
            return _EngProxy(self, name)
        raise AttributeError(name)

    def schedule(self):
        import heapq
        W = int(os.environ.get("MK_WIN", "24"))
        XLAT, SLAT = 1.2, 0.1
        ops = self.ops
        per_eng = {e: [] for e in ENGS}
        epochs = {}
        for op in ops:
            epochs.setdefault(op.epoch, []).append(op)
        prev_sched = None
        for ep in sorted(epochs):
            eops = epochs[ep]
            bar = eops[0] if eops[0].tag == "barrier" else None
            if bar is not None and prev_sched is not None:
                last = {}
                dmas = []
                for e in ENGS:
                    for o in prev_sched[e]:
                        if o.is_dma:
                            dmas.append(o)
                        else:
                            last[e] = o
                bar.deps = list(last.values()) + dmas
                bar.odeps = []
            inset = set(id(o) for o in eops)
            for o in eops:
                o.succ = []
                o.fin = None
            for o in eops:
                o.odeps = [d for d in o.odeps if id(d) in inset]
                o.nun = len(o.odeps)
                for d in o.odeps:
                    d.succ.append(o)
            queues = {e: [o for o in eops if o.eng == e] for e in ENGS}
            heads = {e: 0 for e in ENGS}
            done = {e: [] for e in ENGS}
            free = {e: 0.0 for e in ENGS}
            sched = {e: [] for e in ENGS}
            scheduled = set()
            remaining = len(eops)

            def candidate(e):
                q = queues[e]
                i = heads[e]
                best = None
                cnt_ = 0
                n_ = len(q)
                while i < n_ and cnt_ < W:
                    o = q[i]
                    i += 1
                    if o.fin is not None:
                        continue
                    cnt_ += 1
                    if o.nun > 0:
                        continue
                    r = 0.0
                    for d in o.odeps:
                        t = d.fin + (SLAT if (d.eng == e and not d.is_dma) else XLAT)
                        if t > r:
                            r = t
                    st = r if r > free[e] else free[e]
                    if best is None or st < best[0] - 1e-9:
                        best = (st, o)
                        if st <= free[e] + 1e-9:
                            break
                return best
            cand = {e: candidate(e) for e in ENGS}
            while remaining > 0:
                be = None
                for e in ENGS:
                    c = cand[e]
                    if c is not None and (be is None or c[0] < cand[be][0]):
                        be = e
                assert be is not None, "scheduler deadlock"
                st, o = cand[be]
                issue = o.dur
                if o.is_dma:
                    issue = 1.0 if o.eng == "pool" else 0.3
                o.fin = st + o.dur
                free[be] = st + issue
                sched[be].append(o)
                remaining -= 1
                q = queues[be]
                while heads[be] < len(q) and q[heads[be]].fin is not None:
                    heads[be] += 1
                dirty = {be}
                for s_ in o.succ:
                    s_.nun -= 1
                    if s_.nun == 0:
                        dirty.add(s_.eng)
                for e in dirty:
                    cand[e] = candidate(e)
            for e in ENGS:
                per_eng[e].extend(sched[e])
            prev_sched = sched
            self.sim_time = getattr(self, "sim_time", 0.0) + max(free.values())
        return per_eng

    def finish(self):
        nc = self.nc
        ops = self.ops
        do_sched = int(os.environ.get("MK_SCHED", "1"))
        per_eng_s = self.schedule() if do_sched else None
        for op in ops:
            for d in op.deps:
                d.signal = True
        for op in self.final_ops:
            op.signal = True
        cnt = {}
        dcount = {e: 0 for e in ENGS}
        per_eng = {e: [] for e in ENGS}
        if per_eng_s is None:
            for op in ops:
                per_eng[op.eng].append(op)
        else:
            per_eng = per_eng_s
        for op in [o for e in ENGS for o in per_eng[e]]:
            if op.is_dma:
                k = dcount[op.eng]
                dcount[op.eng] += 1
                op.dsem = (op.eng, k % N_DMA_SEMS)
                op.dval = 16 * (k // N_DMA_SEMS + 1)
            elif op.signal:
                ke = (op.epoch, op.eng)
                cnt[ke] = cnt.get(ke, 0) + 1
                op.cnt = cnt[ke]
        es = self.es
        assert max(cnt.values()) < 60000, max(cnt.values())
        esem = {ke: es.enter_context(nc.semaphore(f"s_{ke[1]}_{ke[0]}")) for ke in cnt}
        dsem = {}
        for e in ENGS:
            if dcount[e]:
                for i in range(min(N_DMA_SEMS, dcount[e])):
                    dsem[(e, i)] = es.enter_context(nc.semaphore(f"d_{e}_{i}"))
        block = es.enter_context(nc.Block())
        engmap = {"pe": "tensor", "act": "scalar", "dve": "vector", "pool": "gpsimd", "sp": "sync"}
        final_ops = self.final_ops

        def emit(e_name):
            def body(eng):
                known = {}
                prev_dma = {}
                my_ops = per_eng[e_name]
                for op in my_ops:
                    waits = {}
                    for d in op.deps:
                        if d.is_dma:
                            key, val = ("d",) + d.dsem, d.dval
                        else:
                            key, val = ("e", d.epoch, d.eng), d.cnt
                        if waits.get(key, 0) < val:
                            waits[key] = val
                    if op.is_dma:
                        if op.dval > 16:
                            key = ("d",) + op.dsem
                            if waits.get(key, 0) < op.dval - 16:
                                waits[key] = op.dval - 16
                    for key, val in waits.items():
                        if known.get(key, 0) >= val:
                            continue
                        known[key] = val
                        sem = dsem[key[1:]] if key[0] == "d" else esem[key[1:]]
                        eng.wait_ge(sem, val)
                    ins = op.fn(eng)
                    if op.is_dma:
                        ins.then_inc(dsem[op.dsem], 16)
                    elif op.signal:
                        ins.then_inc(esem[(op.epoch, e_name)], 1)
                if e_name == "sp":
                    waits = {}
                    for op in final_ops:
                        key = ("d",) + op.dsem
                        waits[key] = max(waits.get(key, 0), op.dval)
                    for key, val in waits.items():
                        if known.get(key, 0) < val:
                            eng.wait_ge(dsem[key[1:]], val)
            return body

        for e in ENGS:
            if per_eng[e] or e == "sp":
                getattr(block, engmap[e])(emit(e))
        self.es.close()
        return {e: (len(per_eng[e]), max([v for k, v in cnt.items() if k[1] == e] + [0]), dcount[e]) for e in ENGS}, len(esem)


class _EngProxy:
    def __init__(self, prog, eng):
        self.prog = prog
        self.eng = eng

    def __getattr__(self, meth):
        if meth == "dma_start":
            def f(out, in_, **kw):
                return self.prog.dma(self.eng, out, in_, **kw)
            return f

        def f(*a, **kw):
            return self.prog.call(self.eng, meth, *a, **kw)
        return f


NCORES = 8
D = 1024; KC = 8; DFF = 2816; FC = 22; DEPTH = 4
SEQ = 2048; DSEQ = 32; PAST = 1024; NB = 4
TT = 256
EPS = 1e-6
NLAYERS = int(os.environ.get("MK_LAYERS", "4"))
DO_MIX = int(os.environ.get("MK_MIX", "1"))
MK_NT = int(os.environ.get("MK_NT", str(SEQ // TT)))
MK_NSEQ = int(os.environ.get("MK_NSEQ", "4"))
MK_CORES = int(os.environ.get("MK_CORES", "8"))
ODD_STAGE = int(os.environ.get("MK_ODD", "9"))
SUB = int(os.environ.get("MK_SUB", "9"))


class VecPack:
    def __init__(self):
        self.cols = []
        self.off = {}
        self.n = 0

    def put(self, name, arr):
        arr = np.ascontiguousarray(arr, dtype=np.float32)
        assert arr.shape[0] == 128
        self.off[name] = (self.n, arr.shape[1])
        self.cols.append(arr)
        self.n += arr.shape[1]

    def put_feat(self, name, v):
        v = np.asarray(v, dtype=np.float32)
        self.put(name, v.reshape(-1, 128).T)

    def array(self):
        return np.concatenate(self.cols, axis=1)


def vec_layout(inputs=None):
    z = lambda *s: np.zeros(s, np.float32)
    g = (lambda k: np.asarray(inputs[k], np.float32)) if inputs is not None else None
    vp = VecPack()
    for l in range(DEPTH):
        vp.put_feat(f"gmix{l}", g("norm_mix_g")[l] if g else z(D))
        vp.put_feat(f"gffn{l}", g("norm_ffn_g")[l] if g else z(D))
        vp.put_feat(f"adab{l}", g("ada_b")[l] if g else z(6 * D))
        for i in range(3):
            vp.put_feat(f"fcw{l}_{i}", g("ffn_conv_w")[l, i] if g else z(DFF))
    for i in range(2):
        vp.put_feat(f"gqlat{i}", g("mla_g_qlat")[i] if g else z(256))
        vp.put_feat(f"gkvlat{i}", g("mla_g_kvlat")[i] if g else z(128))
        gq = z(128, 1); gk = z(128, 1); gkr = z(128, 1)
        if g:
            gq[0:64, 0] = g("mla_g_qn")[i]; gq[64:96, 0] = g("mla_g_qr")[i]
            gk[0:64, 0] = g("mla_g_kn")[i]; gk[64:128, 0] = g("mla_g_kn")[i]
            gkr[64:96, 0] = g("mla_g_kr")[i]
        vp.put(f"gq{i}", gq); vp.put(f"gk{i}", gk); vp.put(f"gkr{i}", gkr)
        for nm, key, nn in (("mu", "rwkv_mu", 1792), ("w0", "rwkv_w0", 512), ("a0", "rwkv_a0", 512), ("kk", "rwkv_k_k", 512),
                            ("ka", "rwkv_k_a", 512), ("lng", "rwkv_lnx_g", 512), ("lnb", "rwkv_lnx_b", 512)):
            vp.put_feat(f"{nm}{i}", g(key)[i] if g else z(nn))
        vp.put_feat(f"rk{i}", g("rwkv_r_k")[i].reshape(-1) if g else z(512))
    for i in range(2):
        vp.put_feat(f"pscale{i}", g("pool_scale")[i] if g else z(512))
        for t in range(4):
            vp.put_feat(f"gcw{i}_{t}", g("gdn_conv_w")[i, t] if g else z(1536))
        vp.put_feat(f"gog{i}", g("gdn_o_g")[i] if g else z(128))
        dtb = z(128, 1); alog = z(128, 1)
        if g:
            dtb[4:8, 0] = g("gdn_dt_bias")[i]; alog[4:8, 0] = g("gdn_a_log")[i]
        vp.put(f"dtb{i}", dtb); vp.put(f"alog{i}", alog)
    return vp


NPOS = SEQ + NB * DSEQ
CST = {}


def const_array():
    CST.clear()
    cols = []
    off = 0

    def put(name, a):
        nonlocal off
        a = np.ascontiguousarray(a, np.float32)
        CST[name] = (off, a.shape[1]); cols.append(a); off += a.shape[1]
    pos = np.concatenate([np.arange(SEQ)] + [PAST + np.arange(DSEQ)] * NB).astype(np.float32)
    inv = (1.0 / (10000.0 ** (np.arange(0, 32, 2, dtype=np.float32) / 32))).astype(np.float32)
    ang = pos[None, :] * inv[:, None]
    C = np.zeros((128, NPOS), np.float32); S = np.zeros((128, NPOS), np.float32)
    C[0:64] = 1.0
    C[64:80] = np.cos(ang); C[80:96] = np.cos(ang)
    S[64:80] = np.sin(ang); S[80:96] = np.sin(ang)
    put("C", C); put("S", S)
    return np.concatenate(cols, axis=1)


CST2 = {}


def const_array2():
    CST2.clear()
    cols = []
    off = 0

    def put(name, a):
        nonlocal off
        a = np.ascontiguousarray(a, np.float32)
        assert a.shape[0] == 128
        CST2[name] = (off, a.shape[1]); cols.append(a); off += a.shape[1]
    Pm = np.zeros((128, 128), np.float32)
    for i in range(16):
        Pm[80 + i, 64 + i] = -1.0
        Pm[64 + i, 80 + i] = 1.0
    put("Pm", Pm)
    BO96 = np.zeros((128, 128), np.float32); BO96[0:64, 0:64] = 1 / 64; BO96[64:96, 64:96] = 1 / 32
    put("BO96", BO96)
    BO64 = np.zeros((128, 128), np.float32); BO64[0:64, 0:64] = 1 / 64; BO64[64:128, 64:128] = 1 / 64
    put("BO64", BO64)
    put("I128", np.eye(128, dtype=np.float32))
    for Cc in (64, 32):
        n2 = 2 * Cc
        blk = (np.arange(n2)[:, None] // Cc) == (np.arange(n2)[None, :] // Cc)
        jj = np.arange(n2)[:, None] % Cc; ii = np.arange(n2)[None, :] % Cc
        MS = (blk & (ii > jj)).astype(np.float32); MI = (blk & (ii >= jj)).astype(np.float32)
        m = np.zeros((128, 2 * n2), np.float32); m[0:n2, 0:n2] = MS; m[0:n2, n2:2 * n2] = MI
        put(f"MSI{Cc}", m)
        mneg = np.zeros((128, 2 * n2), np.float32); mneg[0:n2, 0:n2] = -MS; mneg[0:n2, n2:2 * n2] = MI
        put(f"MSIN{Cc}", mneg)
        tri = np.zeros((128, n2), np.float32); tri[0:n2] = (blk & (jj <= ii)).astype(np.float32)
        put(f"TRI{Cc}", tri)
        ob = np.zeros((128, n2), np.float32); ob[0:n2] = blk.astype(np.float32)
        put(f"ONB{Cc}", ob)
        put(f"NONB{Cc}", -ob)
        lv = 0
        while (1 << lv) < Cc:
            half = 1 << lv; full = 2 * half
            I_ = np.arange(n2)[:, None]; J_ = np.arange(n2)[None, :]
            mk_ = ((I_ // full) == (J_ // full)) & ((I_ % full) >= half) & ((J_ % full) < half)
            mm_ = np.zeros((128, n2), np.float32); mm_[0:n2] = mk_.astype(np.float32)
            put(f"MK{Cc}_{lv}", mm_)
            lv += 1
        stri = np.zeros((128, n2), np.float32); stri[0:n2] = (blk & (jj > ii)).astype(np.float32)
        put(f"STRI{Cc}", stri)
        for hh in range(2):
            sl = np.zeros((128, 128), np.float32); sl[hh * Cc:(hh + 1) * Cc, :] = 1.0
            put(f"BLK{Cc}_{hh}", sl)
        r = np.ones((128, NB * Cc), np.float32); r[:, ::Cc] = 0.0
        put(f"R{Cc}", r)
    ict = np.zeros((128, 4 * 15), np.float32)
    for gi, w in enumerate((2, 4, 8, 16)):
        ict[:, gi * 15:(gi + 1) * 15] = 1.0 / np.minimum(w, np.arange(15) + 1)
    put("ICT", ict)
    oh = np.zeros((128, 8), np.float32); oh[0:8, 0:8] = np.eye(8)
    put("OH8", oh)
    for h in range(4):
        sl = np.zeros((128, 128), np.float32); sl[h, :] = 1.0
        put(f"SEL8_{h}", sl)
    return np.concatenate(cols, axis=1)


class TokTile:
    def __init__(self, nseg, L, rows, seqs, t0, first, last, sample):
        self.nseg, self.L, self.rows, self.seqs, self.t0 = nseg, L, rows, seqs, t0
        self.TT = nseg * L
        self.first, self.last, self.sample = first, last, sample
        self.res = None


def make_tiles():
    tiles = []
    for s in range(MK_NSEQ):
        nt = MK_NT
        for j in range(nt):
            tiles.append(TokTile(1, TT, [s], [s], j * TT, j == 0, j == nt - 1, False))
    tiles.append(TokTile(NB, DSEQ, [4, 5, 6, 7], [0, 1, 2, 3], 0, True, True, True))
    return tiles


def build_program():
    nc = bass.Bass("TRN2", target_bir_lowering=False)
    P = Prog(nc)
    vl = vec_layout(None)
    NV = vl.n

    def din(name, shape, dt=F32):
        return nc.dram_tensor(name, list(shape), dt, kind="ExternalInput").ap()

    def dout(name, shape, dt=F32):
        return nc.dram_tensor(name, list(shape), dt, kind="ExternalOutput").ap()

    xp = din("xp", [NB, KC, 128, SEQ]); xs = din("xs", [NB, KC, 128, DSEQ])
    yp = dout("yp", [NB, KC, 128, SEQ]); ys = dout("ys", [NB, KC, 128, DSEQ])
    cT_d = din("cT", [128, KC, 8])
    vecs_d = din("vecs", [128, NV])
    ada_w = din("ada_w", [DEPTH, D, 6 * D])
    wg_d = din("ffn_w_gate", [DEPTH, D, DFF]); wu_d = din("ffn_w_up", [DEPTH, D, DFF])
    wd_d = din("ffn_w_down", [DEPTH, DFF, D])
    fh_d = din("ffn_hist", [DEPTH, 128, FC, NB, 2])
    o_ffn_p = dout("o_ffn_p", [DEPTH, 128, FC, NB, 2]); o_ffn_s = dout("o_ffn_s", [DEPTH, 128, FC, NB, 2])

    cst_np = const_array(); cst2_np = const_array2()
    NCST = cst_np.shape[1]
    cst_d = din("consts", [128, NCST])
    win_e = din("even_w_in", [2, D, 2208]); wuq_d = din("mla_w_uq", [2, 256, 768]); wukv_d = din("mla_w_ukv", [2, 128, 1024])
    wout_e = din("even_w_out", [2, D, D])
    ckv_past = din("ckv_past", [2, NB, 128, PAST]); kpe_past = din("kpe_past", [2, NB, 32, PAST])
    o_ckv_p = dout("o_ckv_p", [2, NB, 128, SEQ]); o_kpe_p = dout("o_kpe_p", [2, NB, 32, SEQ])
    o_ckv_s = dout("o_ckv_s", [2, NB, 128, DSEQ]); o_kpe_s = dout("o_kpe_s", [2, NB, 32, DSEQ])
    cst2_np = const_array2()
    NCST2 = cst2_np.shape[1]
    cst2_d = din("consts2", [128, NCST2])
    w2_d = din("rwkv_w2", [2, 64, 512]); a2_d = din("rwkv_a2", [2, 64, 512]); g2_d = din("rwkv_g2", [2, 128, 512])
    shift_in = din("shift_in", [2, 128, 14, NB]); wkv_in = din("wkv_in", [2, NB, 4, 128, 64])
    o_shift_p = dout("o_shift_p", [2, 128, 14, NB]); o_shift_s = dout("o_shift_s", [2, 128, 14, NB])
    o_wkv_p = dout("o_wkv_p", [2, NB, 4, 128, 64]); o_wkv_s = dout("o_wkv_s", [2, NB, 4, 128, 64])
    win_o = din("odd_w_in", [2, D, 2568]); poolw_d = din("pool_w", [2, 4, 128, 128]); wout_o = din("odd_w_out", [2, D, D])
    pool_in = din("pool_in", [2, 128, 4, NB, 15]); gconv_in = din("gconv_in", [2, 128, 12, NB, 3]); gdn_in = din("gdn_in", [2, NB, 4, 128, 128])
    o_pool_p = dout("o_pool_p", [2, 128, 4, NB, 15]); o_pool_s = dout("o_pool_s", [2, 128, 4, NB, 15])
    o_gconv_p = dout("o_gconv_p", [2, 128, 12, NB, 3]); o_gconv_s = dout("o_gconv_s", [2, 128, 12, NB, 3])
    o_gdn_p = dout("o_gdn_p", [2, NB, 4, 128, 128]); o_gdn_s = dout("o_gdn_s", [2, NB, 4, 128, 128])
    tiles = make_tiles()
    TTF = 512
    omla_d = nc.dram_tensor("omla_scr", [len(tiles), 128, 4, TT], BF16, kind="Internal").ap()
    omla_res = [Res(f"omla{t}") for t in range(len(tiles))]
    xres = [Res(f"x{t}") for t in range(len(tiles))]
    for t_, tk_ in enumerate(tiles):
        tk_.res = [xres[t_]]
    ftiles = []
    if MK_NT % 2 == 0:
        for t_ in range(0, len(tiles) - 1, 2):
            a_, b_ = tiles[t_], tiles[t_ + 1]
            ft = TokTile(1, 2 * TT, a_.rows, a_.seqs, a_.t0, a_.first, b_.last, False)
            ft.res = [xres[t_], xres[t_ + 1]]
            ftiles.append(ft)
        ftiles.append(tiles[-1])
    else:
        ftiles = tiles

    def x_dram(ti, tk, seg, src_first):
        s = tk.seqs[seg]
        if tk.sample:
            base = xs if src_first else ys
            ap = base[s].rearrange("k p t -> p k t")
        else:
            base = xp if src_first else yp
            ap = base[s][:, :, tk.t0:tk.t0 + tk.L].rearrange("k p t -> p k t")
        return View(ap, tk.res)

    def y_dram(ti, tk, seg):
        s = tk.seqs[seg]
        if tk.sample:
            ap = ys[s].rearrange("k p t -> p k t")
        else:
            ap = yp[s][:, :, tk.t0:tk.t0 + tk.L].rearrange("k p t -> p k t")
        return View(ap, tk.res)

    vecs = P.sbuf("vecs", [128, NV])
    P.sp.dma_start(vecs, vecs_d)

    def V(name, c0=0, n=None):
        o, w = vl.off[name]
        n = w - c0 if n is None else n
        return vecs[:, o + c0:o + c0 + n]

    CH = {}

    def load_consts(names):
        names = ["Pm", "BO96", "BO64", "I128"] + [n_ for n_ in names if n_ not in ("Pm", "BO96", "BO64", "I128")]
        offs = {}
        tot = 0
        for n_ in names:
            offs[n_] = (tot, CST2[n_][1]); tot += CST2[n_][1]
        t_ = P.sbuf("cst2", [128, tot])
        for n_ in names:
            o_, w_ = CST2[n_]
            P.sp.dma_start(t_[:, offs[n_][0]:offs[n_][0] + w_], cst2_d[:, o_:o_ + w_])
        CH["cst2"] = t_; CH["offs"] = offs
        cb_ = P.sbuf("cb", [128, 4 * 128], BF16)
        for j, nm in enumerate(("Pm", "BO96", "BO64", "I128")):
            P.dve.tensor_copy(out=cb_[:, j * 128:(j + 1) * 128], in_=C2(nm))
        return cb_[:, 0:128], cb_[:, 128:256], cb_[:, 256:384], cb_[:, 384:512]

    def C2(name, c0=0, n=None):
        o, w = CH["offs"][name]
        n = w - c0 if n is None else n
        return CH["cst2"][:, o + c0:o + c0 + n]
    ones_bf = P.sbuf("ones_bf", [128, 128], BF16)
    P.dve.memset(ones_bf, 1.0 / D)
    eps_t = P.sbuf("eps_t", [128, 1]); P.dve.memset(eps_t, EPS)
    psb = [P.psum(f"psb{i}", [128, 512]) for i in range(8)]
    for pt_ in psb:
        P.dve.memset(pt_, 0.0)

    class Rot:
        def __init__(self, items):
            self.items, self.i = items, 0

        def next(self):
            it = self.items[self.i % len(self.items)]
            self.i += 1
            return it

    cT = P.sbuf("cT", [128, KC, 8]); P.sp.dma_start(cT, cT_d)
    cs = P.sbuf("cs", [128, KC, 8], BF16)
    P.act.activation(out=cs, in_=cT, func=AF.Silu)
    mod = [P.sbuf(f"mod{l}", [128, 48, 8]) for l in range(DEPTH)]
    modA_m = [P.sbuf(f"modAm{l}", [128, KC, 8]) for l in range(DEPTH)]
    modA_f = [P.sbuf(f"modAf{l}", [128, KC, 8]) for l in range(DEPTH)]
    with P.scope():
        wa_bufs = Rot([P.sbuf(f"wa{i}", [128, KC, 1024], BF16) for i in range(2)])
        for l in range(NLAYERS):
            pm = psb[l % 2]
            for fg in range(6):
                wa = wa_bufs.next()
                P.pool.dma_start(wa, ada_w[l][:, fg * 1024:(fg + 1) * 1024].rearrange("(k p) f -> p k f", p=128))
                for cc in range(8):
                    c = fg * 8 + cc
                    for k in range(KC):
                        P.pe.matmul(out=pm[:, c * 8:(c + 1) * 8], lhsT=wa[:, k, cc * 128:(cc + 1) * 128],
                                    rhs=cs[:, k, :], start=(k == 0), stop=(k == KC - 1))
            P.dve.tensor_tensor(out=mod[l], in0=pm[:, 0:384].rearrange("p (c b) -> p c b", b=8),
                                in1=V(f"adab{l}").bc3(2, 8), op=ALU.add)
            for (A, gname, c0) in ((modA_m[l], f"gmix{l}", 8), (modA_f[l], f"gffn{l}", 32)):
                P.dve.tensor_scalar(out=A, in0=mod[l][:, c0:c0 + 8, :], scalar1=1.0, scalar2=None, op0=ALU.add)
                P.dve.tensor_tensor(out=A, in0=A, in1=V(gname).bc3(2, 8), op=ALU.mult)

    x_raw = P.sbuf("x_t", [128, KC, TTF])
    h_raw = P.sbuf("h_t", [128, KC, TTF], BF16)
    r_raw = P.sbuf("rstd_t", [128, TTF])
    XB = {}
    for nm_, raw_ in (("x", x_raw), ("h", h_raw), ("r", r_raw)):
        rr_ = [Res(nm_ + "_h0"), Res(nm_ + "_h1")]
        if nm_ == "r":
            XB[nm_] = {"full": View(raw_.ap, rr_), 0: View(raw_.ap[:, 0:TT], rr_[0]), 1: View(raw_.ap[:, TT:2 * TT], rr_[1])}
        else:
            XB[nm_] = {"full": View(raw_.ap, rr_), 0: View(raw_.ap[:, :, 0:TT], rr_[0]), 1: View(raw_.ap[:, :, TT:2 * TT], rr_[1])}
    XH = {}

    def use_buf(j):
        for nm_ in ("x", "h", "r"):
            XH[nm_] = XB[nm_][j]
    use_buf("full")
    sqk = Rot([P.sbuf(f"sqk{j}", [128, TTF], BF16) for j in range(2)])
    tmk = Rot([P.sbuf(f"tmk{j}", [128, TTF]) for j in range(2)])
    ps_stat = psb[7]

    def modulate(tk, A, Bsh):
        n = tk.TT
        for k in range(KC):
            sq = sqk.next()
            P.act.activation(out=sq[:, 0:n], in_=XH["x"][:, k, 0:n], func=AF.Square)
            P.pe.matmul(out=ps_stat[:, 0:n], lhsT=ones_bf, rhs=sq[:, 0:n], start=(k == 0), stop=(k == KC - 1))
        P.act.activation(out=XH["r"][:, 0:n], in_=ps_stat[:, 0:n], func=AF.Sqrt, bias=eps_t, scale=1.0)
        P.dve.reciprocal(out=XH["r"][:, 0:n], in_=XH["r"][:, 0:n])
        for k in range(KC):
            for si, row in enumerate(tk.rows):
                sl = slice(si * tk.L, (si + 1) * tk.L)
                t_ = tmk.next()
                P.dve.scalar_tensor_tensor(out=t_[:, sl], in0=XH["x"][:, k, sl], scalar=A[:, k, row:row + 1], in1=XH["r"][:, sl], op0=ALU.mult, op1=ALU.mult)
                P.act.activation(out=XH["h"][:, k, sl], in_=t_[:, sl], func=AF.Identity, bias=Bsh[:, k, row:row + 1], scale=1.0)

    def load_x(ti, tk, first_layer):
        for si in range(tk.nseg):
            P.sp.dma_start(XH["x"][:, :, si * tk.L:(si + 1) * tk.L], x_dram(ti, tk, si, first_layer))

    def store_x(ti, tk, final):
        for si in range(tk.nseg):
            P.sp.dma_start(y_dram(ti, tk, si), XH["x"][:, :, si * tk.L:(si + 1) * tk.L], final=final)

    for l in range(NLAYERS):

        if DO_MIX and l % 2 == 0:
            i = l // 2
            P.barrier()
            with P.scope():
                Pm_b, BO96_b, BO64_b, Ib = load_consts([])
                cst = P.sbuf("cst", [128, NCST]); P.sp.dma_start(cst, cst_d)

                def CS(name):
                    o, w = CST[name]
                    return cst[:, o:o + w]
                ones1 = P.sbuf("ones1", [128, 64], BF16); P.dve.memset(ones1, 1.0)
                ones128 = P.sbuf("ones128", [128, 128], BF16); P.dve.memset(ones128, 1.0 / 128)
                win = P.sbuf("win", [128, KC, 2208], BF16)
                for k in range(KC):
                    P.pool.dma_start(win[:, k, :], win_e[i][k * 128:(k + 1) * 128, :])
                wuq = P.sbuf("wuq", [128, 2, 768], BF16)
                for k in range(2):
                    P.pool.dma_start(wuq[:, k, :], wuq_d[i][k * 128:(k + 1) * 128, :])
                wukv = P.sbuf("wukv", [128, 1024], BF16); P.pool.dma_start(wukv, wukv_d[i])
                wk_c = P.sbuf("wk_c", [128, 8, 64], BF16); wv_c = P.sbuf("wv_c", [128, 8, 64], BF16)
                wukv3 = wukv.rearrange("p (h c) -> p h c", h=8)
                P.dve.tensor_copy(out=wk_c, in_=wukv3[:, :, 0:64]); P.dve.tensor_copy(out=wv_c, in_=wukv3[:, :, 64:128])
                PGq = P.sbuf("PGq", [128, 128], BF16)
                P.dve.tensor_scalar(out=PGq[0:96, 0:96], in0=C2("Pm")[0:96, 0:96], scalar1=V(f"gq{i}")[0:96, :], scalar2=None, op0=ALU.mult)
                PGk = P.sbuf("PGk", [128, 128], BF16)
                P.dve.memset(PGk, 0.0)
                Ctab = CS("C")
                P.dve.tensor_scalar(out=PGk[64:96, 0:96], in0=C2("Pm")[64:96, 0:96], scalar1=V(f"gkr{i}")[64:96, :], scalar2=None, op0=ALU.mult)
                Stab = CS("S")
                NKT = SEQ // 128
                KT = P.sbuf("KT", [96, 8, SEQ], BF16)
                Vt = P.sbuf("Vt", [128, NKT, 8, 64], BF16)
                mixT = P.sbuf("mixT", [128, 4, TT], BF16)
                ckv_f = P.sbuf("ckv_f", [128, TT]); ckv_b = P.sbuf("ckv_b", [128, TT], BF16)
                kpe_f = P.sbuf("kpe_f", [128, TT]); kpe_b = P.sbuf("kpe_b", [128, TT], BF16)
                qlat_b = P.sbuf("qlat_b", [128, 2, TT], BF16)
                sqa = Rot([P.sbuf(f"sqa{j}", [128, TT], BF16) for j in range(3)])
                tfa = Rot([P.sbuf(f"tfa{j}", [128, TT]) for j in range(6)])
                tfb = Rot([P.sbuf(f"tfb{j}", [128, TT]) for j in range(8)])
                qgb = Rot([P.sbuf(f"qgb{j}", [128, TT], BF16) for j in range(3)])
                Qf = P.sbuf("Qf", [96, 8, TT], BF16)
                PT = Rot([P.sbuf(f"PT{j}", [128, TT], BF16) for j in range(6)])
                rsum = Rot([P.sbuf(f"rsum{j}", [128, TT]) for j in range(2)])
                pastb = P.sbuf("pastb", [128, PAST], BF16); kpast = P.sbuf("kpast", [96, PAST], BF16)
                pa = Rot(psb[0:3]); pb = Rot(psb[3:5]); po = Rot(psb[5:7])

                def rstd_from(ps_view, out_view, r0=0, r1=128):
                    P.act.activation(out=out_view, in_=ps_view, func=AF.Sqrt, bias=eps_t[r0:r1, :], scale=1.0)
                    P.dve.reciprocal(out=out_view, in_=out_view)

                def knope_v(src_b, n, kbase, r0=0):
                    for hp in range(4):
                        kp = pa.next()
                        P.pe.matmul(out=kp[:, 0:n], lhsT=wk_c[:, 2 * hp:2 * hp + 2, :].rearrange("p h c -> p (h c)"), rhs=src_b[:, 0:n], start=True, stop=True)
                        sq = sqa.next()
                        P.act.activation(out=sq[:, 0:n], in_=kp[:, 0:n], func=AF.Square)
                        mp = pb.next()
                        P.pe.matmul(out=mp[:, 0:n], lhsT=BO64_b, rhs=sq[:, 0:n], start=True, stop=True)
                        rs = tfa.next()
                        rstd_from(mp[:, 0:n], rs[:, 0:n])
                        t = tfb.next()
                        P.dve.scalar_tensor_tensor(out=t[:, 0:n], in0=kp[:, 0:n], scalar=V(f"gk{i}"), in1=rs[:, 0:n], op0=ALU.mult, op1=ALU.mult)
                        P.pool.tensor_copy(out=KT[0:64, 2 * hp, kbase:kbase + n], in_=t[0:64, 0:n])
                        P.pool.tensor_copy(out=KT[0:64, 2 * hp + 1, kbase:kbase + n], in_=t[64:128, 0:n])
                    for j0 in range(0, n, 128):
                        m = min(128, n - j0)
                        vp_ = pa.next()
                        P.pe.matmul(out=vp_[0:m, 0:512], lhsT=src_b[:, j0:j0 + m], rhs=wv_c.rearrange("p h c -> p (h c)"), start=True, stop=True)
                        P.act.activation(out=Vt[0:m, (kbase + j0) // 128, :, :].rearrange("p h c -> p (h c)"), in_=vp_[0:m, 0:512], func=AF.Copy)

                def rope_norm(src_ps, r0, r1, n, CG, PG, pos0, out_view):
                    sq = sqa.next()
                    P.act.activation(out=sq[0:96, 0:n], in_=src_ps[0:96, 0:n], func=AF.Square)
                    mp = pb.next()
                    P.pe.matmul(out=mp[0:96, 0:n], lhsT=BO96_b[0:96, 0:96], rhs=sq[0:96, 0:n], start=True, stop=True)
                    rs = tfa.next()
                    rstd_from(mp[r0:r1, 0:n], rs[r0:r1, 0:n], r0, r1)
                    qg = qgb.next()
                    if r0 > 0:
                        P.pool.memset(qg[0:r0, 0:n], 0.0)
                    P.dve.tensor_copy(out=qg[r0:r1, 0:n], in_=src_ps[r0:r1, 0:n])
                    rp = pb.next()
                    P.pe.matmul(out=rp[0:96, 0:n], lhsT=PG[0:96, 0:96], rhs=qg[0:96, 0:n], start=True, stop=True)
                    t1 = tfb.next()
                    P.dve.scalar_tensor_tensor(out=t1[r0:r1, 0:n], in0=src_ps[r0:r1, 0:n], scalar=CG[r0:r1, :], in1=Ctab[r0:r1, pos0:pos0 + n], op0=ALU.mult, op1=ALU.mult)
                    t2 = tfb.next()
                    P.dve.tensor_tensor(out=t2[r0:r1, 0:n], in0=rp[r0:r1, 0:n], in1=Stab[r0:r1, pos0:pos0 + n], op=ALU.mult)
                    P.pool.tensor_tensor(out=t1[r0:r1, 0:n], in0=t1[r0:r1, 0:n], in1=t2[r0:r1, 0:n], op=ALU.add)
                    P.pool.tensor_tensor(out=out_view, in0=t1[r0:r1, 0:n], in1=rs[r0:r1, 0:n], op=ALU.mult)

                for ti, tk in enumerate(tiles):
                    use_buf(ti % 2)
                    n, L, ns = tk.TT, tk.L, tk.nseg
                    load_x(ti, tk, l == 0)
                    modulate(tk, modA_m[l], mod[l][:, 0:8, :])
                    pos0 = (SEQ if tk.sample else tk.t0)

                    def proj(c0, m):
                        ps = pa.next()
                        for k in range(KC):
                            P.pe.matmul(out=ps[0:m, 0:n], lhsT=win[:, k, c0:c0 + m], rhs=XH["h"][:, k, 0:n], start=(k == 0), stop=(k == KC - 1))
                        return ps
                    qps = [proj(0, 128), proj(128, 128)]
                    mp = pb.next()
                    for c in range(2):
                        sq = sqa.next()
                        P.act.activation(out=sq[:, 0:n], in_=qps[c][:, 0:n], func=AF.Square)
                        P.pe.matmul(out=mp[:, 0:n], lhsT=ones128, rhs=sq[:, 0:n], start=(c == 0), stop=(c == 1))
                    rs = tfa.next()
                    P.act.activation(out=rs[:, 0:n], in_=mp[:, 0:n], func=AF.Sqrt, bias=eps_t, scale=0.5)
                    P.dve.reciprocal(out=rs[:, 0:n], in_=rs[:, 0:n])
                    for c in range(2):
                        P.dve.scalar_tensor_tensor(out=qlat_b[:, c, 0:n], in0=qps[c][:, 0:n], scalar=V(f"gqlat{i}", c, 1), in1=rs[:, 0:n], op0=ALU.mult, op1=ALU.mult)
                    kvp = proj(256, 128)
                    sq = sqa.next()
                    P.act.activation(out=sq[:, 0:n], in_=kvp[:, 0:n], func=AF.Square)
                    mp = pb.next()
                    P.pe.matmul(out=mp[:, 0:n], lhsT=ones128, rhs=sq[:, 0:n], start=True, stop=True)
                    rs = tfa.next()
                    rstd_from(mp[:, 0:n], rs[:, 0:n])
                    P.dve.scalar_tensor_tensor(out=ckv_f[:, 0:n], in0=kvp[:, 0:n], scalar=V(f"gkvlat{i}"), in1=rs[:, 0:n], op0=ALU.mult, op1=ALU.mult)
                    P.act.activation(out=ckv_b[:, 0:n], in_=ckv_f[:, 0:n], func=AF.Copy)
                    krp = proj(320, 96)
                    rope_norm(krp, 64, 96, n, V(f"gkr{i}"), PGk, pos0, kpe_f[64:96, 0:n])
                    P.act.activation(out=kpe_b[64:96, 0:n], in_=kpe_f[64:96, 0:n], func=AF.Copy)
                    for si in range(ns):
                        sidx = tk.seqs[si]
                        sl = slice(si * L, (si + 1) * L)
                        if tk.sample:
                            P.sp.dma_start(View(o_ckv_s[i, sidx], Res("o")), ckv_f[:, sl], final=True)
                            P.sp.dma_start(View(o_kpe_s[i, sidx], Res("o")), kpe_f[64:96, sl], final=True)
                        else:
                            P.sp.dma_start(View(o_ckv_p[i, sidx][:, tk.t0:tk.t0 + L], Res("o")), ckv_f[:, sl], final=True)
                            P.sp.dma_start(View(o_kpe_p[i, sidx][:, tk.t0:tk.t0 + L], Res("o")), kpe_f[64:96, sl], final=True)
                    for h in range(8):
                        qp = pa.next()
                        for c in range(2):
                            P.pe.matmul(out=qp[0:96, 0:n], lhsT=wuq[:, c, h * 96:(h + 1) * 96], rhs=qlat_b[:, c, 0:n], start=(c == 0), stop=(c == 1))
                        rope_norm(qp, 0, 96, n, V(f"gq{i}"), PGq, pos0, Qf[0:96, h, 0:n])
                    for si in range(ns):
                        sidx = tk.seqs[si]
                        qsl = slice(si * L, (si + 1) * L)
                        if tk.sample:
                            P.pool.dma_start(pastb, ckv_past[i, sidx])
                            P.pool.dma_start(kpast[64:96, :], kpe_past[i, sidx])
                            for j0 in range(0, PAST, TT):
                                knope_v(pastb[:, j0:j0 + TT], TT, j0)
                            P.pool.tensor_copy(out=KT[64:96, :, 0:PAST], in_=kpast[64:96, :].bc3(1, 8))
                            kb = PAST
                        else:
                            kb = tk.t0
                        knope_v(ckv_b[:, qsl], L, kb)
                        P.pool.tensor_copy(out=KT[64:96, :, kb:kb + L], in_=kpe_b[64:96, qsl].bc3(1, 8))
                        kts = []
                        if tk.sample:
                            kts = [(j0, 128, 0, False) for j0 in range(0, PAST, 128)] + [(PAST, L, 0, False)]
                        else:
                            kts = [(j0, 128, 0, False) for j0 in range(0, tk.t0, 128)]
                            for d in range(L // 128):
                                kts.append((tk.t0 + d * 128, 128, d * 128, True))
                        for h in range(8):
                            par = h % 2
                            op_ = po.next()
                            orow = slice(par * 64, par * 64 + 64); srow = slice((1 - par) * 64, (1 - par) * 64 + 64)
                            for kidx, (k0, ksz, q0, msk) in enumerate(kts):
                                nq = L - q0
                                sp_ = pa.next()
                                P.pe.matmul(out=sp_[0:ksz, 0:nq], lhsT=KT[0:96, h, k0:k0 + ksz], rhs=Qf[0:96, h, si * L + q0:(si + 1) * L], start=True, stop=True)
                                pt = PT.next()
                                P.act.activation(out=pt[0:ksz, 0:nq], in_=sp_[0:ksz, 0:nq], func=AF.Exp, scale=float(96 ** -0.5))
                                if msk:
                                    P.pool.memset(pt[64:128, 0:64], 0.0)
                                first = kidx == 0
                                last = kidx == len(kts) - 1
                                P.pe.matmul(out=op_[orow, q0:L], lhsT=Vt[0:ksz, k0 // 128, h, :], rhs=pt[0:ksz, 0:nq], start=first, stop=last, skip_group_check=True)
                                P.pe.matmul(out=op_[srow, q0:L], lhsT=ones1[0:ksz, :], rhs=pt[0:ksz, 0:nq], start=first, stop=last, skip_group_check=True)
                            rsm = rsum.next()
                            P.dve.reciprocal(out=rsm[srow, 0:L], in_=op_[srow, 0:L])
                            P.dve.tensor_tensor(out=mixT[orow, h // 2, qsl], in0=op_[orow, 0:L], in1=rsm[srow, 0:L], op=ALU.mult)
                    P.sp.dma_start(View(omla_d[ti][:, :, 0:n], omla_res[ti]), mixT[:, 0:4, 0:n])
            with P.scope():
                Pm_b, BO96_b, BO64_b, Ib = load_consts(["MSI64", "MSI32", "R64", "R32"] + [n_ for n_ in CST2 if n_.startswith("MK")])
                win_r = P.sbuf("win_r", [128, KC, 1792], BF16)
                for k in range(KC):
                    P.pool.dma_start(win_r[:, k, :], win_e[i][k * 128:(k + 1) * 128, 416:2208])
                w2 = P.sbuf("w2", [128, 512], BF16); P.pool.dma_start(w2[0:64, :], w2_d[i])
                wa2 = P.sbuf("wa2", [128, 512], BF16); P.pool.dma_start(wa2[64:128, :], a2_d[i])
                g2 = P.sbuf("g2", [128, 512], BF16); P.pool.dma_start(g2, g2_d[i])
                wout = P.sbuf("wout", [128, KC, D], BF16)
                for k in range(KC):
                    P.pool.dma_start(wout[:, k, :], wout_e[i][k * 128:(k + 1) * 128, :])
                omka = P.sbuf("omka", [128, 4])
                P.dve.tensor_scalar(out=omka, in0=V(f"ka{i}"), scalar1=-1.0, scalar2=1.0, op0=ALU.mult, op1=ALU.add)
                lneps = P.sbuf("lneps", [128, 1]); P.dve.memset(lneps, 64e-5)
                shift_p = P.sbuf("shift_p", [128, 14, NB]); shift_s = P.sbuf("shift_s", [128, 14, NB])
                P.dve.memset(shift_p, 0.0); P.sp.dma_start(shift_s, shift_in[i])
                Hf = {}; Hb = {}
                for grp in ("p", "s"):
                    for sq_ in range(NB):
                        for hp in range(4):
                            Hf[grp, sq_, hp] = P.sbuf(f"Hf{grp}{sq_}{hp}", [128, 64])
                            Hb[grp, sq_, hp] = P.sbuf(f"Hb{grp}{sq_}{hp}", [128, 64], BF16)
                            if grp == "p":
                                P.pool.memset(Hf[grp, sq_, hp], 0.0)
                            else:
                                P.sp.dma_start(Hf[grp, sq_, hp], wkv_in[i, sq_, hp])
                            P.act.activation(out=Hb[grp, sq_, hp], in_=Hf[grp, sq_, hp], func=AF.Copy)
                rwb = Rot([P.sbuf(f"rwb{j}", [128, TT + NB]) for j in range(2)]); xm_t = P.sbuf("xm_t", [128, 14, TT])
                g_t = P.sbuf("g_t", [128, 4, TT])
                eG_t = P.sbuf("eG_t", [128, 4, TT])
                bonus_t = P.sbuf("bonus_t", [128, 4, TT])
                yT_t = P.sbuf("yT_t", [128, 4, TT])
                AR = P.sbuf("AR", [128, 4, 2 * TT], BF16); BK = P.sbuf("BK", [128, 4, 2 * TT], BF16)
                v_b = P.sbuf("v_b", [128, 4, TT], BF16)
                th_b = P.sbuf("th_b", [128, TT], BF16); da_b = P.sbuf("da_b", [128, TT], BF16); sg_b = P.sbuf("sg_b", [128, TT], BF16)
                mixT = P.sbuf("mixTB", [128, KC, TT], BF16)
                tf = Rot([P.sbuf(f"tf{j}", [128, TT]) for j in range(6)])
                tg = [P.sbuf(f"tg{j}", [128, TT]) for j in range(8)]
                tb = Rot([P.sbuf(f"tb{j}", [128, TT], BF16) for j in range(3)])
                pa = Rot(psb[0:7])
                def mk(name, shape, dt):
                    return [P.sbuf(f"{name}{hp}", shape, dt) for hp in range(4)]
                PB2 = []
                for par_ in range(2):
                    PB2.append(dict(tokm=mk(f"tokm{par_}_", [128, 192], BF16), NBs=mk(f"NBs{par_}_", [128, 256], BF16), NKs=mk(f"NKs{par_}_", [128, 256], BF16),
                                    Nm=[mk(f"Nm{par_}_{m}_", [128, 128], BF16) for m in range(1)],
                                    Xf=mk(f"Xf{par_}_", [128, 64], F32), Xb=mk(f"Xb{par_}_", [128, 64], BF16), Yb=mk(f"Yb{par_}_", [128, 64], BF16),
                                    tH=mk(f"tH{par_}_", [128, 64], F32), Zb=mk(f"Zb{par_}_", [128, 64], BF16),
                                    Tm=mk(f"rTm{par_}_", [128, 128], BF16), TTm=mk(f"rTTm{par_}_", [128, 128], BF16)))
                Lp = [mk(f"Lp{m}_", [128, 128], BF16) for m in range(1)]
                Wb = mk("rWb", [128, 128], BF16); tI = mk("rtI", [128, 128], F32)

                for ti, tk in enumerate(tiles):
                    use_buf(ti % 2)
                    n, L, ns = tk.TT, tk.L, tk.nseg
                    grp = "s" if tk.sample else "p"
                    Cc = 32 if tk.sample else 64
                    nch = n // Cc
                    C2c = 2 * Cc
                    NM = 5 if tk.sample else 6
                    MSI = C2(f"MSI{Cc}"); Rm = C2(f"R{Cc}")
                    shiftst = shift_s if tk.sample else shift_p
                    load_x(ti, tk, l == 0)
                    modulate(tk, modA_m[l], mod[l][:, 0:8, :])
                    for j in range(14):
                        rwj = rwb.next()
                        rw3 = rwj[:, 0:ns * (L + 1)].rearrange("p (s t) -> p s t", s=ns)
                        for si in range(ns):
                            P.pool.tensor_copy(out=rw3[:, si, 0:1], in_=shiftst[:, j, tk.seqs[si]:tk.seqs[si] + 1])
                        ps = pa.next()
                        for k in range(KC):
                            P.pe.matmul(out=ps[:, 0:n], lhsT=win_r[:, k, j * 128:(j + 1) * 128], rhs=XH["h"][:, k, 0:n], start=(k == 0), stop=(k == KC - 1))
                        P.act.activation(out=rw3[:, :, 1:L + 1], in_=ps[:, 0:n].rearrange("p (s t) -> p s t", s=ns), func=AF.Copy)
                        for si in range(ns):
                            P.pool.tensor_copy(out=shiftst[:, j, tk.seqs[si]:tk.seqs[si] + 1], in_=rw3[:, si, L:L + 1])
                        d = tf.next()
                        d3 = d[:, 0:n].rearrange("p (s t) -> p s t", s=ns)
                        P.pool.tensor_tensor(out=d3, in0=rw3[:, :, 0:L], in1=rw3[:, :, 1:L + 1], op=ALU.subtract)
                        P.dve.scalar_tensor_tensor(out=xm_t[:, j, 0:n].rearrange("p (s t) -> p s t", s=ns), in0=d3, scalar=V(f"mu{i}", j, 1),
                                                   in1=rw3[:, :, 1:L + 1], op0=ALU.mult, op1=ALU.add)
                    P.act.activation(out=th_b[0:64, 0:n], in_=xm_t[0:64, 12, 0:n], func=AF.Tanh)
                    P.act.activation(out=da_b[64:128, 0:n], in_=xm_t[64:128, 12, 0:n], func=AF.Copy)
                    P.act.activation(out=sg_b[:, 0:n], in_=xm_t[:, 13, 0:n], func=AF.Sigmoid)
                    for c in range(4):
                        cs_ = slice(c * 128, (c + 1) * 128)
                        r_c, k_c, v_c = xm_t[:, c, 0:n], xm_t[:, 4 + c, 0:n], xm_t[:, 8 + c, 0:n]
                        ps = pa.next()
                        P.pe.matmul(out=ps[:, 0:n], lhsT=w2[0:64, cs_], rhs=th_b[0:64, 0:n], start=True, stop=True)
                        t0_ = tf.next()
                        P.act.activation(out=t0_[:, 0:n], in_=ps[:, 0:n], func=AF.Sigmoid, bias=V(f"w0{i}", c, 1), scale=1.0)
                        P.pool.tensor_scalar(out=tg[0][:, 0:n], in0=t0_[:, 0:n], scalar1=-0.6065306597126334, scalar2=None, op0=ALU.mult)
                        ps = pa.next()
                        P.pe.matmul(out=ps[:, 0:n], lhsT=wa2[64:128, cs_], rhs=da_b[64:128, 0:n], start=True, stop=True)
                        P.act.activation(out=tg[1][:, 0:n], in_=ps[:, 0:n], func=AF.Sigmoid, bias=V(f"a0{i}", c, 1), scale=1.0)
                        ps = pa.next()
                        P.pe.matmul(out=ps[:, 0:n], lhsT=g2[:, cs_], rhs=sg_b[:, 0:n], start=True, stop=True)
                        P.act.activation(out=g_t[:, c, 0:n], in_=ps[:, 0:n], func=AF.Copy)
                        t1 = tf.next()
                        P.pool.tensor_scalar(out=t1[:, 0:n], in0=k_c, scalar1=V(f"kk{i}", c, 1), scalar2=None, op0=ALU.mult)
                        sq = tb.next()
                        P.act.activation(out=sq[:, 0:n], in_=t1[:, 0:n], func=AF.Square)
                        ps = pa.next()
                        P.pe.matmul(out=ps[:, 0:n], lhsT=BO64_b, rhs=sq[:, 0:n], start=True, stop=True)
                        rs = tf.next()
                        P.act.activation(out=rs[:, 0:n], in_=ps[:, 0:n], func=AF.Sqrt, bias=eps_t, scale=64.0)
                        P.dve.reciprocal(out=rs[:, 0:n], in_=rs[:, 0:n])
                        P.dve.tensor_tensor(out=tg[2][:, 0:n], in0=t1[:, 0:n], in1=rs[:, 0:n], op=ALU.mult)
                        t2 = tf.next()
                        P.pool.tensor_scalar(out=t2[:, 0:n], in0=tg[1][:, 0:n], scalar1=V(f"ka{i}", c, 1), scalar2=omka[:, c:c + 1], op0=ALU.mult, op1=ALU.add)
                        P.dve.tensor_tensor(out=tg[3][:, 0:n], in0=k_c, in1=t2[:, 0:n], op=ALU.mult)
                        rkb = tb.next()
                        P.dve.scalar_tensor_tensor(out=rkb[:, 0:n], in0=r_c, scalar=V(f"rk{i}", c, 1), in1=tg[3][:, 0:n], op0=ALU.mult, op1=ALU.mult)
                        ps = pa.next()
                        P.pe.matmul(out=ps[:, 0:n], lhsT=BO64_b, rhs=rkb[:, 0:n], start=True, stop=True)
                        P.dve.scalar_tensor_tensor(out=bonus_t[:, c, 0:n], in0=ps[:, 0:n], scalar=64.0, in1=v_c, op0=ALU.mult, op1=ALU.mult)
                        P.dve.tensor_tensor_scan(out=tg[4][:, 0:n], data0=Rm[:, 0:n], data1=tg[0][:, 0:n], initial=0.0, op0=ALU.mult, op1=ALU.add)
                        P.act.activation(out=eG_t[:, c, 0:n], in_=tg[4][:, 0:n], func=AF.Exp)
                        enG = tg[5]
                        P.act.activation(out=enG[:, 0:n], in_=tg[4][:, 0:n], func=AF.Exp, scale=-1.0)
                        gm = tg[6]
                        P.pool.tensor_tensor(out=gm[:, 0:n], in0=tg[4][:, 0:n], in1=tg[0][:, 0:n], op=ALU.subtract)
                        P.act.activation(out=gm[:, 0:n], in_=gm[:, 0:n], func=AF.Exp)
                        AR4 = AR[:, c, 0:2 * n].rearrange("p (q a t) -> p q a t", a=2, t=Cc)
                        BK4 = BK[:, c, 0:2 * n].rearrange("p (q a t) -> p q a t", a=2, t=Cc)
                        v3 = lambda vv: vv.rearrange("p (q t) -> p q t", t=Cc)
                        P.dve.scalar_tensor_tensor(out=AR4[:, :, 0, :], in0=v3(tg[2][:, 0:n]), scalar=-1.0, in1=v3(gm[:, 0:n]), op0=ALU.mult, op1=ALU.mult)
                        P.pool.tensor_tensor(out=AR4[:, :, 1, :], in0=v3(r_c), in1=v3(eG_t[:, c, 0:n]), op=ALU.mult)
                        bt = tg[7]
                        P.pool.tensor_tensor(out=bt[:, 0:n], in0=tg[2][:, 0:n], in1=tg[1][:, 0:n], op=ALU.mult)
                        P.dve.tensor_tensor(out=BK4[:, :, 0, :], in0=v3(bt[:, 0:n]), in1=v3(enG[:, 0:n]), op=ALU.mult)
                        P.pool.tensor_tensor(out=BK4[:, :, 1, :], in0=v3(tg[3][:, 0:n]), in1=v3(enG[:, 0:n]), op=ALU.mult)
                        P.act.activation(out=v_b[:, c, 0:n], in_=v_c, func=AF.Copy)
                    for ch in range(nch):
                        seq = tk.seqs[ch] if tk.sample else tk.seqs[0]
                        HF = [Hf[grp, seq, hp] for hp in range(4)]; HB = [Hb[grp, seq, hp] for hp in range(4)]
                        pb_ = PB2[ch % 2]
                        tokm, NBs, NKs, Nm, Xf, Xb, Yb, tH, Zb, Tm, TTm = (pb_[k_] for k_ in ("tokm", "NBs", "NKs", "Nm", "Xf", "Xb", "Yb", "tH", "Zb", "Tm", "TTm"))
                        ARc = lambda hp: AR[:, hp, 0:2 * n].rearrange("p (q a t) -> p q a t", a=2, t=Cc)[:, ch]
                        BKc = lambda hp: BK[:, hp, 0:2 * n].rearrange("p (q a t) -> p q a t", a=2, t=Cc)[:, ch]
                        rows = [slice(0, 64), slice(64, 128)]
                        orow = [slice(0, Cc), slice(Cc, C2c)]
                        tsl = slice(ch * Cc, (ch + 1) * Cc)
                        for hp in range(4):
                            ps = pa.next()
                            for hh in range(2):
                                P.pe.matmul(out=ps[orow[hh], 0:64], lhsT=BKc(hp)[rows[hh], 0, :], rhs=Ib[rows[hh], rows[hh]], start=True, stop=True)
                                P.pe.matmul(out=ps[orow[hh], 64:128], lhsT=BKc(hp)[rows[hh], 1, :], rhs=Ib[rows[hh], rows[hh]], start=True, stop=True)
                                P.pe.matmul(out=ps[orow[hh], 128:192], lhsT=v_b[rows[hh], hp, tsl], rhs=Ib[rows[hh], rows[hh]], start=True, stop=True)
                            P.act.activation(out=tokm[hp][0:C2c, :], in_=ps[0:C2c, 0:192], func=AF.Copy)
                        for hp in range(4):
                            for which, dst in ((0, NBs), (1, NKs)):
                                ps = pa.next()
                                for hh in range(2):
                                    for a_ in range(2):
                                        P.pe.matmul(out=ps[orow[hh], a_ * C2c + hh * Cc:a_ * C2c + (hh + 1) * Cc], lhsT=BKc(hp)[rows[hh], which, :],
                                                    rhs=ARc(hp)[rows[hh], a_, :], start=True, stop=True)
                                P.dve.tensor_tensor(out=dst[hp][0:C2c, 0:2 * C2c], in0=ps[0:C2c, 0:2 * C2c], in1=MSI[0:C2c, :], op=ALU.mult)
                        for hp in range(4):
                            P.pool.tensor_copy(out=Nm[0][hp][0:C2c, 0:C2c], in_=NBs[hp][0:C2c, 0:C2c])
                            ps = pa.next()
                            P.pe.matmul(out=ps[0:C2c, 0:C2c], lhsT=NBs[hp][0:C2c, 0:C2c], rhs=Ib[0:C2c, 0:C2c], start=True, stop=True)
                            P.act.activation(out=Lp[0][hp][0:C2c, 0:C2c], in_=ps[0:C2c, 0:C2c], func=AF.Copy)
                        for hp in range(4):
                            tmp_ = tI[hp]
                            P.dve.tensor_tensor(out=tmp_[0:C2c, 0:C2c], in0=Lp[0][hp][0:C2c, 0:C2c], in1=C2(f"MK{Cc}_0")[0:C2c, :], op=ALU.mult)
                            P.dve.tensor_tensor(out=Tm[hp][0:C2c, 0:C2c], in0=tmp_[0:C2c, 0:C2c], in1=C2("I128")[0:C2c, 0:C2c], op=ALU.add)
                            ps = pa.next()
                            P.pe.matmul(out=ps[0:C2c, 0:C2c], lhsT=Tm[hp][0:C2c, 0:C2c], rhs=Ib[0:C2c, 0:C2c], start=True, stop=True)
                            P.act.activation(out=TTm[hp][0:C2c, 0:C2c], in_=ps[0:C2c, 0:C2c], func=AF.Copy)
                        for lv in range(1, NM):
                            for hp in range(4):
                                ps = pa.next()
                                P.pe.matmul(out=ps[0:C2c, 0:C2c], lhsT=Nm[0][hp][0:C2c, 0:C2c], rhs=Tm[hp][0:C2c, 0:C2c], start=True, stop=True)
                                P.act.activation(out=Wb[hp][0:C2c, 0:C2c], in_=ps[0:C2c, 0:C2c], func=AF.Copy)
                                ps = pa.next()
                                P.pe.matmul(out=ps[0:C2c, 0:C2c], lhsT=TTm[hp][0:C2c, 0:C2c], rhs=Wb[hp][0:C2c, 0:C2c], start=True, stop=True)
                                tmp_ = tI[hp]
                                P.dve.tensor_tensor(out=tmp_[0:C2c, 0:C2c], in0=ps[0:C2c, 0:C2c], in1=C2(f"MK{Cc}_{lv}")[0:C2c, :], op=ALU.mult)
                                P.dve.tensor_tensor(out=Tm[hp][0:C2c, 0:C2c], in0=tmp_[0:C2c, 0:C2c], in1=Tm[hp][0:C2c, 0:C2c], op=ALU.add)
                                ps = pa.next()
                                P.pe.matmul(out=ps[0:C2c, 0:C2c], lhsT=Tm[hp][0:C2c, 0:C2c], rhs=Ib[0:C2c, 0:C2c], start=True, stop=True)
                                P.act.activation(out=TTm[hp][0:C2c, 0:C2c], in_=ps[0:C2c, 0:C2c], func=AF.Copy)
                        for hp in range(4):
                            ps = pa.next()
                            P.pe.matmul(out=ps[0:C2c, 0:64], lhsT=NKs[hp][0:C2c, 0:C2c], rhs=tokm[hp][0:C2c, 128:192], start=True, stop=False, skip_group_check=True)
                            for hh in range(2):
                                P.pe.matmul(out=ps[orow[hh], 0:64], lhsT=ARc(hp)[rows[hh], 0, :], rhs=HB[hp][rows[hh], :], start=False, stop=(hh == 1), skip_group_check=True)
                            P.act.activation(out=Xb[hp][0:C2c, :], in_=ps[0:C2c, 0:64], func=AF.Copy)
                        for hp in range(4):
                            ps = pa.next()
                            P.pe.matmul(out=ps[0:C2c, 0:64], lhsT=TTm[hp][0:C2c, 0:C2c], rhs=Xb[hp][0:C2c, :], start=True, stop=True)
                            P.act.activation(out=Zb[hp][0:C2c, :], in_=ps[0:C2c, 0:64], func=AF.Copy)
                        for hp in range(4):
                            ps = pa.next()
                            P.pe.matmul(out=ps[0:C2c, 0:64], lhsT=NBs[hp][0:C2c, C2c:2 * C2c], rhs=Zb[hp][0:C2c, :], start=True, stop=False, skip_group_check=True)
                            P.pe.matmul(out=ps[0:C2c, 0:64], lhsT=NKs[hp][0:C2c, C2c:2 * C2c], rhs=tokm[hp][0:C2c, 128:192], start=False, stop=False, skip_group_check=True)
                            for hh in range(2):
                                P.pe.matmul(out=ps[orow[hh], 0:64], lhsT=ARc(hp)[rows[hh], 1, :], rhs=HB[hp][rows[hh], :], start=False, stop=(hh == 1), skip_group_check=True)
                            P.act.activation(out=Yb[hp][0:C2c, :], in_=ps[0:C2c, 0:64], func=AF.Copy)
                        for hp in range(4):
                            ps = pa.next()
                            for hh in range(2):
                                P.pe.matmul(out=ps[rows[hh], 0:Cc], lhsT=Yb[hp][orow[hh], :], rhs=Ib[orow[hh], orow[hh]], start=True, stop=True)
                            P.dve.tensor_copy(out=yT_t[:, hp, tsl], in_=ps[:, 0:Cc])
                        for hp in range(4):
                            ps = pa.next()
                            for hh in range(2):
                                P.pe.matmul(out=ps[rows[hh], 0:64], lhsT=tokm[hp][orow[hh], 0:64], rhs=Zb[hp][orow[hh], :], start=True, stop=False, skip_group_check=True)
                                P.pe.matmul(out=ps[rows[hh], 0:64], lhsT=tokm[hp][orow[hh], 64:128], rhs=tokm[hp][orow[hh], 128:192], start=False, stop=True, skip_group_check=True)
                            P.dve.tensor_tensor(out=tH[hp], in0=ps[:, 0:64], in1=HF[hp], op=ALU.add)
                            P.pool.tensor_scalar(out=HF[hp], in0=tH[hp], scalar1=eG_t[:, hp, ch * Cc + Cc - 1:ch * Cc + Cc], scalar2=None, op0=ALU.mult)
                            P.act.activation(out=HB[hp], in_=HF[hp], func=AF.Copy)
                    for hp in range(4):
                        yb = tb.next(); sq = tb.next()
                        P.act.activation(out=yb[:, 0:n], in_=yT_t[:, hp, 0:n], func=AF.Copy)
                        P.act.activation(out=sq[:, 0:n], in_=yT_t[:, hp, 0:n], func=AF.Square)
                        pm_ = pa.next(); pe_ = pa.next()
                        P.pe.matmul(out=pm_[:, 0:n], lhsT=BO64_b, rhs=yb[:, 0:n], start=True, stop=True)
                        P.pe.matmul(out=pe_[:, 0:n], lhsT=BO64_b, rhs=sq[:, 0:n], start=True, stop=True)
                        ms = tf.next(); m2 = tf.next(); var = tf.next()
                        P.act.activation(out=ms[:, 0:n], in_=pm_[:, 0:n], func=AF.Copy)
                        P.pool.tensor_tensor(out=m2[:, 0:n], in0=ms[:, 0:n], in1=ms[:, 0:n], op=ALU.mult)
                        P.dve.tensor_tensor(out=var[:, 0:n], in0=pe_[:, 0:n], in1=m2[:, 0:n], op=ALU.subtract)
                        P.dve.tensor_scalar(out=var[:, 0:n], in0=var[:, 0:n], scalar1=0.0, scalar2=None, op0=ALU.max)
                        P.act.activation(out=var[:, 0:n], in_=var[:, 0:n], func=AF.Sqrt, bias=lneps, scale=1.0)
                        P.dve.reciprocal(out=var[:, 0:n], in_=var[:, 0:n])
                        yc = tf.next()
                        P.pool.tensor_tensor(out=yc[:, 0:n], in0=yT_t[:, hp, 0:n], in1=ms[:, 0:n], op=ALU.subtract)
                        P.dve.tensor_tensor(out=yc[:, 0:n], in0=yc[:, 0:n], in1=var[:, 0:n], op=ALU.mult)
                        P.pool.tensor_scalar(out=yc[:, 0:n], in0=yc[:, 0:n], scalar1=V(f"lng{i}", hp, 1), scalar2=V(f"lnb{i}", hp, 1), op0=ALU.mult, op1=ALU.add)
                        P.dve.tensor_tensor(out=yc[:, 0:n], in0=yc[:, 0:n], in1=bonus_t[:, hp, 0:n], op=ALU.add)
                        P.dve.tensor_tensor(out=mixT[:, 4 + hp, 0:n], in0=yc[:, 0:n], in1=g_t[:, hp, 0:n], op=ALU.mult)
                    P.sp.dma_start(mixT[:, 0:4, 0:n], View(omla_d[ti][:, :, 0:n], omla_res[ti]))
                    for oc in range(KC):
                        d_ps = pa.next()
                        for c in range(KC):
                            P.pe.matmul(out=d_ps[:, 0:n], lhsT=wout[:, c, oc * 128:(oc + 1) * 128], rhs=mixT[:, c, 0:n], start=(c == 0), stop=(c == KC - 1))
                        for si, row in enumerate(tk.rows):
                            sl = slice(si * L, (si + 1) * L)
                            P.dve.scalar_tensor_tensor(out=XH["x"][:, oc, sl], in0=d_ps[:, sl], scalar=mod[l][:, 16 + oc, row:row + 1],
                                                       in1=XH["x"][:, oc, sl], op0=ALU.mult, op1=ALU.add)
                    store_x(ti, tk, False)
                P.sp.dma_start(View(o_shift_p[i], Res("o")), shift_p, final=True)
                P.sp.dma_start(View(o_shift_s[i], Res("o")), shift_s, final=True)
                for sq_ in range(NB):
                    for hp in range(4):
                        P.sp.dma_start(View(o_wkv_p[i, sq_, hp], Res("o")), Hf["p", sq_, hp], final=True)
                        P.sp.dma_start(View(o_wkv_s[i, sq_, hp], Res("o")), Hf["s", sq_, hp], final=True)
        if DO_MIX and l % 2 == 1:
            i = l // 2
            P.barrier()
            with P.scope():
                Pm_b, BO96_b, BO64_b, Ib = load_consts([n_ for n_ in CST2 if not n_.startswith("MSI6") and not n_.startswith("MSI3") and n_ not in ("R64", "R32")])
                wino = P.sbuf("wino", [128, KC, 2568], BF16)
                for k in range(KC):
                    P.pool.dma_start(wino[:, k, :], win_o[i][k * 128:(k + 1) * 128, :])
                poolw = P.sbuf("poolw", [128, 4, 128], BF16)
                for gi in range(4):
                    P.pool.dma_start(poolw[:, gi, :], poolw_d[i, gi])
                wout = P.sbuf("wouto", [128, KC, D], BF16)
                for k in range(KC):
                    P.pool.dma_start(wout[:, k, :], wout_o[i][k * 128:(k + 1) * 128, :])
                one_t = P.sbuf("one_t", [128, 1]); P.dve.memset(one_t, 1.0)
                negA = P.sbuf("negA", [128, 1])
                P.act.activation(out=negA, in_=V(f"alog{i}"), func=AF.Exp)
                P.dve.tensor_scalar(out=negA, in0=negA, scalar1=-1.0, scalar2=None, op0=ALU.mult)
                ones128 = P.sbuf("ones128o", [128, 128], BF16); P.dve.memset(ones128, 1.0 / 128)
                phist = {"p": P.sbuf("phist_p", [128, 4, NB, 15]), "s": P.sbuf("phist_s", [128, 4, NB, 15])}
                chist = {"p": P.sbuf("chist_p", [128, 12, NB, 3]), "s": P.sbuf("chist_s", [128, 12, NB, 3])}
                P.dve.memset(phist["p"], 0.0); P.dve.memset(chist["p"], 0.0)
                P.sp.dma_start(phist["s"], pool_in[i]); P.sp.dma_start(chist["s"], gconv_in[i])
                Sf = {}; Sb = {}
                for grp in ("p", "s"):
                    for sq_ in range(NB):
                        for h in range(4):
                            Sf[grp, sq_, h] = P.sbuf(f"Sf{grp}{sq_}{h}", [128, 128])
                            if grp == "p":
                                P.pool.memset(Sf[grp, sq_, h], 0.0)
                            else:
                                P.sp.dma_start(Sf[grp, sq_, h], gdn_in[i, sq_, h])
                Sbc = [P.sbuf(f"Sbc{h}", [128, 128], BF16) for h in range(4)]
                mixT = P.sbuf("mixTo", [128, KC, TT], BF16)
                zs_t = P.sbuf("zs_t", [128, 4, TT]); oT_t = P.sbuf("oT_t", [128, 4, TT])
                KQ = P.sbuf("KQ", [128, 4, 2 * TT], BF16); k_b = P.sbuf("k_b", [128, 4, TT], BF16); v_b = P.sbuf("v_bo", [128, 4, TT], BF16)
                sig8 = P.sbuf("sig8", [8, TT]); g8 = P.sbuf("g8", [8, TT])
                ubs = Rot([P.sbuf(f"ub{j}", [128, TT + 15 * NB]) for j in range(2)])
                sAB = [P.sbuf(f"sAB{j}", [128, TT + 15 * NB]) for j in range(2)]
                cbs = Rot([P.sbuf(f"cbf{j}", [128, TT + 3 * NB]) for j in range(2)])
                tf = Rot([P.sbuf(f"tfo{j}", [128, TT]) for j in range(6)])
                tb = Rot([P.sbuf(f"tbo{j}", [128, TT], BF16) for j in range(3)])
                pa = Rot(psb[0:7])

                def mk(name, shape, dt):
                    return [P.sbuf(f"{name}{hp}", shape, dt) for hp in range(2)]
                def mk2(name, shape, dt):
                    return [[P.sbuf(f"{name}{par_}_{hp}", shape, dt) for hp in range(2)] for par_ in range(2)]
                G2 = dict(cols_s=mk2("cols", [128, 3], F32), ghl=mk2("ghl", [128, 2], BF16), TGh=mk2("TGh", [128, 128], BF16), TGl=mk2("TGl", [128, 128], BF16))
                cbo = {}
                for nm in ("OH8", "ONB64", "NONB64", "TRI64", "STRI64", "BLK64_0", "BLK64_1", "ONB32", "NONB32", "TRI32", "STRI32", "BLK32_0", "BLK32_1",
                           "SEL8_0", "SEL8_1", "SEL8_2", "SEL8_3"):
                    cbo[nm] = P.sbuf("cbo_" + nm, [128, CST2[nm][1]], BF16)
                    P.dve.tensor_copy(out=cbo[nm], in_=C2(nm))
                sig8b = P.sbuf("sig8b", [8, TT], BF16); g8h = P.sbuf("g8h", [8, TT], BF16); g8l = P.sbuf("g8l", [8, TT], BF16); g8r = P.sbuf("g8r", [8, TT])
                G2.update(dict(dmat=mk2("dmat", [128, 128], F32), EM=mk2("EM", [128, 256], F32), eGcol=mk2("eGcol", [128, 1], F32),
                               wcol=mk2("wcol", [128, 1], F32), egc0=mk2("egc0_", [128, 1], F32), egc1=mk2("egc1_", [128, 1], F32),
                               NQs=mk2("NQs", [128, 256], BF16), Nm0=mk2("oNm0_", [128, 128], BF16), Lp0=mk2("oLp0_", [128, 128], BF16),
                               KW=mk2("KW", [128, 128], BF16), tks=mk2("tks", [128, 128], F32), Xf=mk2("oXf", [128, 128], F32), Xb=mk2("oXb", [128, 128], BF16),
                               tqs=mk2("tqs", [128, 128], F32), o_b=mk2("o_b", [128, 128], BF16),
                               Tm=mk2("Tm", [128, 128], BF16), TTm=mk2("TTm", [128, 128], BF16), Wb=mk2("Wb", [128, 128], BF16), Ub=mk2("Ub", [128, 128], BF16)))

                def proj(c0, m, n):
                    ps = pa.next()
                    for k in range(KC):
                        P.pe.matmul(out=ps[0:m, 0:n], lhsT=wino[:, k, c0:c0 + m], rhs=XH["h"][:, k, 0:n], start=(k == 0), stop=(k == KC - 1))
                    return ps

                for ti, tk in enumerate(tiles):
                    use_buf(ti % 2)
                    n, L, ns = tk.TT, tk.L, tk.nseg
                    grp = "s" if tk.sample else "p"
                    Cc = 32 if tk.sample else 64
                    nch = n // Cc
                    C2c = 2 * Cc
                    NM = 5 if tk.sample else 6
                    s0 = tk.seqs[0]
                    load_x(ti, tk, False)
                    modulate(tk, modA_m[l], mod[l][:, 0:8, :])
                    if ODD_STAGE < 9:
                        P.dve.memset(mixT[:, :, 0:n], 0.0); P.dve.memset(oT_t[:, :, 0:n], 0.0)
                    for gi, w in enumerate((2, 4, 8, 16) if ODD_STAGE >= 1 else ()):
                        ps = proj(gi * 128, 128, n)
                        ub = ubs.next()
                        u3 = ub[:, 0:ns * (L + 15)].rearrange("p (s t) -> p s t", s=ns)
                        P.pool.tensor_copy(out=u3[:, :, 0:15], in_=phist[grp][:, gi, s0:s0 + ns, :])
                        P.act.activation(out=u3[:, :, 15:15 + L], in_=ps[:, 0:n].rearrange("p (s t) -> p s t", s=ns), func=AF.Copy)
                        P.pool.tensor_copy(out=phist[grp][:, gi, s0:s0 + ns, :], in_=u3[:, :, L:L + 15])
                        cur = u3
                        sh = 1
                        for lev in range(gi + 1):
                            nxt = sAB[lev % 2][:, 0:ns * (L + 15)].rearrange("p (s t) -> p s t", s=ns)
                            lo = 2 * sh - 1
                            eng = P.dve if lev % 2 == 0 else P.pool
                            eng.tensor_tensor(out=nxt[:, :, lo:L + 15], in0=cur[:, :, lo:L + 15], in1=cur[:, :, lo - sh:L + 15 - sh], op=ALU.add)
                            cur = nxt
                            sh *= 2
                        df = tb.next()
                        d3 = df[:, 0:n].rearrange("p (s t) -> p s t", s=ns)
                        P.dve.scalar_tensor_tensor(out=d3, in0=cur[:, :, 15:15 + L], scalar=1.0 / w, in1=u3[:, :, 15:15 + L], op0=ALU.mult, op1=ALU.subtract)
                        if tk.first and not tk.sample:
                            t_ = tf.next()
                            P.dve.tensor_tensor(out=t_[:, 0:15], in0=cur[:, 0, 15:30], in1=C2("ICT", gi * 15, 15), op=ALU.mult)
                            P.dve.tensor_tensor(out=df[:, 0:15], in0=t_[:, 0:15], in1=u3[:, 0, 15:30], op=ALU.subtract)
                        ps = pa.next()
                        P.pe.matmul(out=ps[:, 0:n], lhsT=poolw[:, gi, :], rhs=df[:, 0:n], start=True, stop=True)
                        P.dve.tensor_scalar(out=mixT[:, gi, 0:n], in0=ps[:, 0:n], scalar1=V(f"pscale{i}", gi, 1), scalar2=None, op0=ALU.mult)
                    if ODD_STAGE < 2:
                        continue_ = True
                    ps = proj(2560, 8, n)
                    P.act.activation(out=sig8[:, 0:n], in_=ps[0:8, 0:n], func=AF.Sigmoid)
                    e8 = tf.next()
                    P.act.activation(out=e8[0:8, 0:n], in_=ps[0:8, 0:n], func=AF.Exp, bias=V(f"dtb{i}")[0:8, :], scale=1.0)
                    P.act.activation(out=e8[0:8, 0:n], in_=e8[0:8, 0:n], func=AF.Ln, bias=one_t[0:8, :], scale=1.0)
                    P.dve.tensor_scalar(out=g8[:, 0:n], in0=e8[0:8, 0:n], scalar1=negA[0:8, :], scalar2=None, op0=ALU.mult)
                    P.act.activation(out=sig8b[:, 0:n], in_=sig8[:, 0:n], func=AF.Copy)
                    P.act.activation(out=g8h[:, 0:n], in_=g8[:, 0:n], func=AF.Copy)
                    P.dve.tensor_tensor(out=g8r[:, 0:n], in0=g8[:, 0:n], in1=g8h[:, 0:n], op=ALU.subtract)
                    P.act.activation(out=g8l[:, 0:n], in_=g8r[:, 0:n], func=AF.Copy)
                    for j in range(12):
                        ps = proj(512 + j * 128, 128, n)
                        cbf = cbs.next()
                        c3 = cbf[:, 0:ns * (L + 3)].rearrange("p (s t) -> p s t", s=ns)
                        P.pool.tensor_copy(out=c3[:, :, 0:3], in_=chist[grp][:, j, s0:s0 + ns, :])
                        P.act.activation(out=c3[:, :, 3:3 + L], in_=ps[:, 0:n].rearrange("p (s t) -> p s t", s=ns), func=AF.Copy)
                        P.pool.tensor_copy(out=chist[grp][:, j, s0:s0 + ns, :], in_=c3[:, :, L:L + 3])
                        acc = tf.next()
                        a3 = acc[:, 0:n].rearrange("p (s t) -> p s t", s=ns)
                        P.act.activation(out=a3, in_=c3[:, :, 0:L], func=AF.Copy, scale=V(f"gcw{i}_0", j, 1))
                        for t in range(1, 4):
                            P.dve.scalar_tensor_tensor(out=a3, in0=c3[:, :, t:t + L], scalar=V(f"gcw{i}_{t}", j, 1), in1=a3, op0=ALU.mult, op1=ALU.add)
                        sl_ = tf.next()
                        P.act.activation(out=sl_[:, 0:n], in_=acc[:, 0:n], func=AF.Silu)
                        h = j % 4
                        if j < 8:
                            sq = tb.next()
                            P.act.activation(out=sq[:, 0:n], in_=sl_[:, 0:n], func=AF.Square)
                            ps2 = pa.next()
                            P.pe.matmul(out=ps2[:, 0:n], lhsT=ones128, rhs=sq[:, 0:n], start=True, stop=True)
                            rs = tf.next()
                            P.act.activation(out=rs[:, 0:n], in_=ps2[:, 0:n], func=AF.Sqrt, bias=eps_t, scale=128.0)
                            P.dve.reciprocal(out=rs[:, 0:n], in_=rs[:, 0:n])
                            if j < 4:
                                KQ4 = KQ[:, h, 0:2 * n].rearrange("p (q a t) -> p q a t", a=2, t=Cc)
                                P.dve.scalar_tensor_tensor(out=KQ4[:, :, 1, :], in0=sl_[:, 0:n].rearrange("p (q t) -> p q t", t=Cc), scalar=float(128 ** -0.5),
                                                           in1=rs[:, 0:n].rearrange("p (q t) -> p q t", t=Cc), op0=ALU.mult, op1=ALU.mult)
                            else:
                                P.dve.tensor_tensor(out=k_b[:, h, 0:n], in0=sl_[:, 0:n], in1=rs[:, 0:n], op=ALU.mult)
                                psb_ = pa.next()
                                P.pe.matmul(out=psb_[:, 0:n], lhsT=cbo[f"SEL8_{h}"][0:8, :], rhs=sig8b[0:8, 0:n], start=True, stop=True)
                                KQ4 = KQ[:, h, 0:2 * n].rearrange("p (q a t) -> p q a t", a=2, t=Cc)
                                P.dve.tensor_tensor(out=KQ4[:, :, 0, :], in0=psb_[:, 0:n].rearrange("p (q t) -> p q t", t=Cc),
                                                    in1=k_b[:, h, 0:n].rearrange("p (q t) -> p q t", t=Cc), op=ALU.mult)
                        else:
                            P.act.activation(out=v_b[:, h, 0:n], in_=sl_[:, 0:n], func=AF.Copy)
                    for h in range(4):
                        ps = proj(2048 + h * 128, 128, n)
                        P.act.activation(out=zs_t[:, h, 0:n], in_=ps[:, 0:n], func=AF.Silu)
                    MSIN = C2(f"MSIN{Cc}"); TRI = C2(f"TRI{Cc}")
                    orow = [slice(0, Cc), slice(Cc, C2c)]
                    for ch in range(nch if ODD_STAGE >= 3 else 0):
                        seq = tk.seqs[ch] if tk.sample else tk.seqs[0]
                        tsl = slice(ch * Cc, (ch + 1) * Cc)
                        KQc = lambda h: KQ[:, h, 0:2 * n].rearrange("p (q a t) -> p q a t", a=2, t=Cc)[:, ch]
                        if tk.sample or (tk.first and ch == 0):
                            for h_ in range(4):
                                P.act.activation(out=Sbc[h_], in_=Sf[grp, seq, h_], func=AF.Copy)
                        gq_ = {k_: v_[ch % 2] for k_, v_ in G2.items()}
                        cols_s, ghl, TGh, TGl, dmat, EM, eGcol, wcol = (gq_[k_] for k_ in ("cols_s", "ghl", "TGh", "TGl", "dmat", "EM", "eGcol", "wcol"))
                        egc = [gq_["egc0"], gq_["egc1"]]
                        NQs, KW, tks, Xf, Xb, tqs, o_b, Tm, TTm, Wb, Ub = (gq_[k_] for k_ in ("NQs", "KW", "tks", "Xf", "Xb", "tqs", "o_b", "Tm", "TTm", "Wb", "Ub"))
                        Nm = [gq_["Nm0"]]; Lp = [gq_["Lp0"]]
                        for hp in range(2 if SUB >= 1 else 0):
                            ps = pa.next()
                            for hh in range(2):
                                h = 2 * hp + hh
                                P.pe.matmul(out=ps[orow[hh], 0:1], lhsT=sig8b[0:8, tsl], rhs=cbo["OH8"][0:8, h:h + 1], start=True, stop=True)
                                P.pe.matmul(out=ps[orow[hh], 1:2], lhsT=g8h[0:8, tsl], rhs=cbo["OH8"][0:8, 4 + h:5 + h], start=True, stop=True)
                                P.pe.matmul(out=ps[orow[hh], 2:3], lhsT=g8l[0:8, tsl], rhs=cbo["OH8"][0:8, 4 + h:5 + h], start=True, stop=True)
                            P.act.activation(out=cols_s[hp][0:C2c, 0:3], in_=ps[0:C2c, 0:3], func=AF.Copy)
                            P.dve.tensor_copy(out=ghl[hp][0:C2c, :], in_=cols_s[hp][0:C2c, 1:3])
                            P.pool.tensor_scalar(out=TGh[hp][0:C2c, 0:C2c], in0=TRI[0:C2c, :], scalar1=cols_s[hp][0:C2c, 1:2], scalar2=None, op0=ALU.mult)
                            P.pool.tensor_scalar(out=TGl[hp][0:C2c, 0:C2c], in0=TRI[0:C2c, :], scalar1=cols_s[hp][0:C2c, 2:3], scalar2=None, op0=ALU.mult)
                            ps = pa.next()
                            P.pe.matmul(out=ps[0:C2c, 0:C2c], lhsT=cbo[f"ONB{Cc}"][0:C2c, 0:C2c], rhs=TGh[hp][0:C2c, 0:C2c], start=True, stop=False)
                            P.pe.matmul(out=ps[0:C2c, 0:C2c], lhsT=cbo[f"ONB{Cc}"][0:C2c, 0:C2c], rhs=TGl[hp][0:C2c, 0:C2c], start=False, stop=False)
                            P.pe.matmul(out=ps[0:C2c, 0:C2c], lhsT=TGh[hp][0:C2c, 0:C2c], rhs=cbo[f"NONB{Cc}"][0:C2c, 0:C2c], start=False, stop=False)
                            P.pe.matmul(out=ps[0:C2c, 0:C2c], lhsT=TGl[hp][0:C2c, 0:C2c], rhs=cbo[f"NONB{Cc}"][0:C2c, 0:C2c], start=False, stop=True)
                            P.dve.tensor_scalar(out=dmat[hp][0:C2c, 0:C2c], in0=ps[0:C2c, 0:C2c], scalar1=0.0, scalar2=None, op0=ALU.min)
                            P.act.activation(out=dmat[hp][0:C2c, 0:C2c], in_=dmat[hp][0:C2c, 0:C2c], func=AF.Exp)
                            P.dve.tensor_tensor(out=EM[hp][0:C2c, 0:2 * C2c].rearrange("p (a t) -> p a t", a=2), in0=MSIN[0:C2c, :].rearrange("p (a t) -> p a t", a=2),
                                                in1=dmat[hp][0:C2c, 0:C2c].bc3(1, 2), op=ALU.mult)
                            ps = pa.next()
                            for q_ in range(2):
                                P.pe.matmul(out=ps[0:C2c, 0:1], lhsT=cbo[f"TRI{Cc}"][0:C2c, 0:C2c], rhs=ghl[hp][0:C2c, q_:q_ + 1], start=(q_ == 0), stop=(q_ == 1))
                            ps2 = pa.next()
                            for q_ in range(2):
                                P.pe.matmul(out=ps2[0:C2c, 0:1], lhsT=cbo[f"STRI{Cc}"][0:C2c, 0:C2c], rhs=ghl[hp][0:C2c, q_:q_ + 1], start=(q_ == 0), stop=(q_ == 1))
                            P.act.activation(out=eGcol[hp][0:C2c, :], in_=ps[0:C2c, 0:1], func=AF.Exp)
                            P.act.activation(out=wcol[hp][0:C2c, :], in_=ps2[0:C2c, 0:1], func=AF.Exp)
                            for hh in range(2):
                                ps = pa.next()
                                for q_ in range(2):
                                    P.pe.matmul(out=ps[:, 0:1], lhsT=cbo[f"BLK{Cc}_{hh}"][0:C2c, :], rhs=ghl[hp][0:C2c, q_:q_ + 1], start=(q_ == 0), stop=(q_ == 1))
                                P.act.activation(out=egc[hh][hp], in_=ps[:, 0:1], func=AF.Exp)
                        for hp in range(2 if SUB >= 2 else 0):
                            ps = pa.next()
                            for hh in range(2):
                                h = 2 * hp + hh
                                for a_ in range(2):
                                    P.pe.matmul(out=ps[orow[hh], a_ * C2c + hh * Cc:a_ * C2c + (hh + 1) * Cc], lhsT=k_b[:, h, tsl], rhs=KQc(h)[:, a_, :], start=True, stop=True)
                            P.dve.tensor_tensor(out=NQs[hp][0:C2c, 0:2 * C2c], in0=ps[0:C2c, 0:2 * C2c], in1=EM[hp][0:C2c, 0:2 * C2c], op=ALU.mult)
                        for hp in range(2 if SUB >= 3 else 0):
                            P.pool.tensor_copy(out=Nm[0][hp][0:C2c, 0:C2c], in_=NQs[hp][0:C2c, 0:C2c])
                            ps = pa.next()
                            P.pe.matmul(out=ps[0:C2c, 0:C2c], lhsT=NQs[hp][0:C2c, 0:C2c], rhs=Ib[0:C2c, 0:C2c], start=True, stop=True)
                            P.act.activation(out=Lp[0][hp][0:C2c, 0:C2c], in_=ps[0:C2c, 0:C2c], func=AF.Copy)
                        LV = NM
                        for hp in range(2):
                            tmp_ = tks[hp]
                            P.dve.tensor_tensor(out=tmp_[0:C2c, 0:C2c], in0=Lp[0][hp][0:C2c, 0:C2c], in1=C2(f"MK{Cc}_0")[0:C2c, :], op=ALU.mult)
                            P.dve.tensor_tensor(out=Tm[hp][0:C2c, 0:C2c], in0=tmp_[0:C2c, 0:C2c], in1=C2("I128")[0:C2c, 0:C2c], op=ALU.add)
                            ps = pa.next()
                            P.pe.matmul(out=ps[0:C2c, 0:C2c], lhsT=Tm[hp][0:C2c, 0:C2c], rhs=Ib[0:C2c, 0:C2c], start=True, stop=True)
                            P.act.activation(out=TTm[hp][0:C2c, 0:C2c], in_=ps[0:C2c, 0:C2c], func=AF.Copy)
                        for lv in range(1, LV):
                            for hp in range(2):
                                ps = pa.next()
                                P.pe.matmul(out=ps[0:C2c, 0:C2c], lhsT=Nm[0][hp][0:C2c, 0:C2c], rhs=Tm[hp][0:C2c, 0:C2c], start=True, stop=True)
                                P.act.activation(out=Wb[hp][0:C2c, 0:C2c], in_=ps[0:C2c, 0:C2c], func=AF.Copy)
                                ps = pa.next()
                                P.pe.matmul(out=ps[0:C2c, 0:C2c], lhsT=TTm[hp][0:C2c, 0:C2c], rhs=Wb[hp][0:C2c, 0:C2c], start=True, stop=True)
                                tmp_ = tks[hp]
                                P.dve.tensor_tensor(out=tmp_[0:C2c, 0:C2c], in0=ps[0:C2c, 0:C2c], in1=C2(f"MK{Cc}_{lv}")[0:C2c, :], op=ALU.mult)
                                P.dve.tensor_tensor(out=Tm[hp][0:C2c, 0:C2c], in0=tmp_[0:C2c, 0:C2c], in1=Tm[hp][0:C2c, 0:C2c], op=ALU.add)
                                ps = pa.next()
                                P.pe.matmul(out=ps[0:C2c, 0:C2c], lhsT=Tm[hp][0:C2c, 0:C2c], rhs=Ib[0:C2c, 0:C2c], start=True, stop=True)
                                P.act.activation(out=TTm[hp][0:C2c, 0:C2c], in_=ps[0:C2c, 0:C2c], func=AF.Copy)
                        for hp in range(2 if SUB >= 5 else 0):
                            pT = pa.next()
                            for hh in range(2):
                                h = 2 * hp + hh
                                P.pe.matmul(out=pT[orow[hh], 0:128], lhsT=v_b[:, h, tsl], rhs=Ib, start=True, stop=True)
                                P.pe.matmul(out=pT[orow[hh], 128:256], lhsT=k_b[:, h, tsl], rhs=Ib, start=True, stop=True)
                            P.dve.tensor_scalar(out=KW[hp][0:C2c, :], in0=pT[0:C2c, 128:256], scalar1=wcol[hp][0:C2c, :], scalar2=None, op0=ALU.mult)
                            pK = pa.next()
                            for hh in range(2):
                                h = 2 * hp + hh
                                P.pe.matmul(out=pK[orow[hh], 0:128], lhsT=k_b[:, h, tsl], rhs=Sbc[h], start=True, stop=True)
                            P.dve.tensor_scalar(out=tks[hp][0:C2c, :], in0=pK[0:C2c, 0:128], scalar1=eGcol[hp][0:C2c, :], scalar2=None, op0=ALU.mult)
                            P.dve.tensor_tensor(out=Xf[hp][0:C2c, :], in0=pT[0:C2c, 0:128], in1=tks[hp][0:C2c, :], op=ALU.subtract)
                            P.pool.tensor_scalar(out=Xf[hp][0:C2c, :], in0=Xf[hp][0:C2c, :], scalar1=cols_s[hp][0:C2c, 0:1], scalar2=None, op0=ALU.mult)
                            P.act.activation(out=Xb[hp][0:C2c, :], in_=Xf[hp][0:C2c, :], func=AF.Copy)
                        for hp in range(2):
                            ps = pa.next()
                            P.pe.matmul(out=ps[0:C2c, 0:128], lhsT=TTm[hp][0:C2c, 0:C2c], rhs=Xb[hp][0:C2c, :], start=True, stop=True)
                            P.act.activation(out=Ub[hp][0:C2c, :], in_=ps[0:C2c, 0:128], func=AF.Copy)
                        for hp in range(2 if SUB >= 7 else 0):
                            pQ = pa.next()
                            for hh in range(2):
                                h = 2 * hp + hh
                                P.pe.matmul(out=pQ[orow[hh], 0:128], lhsT=KQc(h)[:, 1, :], rhs=Sbc[h], start=True, stop=True)
                            P.dve.tensor_scalar(out=tqs[hp][0:C2c, :], in0=pQ[0:C2c, 0:128], scalar1=eGcol[hp][0:C2c, :], scalar2=None, op0=ALU.mult)
                            pO = pa.next()
                            P.pe.matmul(out=pO[0:C2c, 0:128], lhsT=NQs[hp][0:C2c, C2c:2 * C2c], rhs=Ub[hp][0:C2c, :], start=True, stop=True)
                            P.dve.tensor_tensor(out=o_b[hp][0:C2c, :], in0=pO[0:C2c, 0:128], in1=tqs[hp][0:C2c, :], op=ALU.add)
                            if SUB >= 8:
                                for hh in range(2):
                                    pOT = pa.next()
                                    P.pe.matmul(out=pOT[:, 0:Cc], lhsT=o_b[hp][orow[hh], :], rhs=Ib[orow[hh], orow[hh]], start=True, stop=True)
                                    P.act.activation(out=oT_t[:, 2 * hp + hh, tsl], in_=pOT[:, 0:Cc], func=AF.Copy)
                            for hh in range(2 if SUB >= 9 else 0):
                                h = 2 * hp + hh
                                pS = pa.next()
                                P.pe.matmul(out=pS[:, 0:128], lhsT=KW[hp][orow[hh], :], rhs=Ub[hp][orow[hh], :], start=True, stop=True)
                                P.dve.scalar_tensor_tensor(out=Sf[grp, seq, h], in0=Sf[grp, seq, h], scalar=egc[hh][hp], in1=pS[:, 0:128], op0=ALU.mult, op1=ALU.add)
                                P.act.activation(out=Sbc[h], in_=Sf[grp, seq, h], func=AF.Copy)
                    for h in range(4):
                        sq = tb.next()
                        P.act.activation(out=sq[:, 0:n], in_=oT_t[:, h, 0:n], func=AF.Square)
                        ps = pa.next()
                        P.pe.matmul(out=ps[:, 0:n], lhsT=ones128, rhs=sq[:, 0:n], start=True, stop=True)
                        rs = tf.next()
                        P.act.activation(out=rs[:, 0:n], in_=ps[:, 0:n], func=AF.Sqrt, bias=eps_t, scale=1.0)
                        P.dve.reciprocal(out=rs[:, 0:n], in_=rs[:, 0:n])
                        t_ = tf.next()
                        P.dve.scalar_tensor_tensor(out=t_[:, 0:n], in0=oT_t[:, h, 0:n], scalar=V(f"gog{i}"), in1=rs[:, 0:n], op0=ALU.mult, op1=ALU.mult)
                        P.pool.tensor_tensor(out=mixT[:, 4 + h, 0:n], in0=t_[:, 0:n], in1=zs_t[:, h, 0:n], op=ALU.mult)
                    for oc in range(KC):
                        d_ps = pa.next()
                        for c in range(KC):
                            P.pe.matmul(out=d_ps[:, 0:n], lhsT=wout[:, c, oc * 128:(oc + 1) * 128], rhs=mixT[:, c, 0:n], start=(c == 0), stop=(c == KC - 1))
                        for si, row in enumerate(tk.rows):
                            sl = slice(si * L, (si + 1) * L)
                            P.dve.scalar_tensor_tensor(out=XH["x"][:, oc, sl], in0=d_ps[:, sl], scalar=mod[l][:, 16 + oc, row:row + 1],
                                                       in1=XH["x"][:, oc, sl], op0=ALU.mult, op1=ALU.add)
                    store_x(ti, tk, False)
                for grp, (op_, oc_, og_) in (("p", (o_pool_p, o_gconv_p, o_gdn_p)), ("s", (o_pool_s, o_gconv_s, o_gdn_s))):
                    P.sp.dma_start(View(op_[i], Res("o")), phist[grp], final=True)
                    P.sp.dma_start(View(oc_[i], Res("o")), chist[grp], final=True)
                    for sq_ in range(NB):
                        for h in range(4):
                            P.sp.dma_start(View(og_[i, sq_, h], Res("o")), Sf[grp, sq_, h], final=True)
        P.barrier()
        with P.scope():
            wg = P.sbuf("wg", [128, KC, DFF], BF16); wu = P.sbuf("wu", [128, KC, DFF], BF16)
            wd = P.sbuf("wd", [128, FC, D], BF16)
            for k in range(KC):
                P.pool.dma_start(wg[:, k, :], wg_d[l][k * 128:(k + 1) * 128, :])
                P.pool.dma_start(wu[:, k, :], wu_d[l][k * 128:(k + 1) * 128, :])
            for c in range(FC):
                P.pool.dma_start(wd[:, c, :], wd_d[l][c * 128:(c + 1) * 128, :])
            hist_p = P.sbuf("hist_p", [128, FC, NB, 2]); hist_s = P.sbuf("hist_s", [128, FC, NB, 2])
            P.dve.memset(hist_p, 0.0)
            P.sp.dma_start(hist_s, fh_d[l])
            act_t = P.sbuf("act_t", [128, FC, TTF], BF16)
            gbufs = Rot([P.sbuf(f"gbuf{i}", [128, TTF + 2 * NB]) for i in range(2)])
            accs = Rot([P.sbuf(f"acc{i}", [128, TTF]) for i in range(2)])
            gps = Rot(psb[0:3]); ups = Rot(psb[3:6]); dps = Rot(psb[0:6])
            first_layer = (l == 0 and not DO_MIX) or False
            for ti, tk in enumerate(ftiles):
                use_buf("full" if MK_NT % 2 == 0 else 0)
                n, L, ns = tk.TT, tk.L, tk.nseg
                load_x(ti, tk, l == 0 and not DO_MIX)
                modulate(tk, modA_f[l], mod[l][:, 24:32, :])
                hist = hist_s if tk.sample else hist_p
                s0 = tk.seqs[0]
                for c in range(FC):
                    g_ps = gps.next(); u_ps = ups.next()
                    for k in range(KC):
                        P.pe.matmul(out=g_ps[:, 0:n], lhsT=wg[:, k, c * 128:(c + 1) * 128], rhs=XH["h"][:, k, 0:n],
                                    start=(k == 0), stop=(k == KC - 1))
                    for k in range(KC):
                        P.pe.matmul(out=u_ps[:, 0:n], lhsT=wu[:, k, c * 128:(c + 1) * 128], rhs=XH["h"][:, k, 0:n],
                                    start=(k == 0), stop=(k == KC - 1))
                    gb = gbufs.next()
                    gb3 = gb[:, 0:ns * (L + 2)].rearrange("p (s t) -> p s t", s=ns)
                    P.pool.tensor_copy(out=gb3[:, :, 0:2], in_=hist[:, c, s0:s0 + ns, :])
                    P.act.activation(out=gb3[:, :, 2:L + 2], in_=g_ps[:, 0:n].rearrange("p (s t) -> p s t", s=ns), func=AF.Copy)
                    P.pool.tensor_copy(out=hist[:, c, s0:s0 + ns, :], in_=gb3[:, :, L:L + 2])
                    acc = accs.next()
                    a3 = acc[:, 0:n].rearrange("p (s t) -> p s t", s=ns)
                    P.act.activation(out=a3, in_=gb3[:, :, 0:L], func=AF.Copy, scale=V(f"fcw{l}_0", c, 1))
                    P.dve.scalar_tensor_tensor(out=a3, in0=gb3[:, :, 1:L + 1], scalar=V(f"fcw{l}_1", c, 1), in1=a3, op0=ALU.mult, op1=ALU.add)
                    P.dve.scalar_tensor_tensor(out=a3, in0=gb3[:, :, 2:L + 2], scalar=V(f"fcw{l}_2", c, 1), in1=a3, op0=ALU.mult, op1=ALU.add)
                    P.act.activation(out=acc[:, 0:n], in_=acc[:, 0:n], func=AF.Silu)
                    P.dve.tensor_tensor(out=act_t[:, c, 0:n], in0=u_ps[:, 0:n], in1=acc[:, 0:n], op=ALU.mult)
                for oc in range(KC):
                    d_ps = dps.next()
                    for c in range(FC):
                        P.pe.matmul(out=d_ps[:, 0:n], lhsT=wd[:, c, oc * 128:(oc + 1) * 128], rhs=act_t[:, c, 0:n],
                                    start=(c == 0), stop=(c == FC - 1))
                    for si, row in enumerate(tk.rows):
                        sl = slice(si * L, (si + 1) * L)
                        P.dve.scalar_tensor_tensor(out=XH["x"][:, oc, sl], in0=d_ps[:, sl], scalar=mod[l][:, 40 + oc, row:row + 1],
                                                   in1=XH["x"][:, oc, sl], op0=ALU.mult, op1=ALU.add)
                store_x(ti, tk, l == NLAYERS - 1)
            P.sp.dma_start(View(o_ffn_p[l], Res("o")), hist_p, final=True)
            P.sp.dma_start(View(o_ffn_s[l], Res("o")), hist_s, final=True)
    stats = P.finish()
    return nc, stats


_CACHE = {}


def kernel(**inputs):
    f32 = np.float32
    inp = {k: np.asarray(v) for k, v in inputs.items()}
    if "nc" not in _CACHE:
        _CACHE["nc"], _CACHE["stats"] = build_program()
    nc = _CACHE["nc"]
    vl = vec_layout(inp)
    vecs = vl.array()
    cst_np = const_array(); cst2_np = const_array2()
    B = inp["x_prompt"].shape[0]
    in_maps = []
    fh_all = inp["state_ffn_conv"]
    for c in range(NCORES):
        bs = slice(c * NB, (c + 1) * NB)
        m = {}
        m["xp"] = np.ascontiguousarray(inp["x_prompt"][bs].transpose(0, 2, 1).reshape(NB, KC, 128, SEQ), dtype=f32)
        m["xs"] = np.ascontiguousarray(inp["x_sample"][bs].transpose(0, 2, 1).reshape(NB, KC, 128, DSEQ), dtype=f32)
        cc = np.concatenate([inp["c_prompt"][bs], inp["c_sample"][bs]], axis=0)
        m["cT"] = np.ascontiguousarray(cc.T.reshape(KC, 128, 8).transpose(1, 0, 2), dtype=f32)
        m["vecs"] = vecs
        m["ada_w"] = inp["ada_w"]; m["ffn_w_gate"] = inp["ffn_w_gate"]; m["ffn_w_up"] = inp["ffn_w_up"]
        m["ffn_w_down"] = inp["ffn_w_down"]
        m["ffn_hist"] = np.ascontiguousarray(fh_all[:, bs].reshape(DEPTH, NB, 2, FC, 128).transpose(0, 4, 3, 1, 2), dtype=f32)
        m["consts"] = cst_np; m["consts2"] = cst2_np
        for k in ("rwkv_w2", "rwkv_a2", "rwkv_g2"):
            m[k] = inp[k]
        m["shift_in"] = np.ascontiguousarray(inp["state_rwkv_shift"][:, bs].reshape(2, NB, 14, 128).transpose(0, 3, 2, 1), dtype=f32)
        m["wkv_in"] = np.ascontiguousarray(inp["state_rwkv_wkv"][:, bs].reshape(2, NB, 4, 2, 64, 64).transpose(0, 1, 2, 3, 5, 4).reshape(2, NB, 4, 128, 64), dtype=f32)
        for k in ("even_w_in", "mla_w_uq", "mla_w_ukv", "even_w_out"):
            m[k] = inp[k]
        m["ckv_past"] = np.ascontiguousarray(inp["cache_mla_ckv"][:, bs].transpose(0, 1, 3, 2), dtype=f32)
        m["kpe_past"] = np.ascontiguousarray(inp["cache_mla_kpe"][:, bs].transpose(0, 1, 3, 2), dtype=f32)
        for k in ("odd_w_in", "pool_w", "odd_w_out"):
            m[k] = inp[k]
        m["pool_in"] = np.ascontiguousarray(inp["state_pool"][:, bs].reshape(2, NB, 15, 4, 128).transpose(0, 4, 3, 1, 2), dtype=f32)
        m["gconv_in"] = np.ascontiguousarray(inp["state_gdn_conv"][:, bs].reshape(2, NB, 3, 12, 128).transpose(0, 4, 3, 1, 2), dtype=f32)
        m["gdn_in"] = np.ascontiguousarray(inp["state_gdn"][:, bs], dtype=f32)
        in_maps.append(m)
    res = run_bass_kernel_spmd(nc, in_maps[:MK_CORES], core_ids=list(range(MK_CORES)))
    R = list(res.results) + [res.results[0]] * (NCORES - MK_CORES)

    def cat(name, fn):
        return np.concatenate([fn(np.asarray(r[name])) for r in R], axis=0)

    y_prompt = cat("yp", lambda a: a.reshape(NB, D, SEQ).transpose(0, 2, 1))
    y_sample = cat("ys", lambda a: a.reshape(NB, D, DSEQ).transpose(0, 2, 1))
    ffn_fix = lambda a: a.transpose(0, 3, 4, 2, 1).reshape(DEPTH, NB, 2, DFF)
    p_ffn = np.concatenate([ffn_fix(np.asarray(r["o_ffn_p"])) for r in R], axis=1)
    s_ffn = np.concatenate([ffn_fix(np.asarray(r["o_ffn_s"])) for r in R], axis=1)
    _CACHE["raw"] = R
    tr = lambda name: np.concatenate([np.asarray(r[name]).transpose(0, 1, 3, 2) for r in R], axis=1)
    p_ckv, p_kpe, s_ckv, s_kpe = tr("o_ckv_p"), tr("o_kpe_p"), tr("o_ckv_s"), tr("o_kpe_s")
    shf = lambda name: np.concatenate([np.asarray(r[name]).transpose(0, 3, 2, 1).reshape(2, NB, 1792) for r in R], axis=1)
    p_shift, s_shift = shf("o_shift_p"), shf("o_shift_s")
    wkvf = lambda name: np.concatenate([np.asarray(r[name]).reshape(2, NB, 4, 2, 64, 64).transpose(0, 1, 2, 3, 5, 4).reshape(2, NB, 8, 64, 64) for r in R], axis=1)
    p_wkv, s_wkv = wkvf("o_wkv_p"), wkvf("o_wkv_s")
    poolf = lambda name: np.concatenate([np.asarray(r[name]).transpose(0, 3, 4, 2, 1).reshape(2, NB, 15, 512) for r in R], axis=1)
    gcf = lambda name: np.concatenate([np.asarray(r[name]).transpose(0, 3, 4, 2, 1).reshape(2, NB, 3, 1536) for r in R], axis=1)
    gdf = lambda name: np.concatenate([np.asarray(r[name]) for r in R], axis=1)
    p_pool, s_pool, p_gc, s_gc, p_gdn, s_gdn = poolf("o_pool_p"), poolf("o_pool_s"), gcf("o_gconv_p"), gcf("o_gconv_s"), gdf("o_gdn_p"), gdf("o_gdn_s")
    z = lambda *s: np.zeros(s, f32)
    NE, NO = 2, 2
    outs = (np.ascontiguousarray(y_prompt, f32), np.ascontiguousarray(y_sample, f32),
            np.ascontiguousarray(p_ckv, f32), np.ascontiguousarray(p_kpe, f32), np.ascontiguousarray(p_shift, f32), np.ascontiguousarray(p_wkv, f32),
            np.ascontiguousarray(p_pool, f32), np.ascontiguousarray(p_gc, f32), np.ascontiguousarray(p_gdn, f32), np.ascontiguousarray(p_ffn, f32),
            np.ascontiguousarray(s_ckv, f32), np.ascontiguousarray(s_kpe, f32), np.ascontiguousarray(s_shift, f32), np.ascontiguousarray(s_wkv, f32),
            np.ascontiguousarray(s_pool, f32), np.ascontiguousarray(s_gc, f32), np.ascontiguousarray(s_gdn, f32), np.ascontiguousarray(s_ffn, f32))
    return outs
```

```python
from contextlib import ExitStack
import os
import numpy as np
import concourse.bass as bass
import concourse.mybir as mybir
from concourse.bass_utils import run_bass_kernel_spmd

F32 = mybir.dt.float32
BF16 = mybir.dt.bfloat16
I32 = mybir.dt.int32
ALU = mybir.AluOpType
AF = mybir.ActivationFunctionType
AX = mybir.AxisListType

ENGS = ("pe", "act", "dve", "pool", "sp")
N_DMA_SEMS = 8


class Res:
    __slots__ = ("name", "excl", "last_w", "readers")

    def __init__(self, name, excl=False):
        self.name = name
        self.excl = excl
        self.last_w = None
        self.readers = {}

    def add_reader(self, op):
        self.readers[id(op)] = op


class View:
    __slots__ = ("ap", "res")

    def __init__(self, ap, res):
        self.ap = ap
        self.res = res if isinstance(res, (list, tuple)) else [res]

    def __getitem__(self, idx):
        return View(self.ap[idx], self.res)

    def rearrange(self, *a, **k):
        return View(self.ap.rearrange(*a, **k), self.res)

    def bitcast(self, dt):
        return View(self.ap.bitcast(dt), self.res)

    @property
    def shape(self):
        return self.ap.shape

    def bc3(self, axis, n):
        sh = list(self.ap.shape)
        sh.insert(axis, n)
        return View(self.ap.unsqueeze(axis).to_broadcast(sh), self.res)


class Tile(View):
    pass


class Op:
    __slots__ = ("eng", "fn", "deps", "is_dma", "signal", "cnt", "dsem", "dval", "idx", "tag", "epoch", "odeps", "dur", "succ", "fin", "nun")

    def __init__(self, eng, fn, is_dma, tag=""):
        self.eng = eng
        self.fn = fn
        self.deps = []
        self.is_dma = is_dma
        self.signal = False
        self.cnt = 0
        self.dsem = None
        self.dval = 0
        self.idx = -1
        self.tag = tag
        self.epoch = 0
        self.odeps = []
        self.dur = 0.3
        self.succ = []
        self.fin = None
        self.nun = 0


class Prog:
    def __init__(self, nc):
        self.nc = nc
        self.es = ExitStack()
        self.ops = []
        self.final_ops = []
        self.n_res = 0
        self.bar_op = None
        self.bar_idx = 0
        self.epoch = 0

    def sbuf(self, name, shape, dt=F32):
        self.n_res += 1
        name = f"sb_{name}_{self.n_res}"
        t = self.es.enter_context(self.nc.sbuf_tensor(name, list(shape), dt))
        return Tile(t[:], Res(name))

    def psum(self, name, shape, dt=F32):
        self.n_res += 1
        name = f"ps_{name}_{self.n_res}"
        t = self.es.enter_context(self.nc.psum_tensor(name, list(shape), dt))
        return Tile(t[:], Res(name, excl=True))

    def dram(self, name, shape, dt=F32, kind="Internal"):
        t = self.nc.dram_tensor(name, list(shape), dt, kind=kind)
        return Tile(t.ap(), Res(name))

    def scope(self):
        prog = self

        class _S:
            def __enter__(s2):
                s2.saved = prog.es
                prog.es = ExitStack()
                return s2

            def __exit__(s2, *a):
                prog.es.close()
                prog.es = s2.saved
                prog.barrier()
                return False
        return _S()

    def barrier(self):
        last = {}
        dmas = []
        for op in self.ops[self.bar_idx:]:
            if op.is_dma:
                dmas.append(op)
            else:
                last[op.eng] = op
        op = Op("sp", lambda e: e.nop(), False, "barrier")
        op.deps = list(last.values()) + dmas + ([self.bar_op] if self.bar_op else [])
        self.epoch += 1
        op.epoch = self.epoch
        op.idx = len(self.ops)
        self.ops.append(op)
        self.bar_op = op
        self.bar_idx = len(self.ops)

    def _record(self, op, reads, writes):
        deps = []
        for r in reads:
            if r.excl:
                writes = list(writes) + [r]
                continue
            if r.last_w is not None:
                deps.append(r.last_w)
            r.add_reader(op)
        for w in writes:
            if w.last_w is not None:
                deps.append(w.last_w)
            deps.extend(w.readers.values())
            w.last_w = op
            w.readers = {}
        seen = set()
        if self.bar_op is not None:
            deps.append(self.bar_op)
        op.epoch = self.epoch
        for d in deps:
            if d is op or id(d) in seen:
                continue
            if d.idx < self.bar_idx and d is not self.bar_op:
                continue
            op.odeps.append(d)
            seen.add(id(d))
            if (not d.is_dma) and d.eng == "pe" and op.eng == "pe" and not op.is_dma:
                continue
            op.deps.append(d)
        op.idx = len(self.ops)
        self.ops.append(op)
        return op

    def call(self, eng, meth, *args, tag="", **kw):
        if eng == "pool" and meth == "tensor_scalar" and kw.get("scalar2", 0) is None:
            kw = dict(kw); kw["scalar2"] = 0.0; kw["op1"] = ALU.add
        reads, writes = [], []
        a2 = list(args)
        for i, a in enumerate(a2):
            if isinstance(a, View):
                (writes if (i == 0 and "out" not in kw) else reads).extend(a.res)
                a2[i] = a.ap
        k2 = dict(kw)
        for k, a in kw.items():
            if isinstance(a, View):
                (writes if k in ("out", "accum_out") else reads).extend(a.res)
                k2[k] = a.ap

        def fn(e, a2=a2, k2=k2, meth=meth):
            return getattr(e, meth)(*a2, **k2)

        op = Op(eng, fn, False, tag or meth)
        oap = k2.get("out", a2[0] if a2 else None)
        free = 1
        try:
            for d_ in oap.shape[1:]:
                free *= int(d_)
        except Exception:
            free = 64
        if eng == "pe":
            op.dur = 0.05 + free / 1500.0
        elif eng == "act":
            op.dur = 0.2 + free / 1200.0
        elif eng == "dve":
            op.dur = (0.1 + free * 0.0065) if meth == "reciprocal" else (0.08 + free / 960.0)
        else:
            op.dur = 0.25 + free / 500.0
        return self._record(op, reads, writes)

    def dma(self, eng, out, in_, final=False, **kw):
        reads = list(in_.res) if isinstance(in_, View) else []
        writes = list(out.res) if isinstance(out, View) else []
        oap = out.ap if isinstance(out, View) else out
        iap = in_.ap if isinstance(in_, View) else in_

        def fn(e, oap=oap, iap=iap, kw=kw):
            return e.dma_start(out=oap, in_=iap, **kw)

        op_ = Op(eng, fn, True, "dma")
        nbytes = 4
        try:
            for d_ in oap.shape:
                nbytes *= int(d_)
        except Exception:
            nbytes = 65536
        op_.dur = 2.0 + nbytes / 150000.0
        op = self._record(op_, reads, writes)
        if final:
            self.final_ops.append(op)
        return op

    def __getattr__(self, name):
        if name in ENGS:
            return _EngProxy(self, name)
        raise AttributeError(name)

    def schedule(self):
        import heapq
        W = int(os.environ.get("MK_WIN", "48"))
        XLAT, SLAT = 0.7, 0.05
        ops = self.ops
        per_eng = {e: [] for e in ENGS}
        epochs = {}
        for op in ops:
            epochs.setdefault(op.epoch, []).append(op)
        prev_sched = None
        for ep in sorted(epochs):
            eops = epochs[ep]
            bar = eops[0] if eops[0].tag == "barrier" else None
            if bar is not None and prev_sched is not None:
                last = {}
                dmas = []
                for e in ENGS:
                    for o in prev_sched[e]:
                        if o.is_dma:
                            dmas.append(o)
                        else:
                            last[e] = o
                bar.deps = list(last.values()) + dmas
                bar.odeps = []
            inset = set(id(o) for o in eops)
            for o in eops:
                o.succ = []
                o.fin = None
            for o in eops:
                o.odeps = [d for d in o.odeps if id(d) in inset]
                o.nun = len(o.odeps)
                for d in o.odeps:
                    d.succ.append(o)
            queues = {e: [o for o in eops if o.eng == e] for e in ENGS}
            heads = {e: 0 for e in ENGS}
            done = {e: [] for e in ENGS}
            free = {e: 0.0 for e in ENGS}
            sched = {e: [] for e in ENGS}
            scheduled = set()
            remaining = len(eops)

            def candidate(e):
                q = queues[e]
                i = heads[e]
                best = None
                cnt_ = 0
                n_ = len(q)
                while i < n_ and cnt_ < W:
                    o = q[i]
                    i += 1
                    if o.fin is not None:
                        continue
                    cnt_ += 1
                    if o.nun > 0:
                        continue
                    r = 0.0
                    for d in o.odeps:
                        t = d.fin + (SLAT if (d.eng == e and not d.is_dma) else XLAT)
                        if t > r:
                            r = t
                    st = r if r > free[e] else free[e]
                    if best is None or st < best[0] - 1e-9:
                        best = (st, o)
                        if st <= free[e] + 1e-9:
                            break
                return best
            cand = {e: candidate(e) for e in ENGS}
            while remaining > 0:
                be = None
                for e in ENGS:
                    c = cand[e]
                    if c is not None and (be is None or c[0] < cand[be][0]):
                        be = e
                assert be is not None, "scheduler deadlock"
                st, o = cand[be]
                issue = o.dur
                if o.is_dma:
                    issue = 1.0 if o.eng == "pool" else 0.3
                o.fin = st + o.dur
                free[be] = st + issue
                sched[be].append(o)
                remaining -= 1
                q = queues[be]
                while heads[be] < len(q) and q[heads[be]].fin is not None:
                    heads[be] += 1
                dirty = {be}
                for s_ in o.succ:
                    s_.nun -= 1
                    if s_.nun == 0:
                        dirty.add(s_.eng)
                for e in dirty:
                    cand[e] = candidate(e)
            for e in ENGS:
                per_eng[e].extend(sched[e])
            prev_sched = sched
            self.sim_time = getattr(self, "sim_time", 0.0) + max(free.values())
        return per_eng

    def finish(self):
        nc = self.nc
        ops = self.ops
        do_sched = int(os.environ.get("MK_SCHED", "1"))
        per_eng_s = self.schedule() if do_sched else None
        for op in ops:
            for d in op.deps:
                d.signal = True
        for op in self.final_ops:
            op.signal = True
        cnt = {}
        dcount = {e: 0 for e in ENGS}
        per_eng = {e: [] for e in ENGS}
        if per_eng_s is None:
            for op in ops:
                per_eng[op.eng].append(op)
        else:
            per_eng = per_eng_s
        for op in [o for e in ENGS for o in per_eng[e]]:
            if op.is_dma:
                k = dcount[op.eng]
                dcount[op.eng] += 1
                op.dsem = (op.eng, k % N_DMA_SEMS)
                op.dval = 16 * (k // N_DMA_SEMS + 1)
            elif op.signal:
                ke = (op.epoch, op.eng)
                cnt[ke] = cnt.get(ke, 0) + 1
                op.cnt = cnt[ke]
        es = self.es
        assert max(cnt.values()) < 60000, max(cnt.values())
        esem = {ke: es.enter_context(nc.semaphore(f"s_{ke[1]}_{ke[0]}")) for ke in cnt}
        dsem = {}
        for e in ENGS:
            if dcount[e]:
                for i in range(min(N_DMA_SEMS, dcount[e])):
                    dsem[(e, i)] = es.enter_context(nc.semaphore(f"d_{e}_{i}"))
        block = es.enter_context(nc.Block())
        engmap = {"pe": "tensor", "act": "scalar", "dve": "vector", "pool": "gpsimd", "sp": "sync"}
        final_ops = self.final_ops

        def emit(e_name):
            def body(eng):
                known = {}
                prev_dma = {}
                my_ops = per_eng[e_name]
                for op in my_ops:
                    waits = {}
                    for d in op.deps:
                        if d.is_dma:
                            key, val = ("d",) + d.dsem, d.dval
                        else:
                            key, val = ("e", d.epoch, d.eng), d.cnt
                        if waits.get(key, 0) < val:
                            waits[key] = val
                    if op.is_dma:
                        if op.dval > 16:
                            key = ("d",) + op.dsem
                            if waits.get(key, 0) < op.dval - 16:
                                waits[key] = op.dval - 16
                    for key, val in waits.items():
                        if known.get(key, 0) >= val:
                            continue
                        known[key] = val
                        sem = dsem[key[1:]] if key[0] == "d" else esem[key[1:]]
                        eng.wait_ge(sem, val)
                    ins = op.fn(eng)
                    if op.is_dma:
                        ins.then_inc(dsem[op.dsem], 16)
                    elif op.signal:
                        ins.then_inc(esem[(op.epoch, e_name)], 1)
                if e_name == "sp":
                    waits = {}
                    for op in final_ops:
                        key = ("d",) + op.dsem
                        waits[key] = max(waits.get(key, 0), op.dval)
                    for key, val in waits.items():
                        if known.get(key, 0) < val:
                            eng.wait_ge(dsem[key[1:]], val)
            return body

        for e in ENGS:
            if per_eng[e] or e == "sp":
                getattr(block, engmap[e])(emit(e))
        self.es.close()
        return {e: (len(per_eng[e]), max([v for k, v in cnt.items() if k[1] == e] + [0]), dcount[e]) for e in ENGS}, len(esem)


class _EngProxy:
    def __init__(self, prog, eng):
        self.prog = prog
        self.eng = eng

    def __getattr__(self, meth):
        if meth == "dma_start":
            def f(out, in_, **kw):
                return self.prog.dma(self.eng, out, in_, **kw)
            return f

        def f(*a, **kw):
            return self.prog.call(self.eng, meth, *a, **kw)
        return f


NCORES = 8
D = 1024; KC = 8; DFF = 2816; FC = 22; DEPTH = 4
SEQ = 2048; DSEQ = 32; PAST = 1024; NB = 4
TT = 256
EPS = 1e-6
NLAYERS = int(os.environ.get("MK_LAYERS", "4"))
DO_MIX = int(os.environ.get("MK_MIX", "1"))
MK_NT = int(os.environ.get("MK_NT", str(SEQ // TT)))
MK_NSEQ = int(os.environ.get("MK_NSEQ", "4"))
MK_CORES = int(os.environ.get("MK_CORES", "8"))
ODD_STAGE = int(os.environ.get("MK_ODD", "9"))
SUB = int(os.environ.get("MK_SUB", "9"))


class VecPack:
    def __init__(self):
        self.cols = []
        self.off = {}
        self.n = 0

    def put(self, name, arr):
        arr = np.ascontiguousarray(arr, dtype=np.float32)
        assert arr.shape[0] == 128
        self.off[name] = (self.n, arr.shape[1])
        self.cols.append(arr)
        self.n += arr.shape[1]

    def put_feat(self, name, v):
        v = np.asarray(v, dtype=np.float32)
        self.put(name, v.reshape(-1, 128).T)

    def array(self):
        return np.concatenate(self.cols, axis=1)


def vec_layout(inputs=None):
    z = lambda *s: np.zeros(s, np.float32)
    g = (lambda k: np.asarray(inputs[k], np.float32)) if inputs is not None else None
    vp = VecPack()
    for l in range(DEPTH):
        vp.put_feat(f"gmix{l}", g("norm_mix_g")[l] if g else z(D))
        vp.put_feat(f"gffn{l}", g("norm_ffn_g")[l] if g else z(D))
        vp.put_feat(f"adab{l}", g("ada_b")[l] if g else z(6 * D))
        for i in range(3):
            vp.put_feat(f"fcw{l}_{i}", g("ffn_conv_w")[l, i] if g else z(DFF))
    for i in range(2):
        vp.put_feat(f"gqlat{i}", g("mla_g_qlat")[i] if g else z(256))
        vp.put_feat(f"gkvlat{i}", g("mla_g_kvlat")[i] if g else z(128))
        gq = z(128, 1); gk = z(128, 1); gkr = z(128, 1)
        if g:
            gq[0:64, 0] = g("mla_g_qn")[i]; gq[64:96, 0] = g("mla_g_qr")[i]
            gk[0:64, 0] = g("mla_g_kn")[i]; gk[64:128, 0] = g("mla_g_kn")[i]
            gkr[64:96, 0] = g("mla_g_kr")[i]
        vp.put(f"gq{i}", gq); vp.put(f"gk{i}", gk); vp.put(f"gkr{i}", gkr)
        for nm, key, nn in (("mu", "rwkv_mu", 1792), ("w0", "rwkv_w0", 512), ("a0", "rwkv_a0", 512), ("kk", "rwkv_k_k", 512),
                            ("ka", "rwkv_k_a", 512), ("lng", "rwkv_lnx_g", 512), ("lnb", "rwkv_lnx_b", 512)):
            vp.put_feat(f"{nm}{i}", g(key)[i] if g else z(nn))
        vp.put_feat(f"rk{i}", g("rwkv_r_k")[i].reshape(-1) if g else z(512))
    for i in range(2):
        vp.put_feat(f"pscale{i}", g("pool_scale")[i] if g else z(512))
        for t in range(4):
            vp.put_feat(f"gcw{i}_{t}", g("gdn_conv_w")[i, t] if g else z(1536))
        vp.put_feat(f"gog{i}", g("gdn_o_g")[i] if g else z(128))
        dtb = z(128, 1); alog = z(128, 1)
        if g:
            dtb[4:8, 0] = g("gdn_dt_bias")[i]; alog[4:8, 0] = g("gdn_a_log")[i]
        vp.put(f"dtb{i}", dtb); vp.put(f"alog{i}", alog)
    return vp


NPOS = SEQ + NB * DSEQ
CST = {}


def const_array():
    CST.clear()
    cols = []
    off = 0

    def put(name, a):
        nonlocal off
        a = np.ascontiguousarray(a, np.float32)
        CST[name] = (off, a.shape[1]); cols.append(a); off += a.shape[1]
    pos = np.concatenate([np.arange(SEQ)] + [PAST + np.arange(DSEQ)] * NB).astype(np.float32)
    inv = (1.0 / (10000.0 ** (np.arange(0, 32, 2, dtype=np.float32) / 32))).astype(np.float32)
    ang = pos[None, :] * inv[:, None]
    C = np.zeros((128, NPOS), np.float32); S = np.zeros((128, NPOS), np.float32)
    C[0:64] = 1.0
    C[64:80] = np.cos(ang); C[80:96] = np.cos(ang)
    S[64:80] = np.sin(ang); S[80:96] = np.sin(ang)
    put("C", C); put("S", S)
    return np.concatenate(cols, axis=1)


CST2 = {}


def const_array2():
    CST2.clear()
    cols = []
    off = 0

    def put(name, a):
        nonlocal off
        a = np.ascontiguousarray(a, np.float32)
        assert a.shape[0] == 128
        CST2[name] = (off, a.shape[1]); cols.append(a); off += a.shape[1]
    Pm = np.zeros((128, 128), np.float32)
    for i in range(16):
        Pm[80 + i, 64 + i] = -1.0
        Pm[64 + i, 80 + i] = 1.0
    put("Pm", Pm)
    BO96 = np.zeros((128, 128), np.float32); BO96[0:64, 0:64] = 1 / 64; BO96[64:96, 64:96] = 1 / 32
    put("BO96", BO96)
    BO64 = np.zeros((128, 128), np.float32); BO64[0:64, 0:64] = 1 / 64; BO64[64:128, 64:128] = 1 / 64
    put("BO64", BO64)
    put("I128", np.eye(128, dtype=np.float32))
    for Cc in (64, 32):
        n2 = 2 * Cc
        blk = (np.arange(n2)[:, None] // Cc) == (np.arange(n2)[None, :] // Cc)
        jj = np.arange(n2)[:, None] % Cc; ii = np.arange(n2)[None, :] % Cc
        MS = (blk & (ii > jj)).astype(np.float32); MI = (blk & (ii >= jj)).astype(np.float32)
        m = np.zeros((128, 2 * n2), np.float32); m[0:n2, 0:n2] = MS; m[0:n2, n2:2 * n2] = MI
        put(f"MSI{Cc}", m)
        mneg = np.zeros((128, 2 * n2), np.float32); mneg[0:n2, 0:n2] = -MS; mneg[0:n2, n2:2 * n2] = MI
        put(f"MSIN{Cc}", mneg)
        tri = np.zeros((128, n2), np.float32); tri[0:n2] = (blk & (jj <= ii)).astype(np.float32)
        put(f"TRI{Cc}", tri)
        ob = np.zeros((128, n2), np.float32); ob[0:n2] = blk.astype(np.float32)
        put(f"ONB{Cc}", ob)
        put(f"NONB{Cc}", -ob)
        lv = 0
        while (1 << lv) < Cc:
            half = 1 << lv; full = 2 * half
            I_ = np.arange(n2)[:, None]; J_ = np.arange(n2)[None, :]
            mk_ = ((I_ // full) == (J_ // full)) & ((I_ % full) >= half) & ((J_ % full) < half)
            mm_ = np.zeros((128, n2), np.float32); mm_[0:n2] = mk_.astype(np.float32)
            put(f"MK{Cc}_{lv}", mm_)
            lv += 1
        stri = np.zeros((128, n2), np.float32); stri[0:n2] = (blk & (jj > ii)).astype(np.float32)
        put(f"STRI{Cc}", stri)
        for hh in range(2):
            sl = np.zeros((128, 128), np.float32); sl[hh * Cc:(hh + 1) * Cc, :] = 1.0
            put(f"BLK{Cc}_{hh}", sl)
        r = np.ones((128, NB * Cc), np.float32); r[:, ::Cc] = 0.0
        put(f"R{Cc}", r)
    ict = np.zeros((128, 4 * 15), np.float32)
    for gi, w in enumerate((2, 4, 8, 16)):
        ict[:, gi * 15:(gi + 1) * 15] = 1.0 / np.minimum(w, np.arange(15) + 1)
    put("ICT", ict)
    oh = np.zeros((128, 8), np.float32); oh[0:8, 0:8] = np.eye(8)
    put("OH8", oh)
    for h in range(4):
        sl = np.zeros((128, 128), np.float32); sl[h, :] = 1.0
        put(f"SEL8_{h}", sl)
    return np.concatenate(cols, axis=1)


class TokTile:
    def __init__(self, nseg, L, rows, seqs, t0, first, last, sample):
        self.nseg, self.L, self.rows, self.seqs, self.t0 = nseg, L, rows, seqs, t0
        self.TT = nseg * L
        self.first, self.last, self.sample = first, last, sample
        self.res = None


def make_tiles():
    tiles = []
    for s in range(MK_NSEQ):
        nt = MK_NT
        for j in range(nt):
            tiles.append(TokTile(1, TT, [s], [s], j * TT, j == 0, j == nt - 1, False))
    tiles.append(TokTile(NB, DSEQ, [4, 5, 6, 7], [0, 1, 2, 3], 0, True, True, True))
    return tiles


def build_program():
    nc = bass.Bass("TRN2", target_bir_lowering=False)
    P = Prog(nc)
    vl = vec_layout(None)
    NV = vl.n

    def din(name, shape, dt=F32):
        return nc.dram_tensor(name, list(shape), dt, kind="ExternalInput").ap()

    def dout(name, shape, dt=F32):
        return nc.dram_tensor(name, list(shape), dt, kind="ExternalOutput").ap()

    xp = din("xp", [NB, KC, 128, SEQ]); xs = din("xs", [NB, KC, 128, DSEQ])
    yp = dout("yp", [NB, KC, 128, SEQ]); ys = dout("ys", [NB, KC, 128, DSEQ])
    cT_d = din("cT", [128, KC, 8])
    vecs_d = din("vecs", [128, NV])
    ada_w = din("ada_w", [DEPTH, D, 6 * D])
    wg_d = din("ffn_w_gate", [DEPTH, D, DFF]); wu_d = din("ffn_w_up", [DEPTH, D, DFF])
    wd_d = din("ffn_w_down", [DEPTH, DFF, D])
    fh_d = din("ffn_hist", [DEPTH, 128, FC, NB, 2])
    o_ffn_p = dout("o_ffn_p", [DEPTH, 128, FC, NB, 2]); o_ffn_s = dout("o_ffn_s", [DEPTH, 128, FC, NB, 2])

    cst_np = const_array(); cst2_np = const_array2()
    NCST = cst_np.shape[1]
    cst_d = din("consts", [128, NCST])
    win_e = din("even_w_in", [2, D, 2208]); wuq_d = din("mla_w_uq", [2, 256, 768]); wukv_d = din("mla_w_ukv", [2, 128, 1024])
    wout_e = din("even_w_out", [2, D, D])
    ckv_past = din("ckv_past", [2, NB, 128, PAST]); kpe_past = din("kpe_past", [2, NB, 32, PAST])
    o_ckv_p = dout("o_ckv_p", [2, NB, 128, SEQ]); o_kpe_p = dout("o_kpe_p", [2, NB, 32, SEQ])
    o_ckv_s = dout("o_ckv_s", [2, NB, 128, DSEQ]); o_kpe_s = dout("o_kpe_s", [2, NB, 32, DSEQ])
    cst2_np = const_array2()
    NCST2 = cst2_np.shape[1]
    cst2_d = din("consts2", [128, NCST2])
    w2_d = din("rwkv_w2", [2, 64, 512]); a2_d = din("rwkv_a2", [2, 64, 512]); g2_d = din("rwkv_g2", [2, 128, 512])
    shift_in = din("shift_in", [2, 128, 14, NB]); wkv_in = din("wkv_in", [2, NB, 4, 128, 64])
    o_shift_p = dout("o_shift_p", [2, 128, 14, NB]); o_shift_s = dout("o_shift_s", [2, 128, 14, NB])
    o_wkv_p = dout("o_wkv_p", [2, NB, 4, 128, 64]); o_wkv_s = dout("o_wkv_s", [2, NB, 4, 128, 64])
    win_o = din("odd_w_in", [2, D, 2568]); poolw_d = din("pool_w", [2, 4, 128, 128]); wout_o = din("odd_w_out", [2, D, D])
    pool_in = din("pool_in", [2, 128, 4, NB, 15]); gconv_in = din("gconv_in", [2, 128, 12, NB, 3]); gdn_in = din("gdn_in", [2, NB, 4, 128, 128])
    o_pool_p = dout("o_pool_p", [2, 128, 4, NB, 15]); o_pool_s = dout("o_pool_s", [2, 128, 4, NB, 15])
    o_gconv_p = dout("o_gconv_p", [2, 128, 12, NB, 3]); o_gconv_s = dout("o_gconv_s", [2, 128, 12, NB, 3])
    o_gdn_p = dout("o_gdn_p", [2, NB, 4, 128, 128]); o_gdn_s = dout("o_gdn_s", [2, NB, 4, 128, 128])
    tiles = make_tiles()
    TTF = 512
    omla_d = nc.dram_tensor("omla_scr", [len(tiles), 128, 4, TT], BF16, kind="Internal").ap()
    omla_res = [Res(f"omla{t}") for t in range(len(tiles))]
    xres = [Res(f"x{t}") for t in range(len(tiles))]
    for t_, tk_ in enumerate(tiles):
        tk_.res = [xres[t_]]
    ftiles = []
    if MK_NT % 2 == 0:
        for t_ in range(0, len(tiles) - 1, 2):
            a_, b_ = tiles[t_], tiles[t_ + 1]
            ft = TokTile(1, 2 * TT, a_.rows, a_.seqs, a_.t0, a_.first, b_.last, False)
            ft.res = [xres[t_], xres[t_ + 1]]
            ftiles.append(ft)
        ftiles.append(tiles[-1])
    else:
        ftiles = tiles

    def x_dram(ti, tk, seg, src_first):
        s = tk.seqs[seg]
        if tk.sample:
            base = xs if src_first else ys
            ap = base[s].rearrange("k p t -> p k t")
        else:
            base = xp if src_first else yp
            ap = base[s][:, :, tk.t0:tk.t0 + tk.L].rearrange("k p t -> p k t")
        return View(ap, tk.res)

    def y_dram(ti, tk, seg):
        s = tk.seqs[seg]
        if tk.sample:
            ap = ys[s].rearrange("k p t -> p k t")
        else:
            ap = yp[s][:, :, tk.t0:tk.t0 + tk.L].rearrange("k p t -> p k t")
        return View(ap, tk.res)

    vecs = P.sbuf("vecs", [128, NV])
    P.sp.dma_start(vecs, vecs_d)

    def V(name, c0=0, n=None):
        o, w = vl.off[name]
        n = w - c0 if n is None else n
        return vecs[:, o + c0:o + c0 + n]

    CH = {}

    def load_consts(names):
        names = ["Pm", "BO96", "BO64", "I128"] + [n_ for n_ in names if n_ not in ("Pm", "BO96", "BO64", "I128")]
        offs = {}
        tot = 0
        for n_ in names:
            offs[n_] = (tot, CST2[n_][1]); tot += CST2[n_][1]
        t_ = P.sbuf("cst2", [128, tot])
        for n_ in names:
            o_, w_ = CST2[n_]
            P.sp.dma_start(t_[:, offs[n_][0]:offs[n_][0] + w_], cst2_d[:, o_:o_ + w_])
        CH["cst2"] = t_; CH["offs"] = offs
        cb_ = P.sbuf("cb", [128, 4 * 128], BF16)
        for j, nm in enumerate(("Pm", "BO96", "BO64", "I128")):
            P.dve.tensor_copy(out=cb_[:, j * 128:(j + 1) * 128], in_=C2(nm))
        return cb_[:, 0:128], cb_[:, 128:256], cb_[:, 256:384], cb_[:, 384:512]

    def C2(name, c0=0, n=None):
        o, w = CH["offs"][name]
        n = w - c0 if n is None else n
        return CH["cst2"][:, o + c0:o + c0 + n]
    ones_bf = P.sbuf("ones_bf", [128, 128], BF16)
    P.dve.memset(ones_bf, 1.0 / D)
    eps_t = P.sbuf("eps_t", [128, 1]); P.dve.memset(eps_t, EPS)
    psb = [P.psum(f"psb{i}", [128, 512]) for i in range(8)]
    for pt_ in psb:
        P.dve.memset(pt_, 0.0)

    class Rot:
        def __init__(self, items):
            self.items, self.i = items, 0

        def next(self):
            it = self.items[self.i % len(self.items)]
            self.i += 1
            return it

    cT = P.sbuf("cT", [128, KC, 8]); P.sp.dma_start(cT, cT_d)
    cs = P.sbuf("cs", [128, KC, 8], BF16)
    P.act.activation(out=cs, in_=cT, func=AF.Silu)
    mod = [P.sbuf(f"mod{l}", [128, 48, 8]) for l in range(DEPTH)]
    modA_m = [P.sbuf(f"modAm{l}", [128, KC, 8]) for l in range(DEPTH)]
    modA_f = [P.sbuf(f"modAf{l}", [128, KC, 8]) for l in range(DEPTH)]
    with P.scope():
        wa_bufs = Rot([P.sbuf(f"wa{i}", [128, KC, 1024], BF16) for i in range(2)])
        for l in range(NLAYERS):
            pm = psb[l % 2]
            for fg in range(6):
                wa = wa_bufs.next()
                P.pool.dma_start(wa, ada_w[l][:, fg * 1024:(fg + 1) * 1024].rearrange("(k p) f -> p k f", p=128))
                for cc in range(8):
                    c = fg * 8 + cc
                    for k in range(KC):
                        P.pe.matmul(out=pm[:, c * 8:(c + 1) * 8], lhsT=wa[:, k, cc * 128:(cc + 1) * 128],
                                    rhs=cs[:, k, :], start=(k == 0), stop=(k == KC - 1))
            P.dve.tensor_tensor(out=mod[l], in0=pm[:, 0:384].rearrange("p (c b) -> p c b", b=8),
                                in1=V(f"adab{l}").bc3(2, 8), op=ALU.add)
            for (A, gname, c0) in ((modA_m[l], f"gmix{l}", 8), (modA_f[l], f"gffn{l}", 32)):
                P.dve.tensor_scalar(out=A, in0=mod[l][:, c0:c0 + 8, :], scalar1=1.0, scalar2=None, op0=ALU.add)
                P.dve.tensor_tensor(out=A, in0=A, in1=V(gname).bc3(2, 8), op=ALU.mult)

    x_raw = P.sbuf("x_t", [128, KC, TTF])
    h_raw = P.sbuf("h_t", [128, KC, TTF], BF16)
    r_raw = P.sbuf("rstd_t", [128, TTF])
    XB = {}
    for nm_, raw_ in (("x", x_raw), ("h", h_raw), ("r", r_raw)):
        rr_ = [Res(nm_ + "_h0"), Res(nm_ + "_h1")]
        if nm_ == "r":
            XB[nm_] = {"full": View(raw_.ap, rr_), 0: View(raw_.ap[:, 0:TT], rr_[0]), 1: View(raw_.ap[:, TT:2 * TT], rr_[1])}
        else:
            XB[nm_] = {"full": View(raw_.ap, rr_), 0: View(raw_.ap[:, :, 0:TT], rr_[0]), 1: View(raw_.ap[:, :, TT:2 * TT], rr_[1])}
    XH = {}

    def use_buf(j):
        for nm_ in ("x", "h", "r"):
            XH[nm_] = XB[nm_][j]
    use_buf("full")
    sqk = Rot([P.sbuf(f"sqk{j}", [128, TTF], BF16) for j in range(2)])
    tmk = Rot([P.sbuf(f"tmk{j}", [128, TTF]) for j in range(2)])
    ps_stat = psb[7]

    def modulate(tk, A, Bsh):
        n = tk.TT
        for k in range(KC):
            sq = sqk.next()
            P.act.activation(out=sq[:, 0:n], in_=XH["x"][:, k, 0:n], func=AF.Square)
            P.pe.matmul(out=ps_stat[:, 0:n], lhsT=ones_bf, rhs=sq[:, 0:n], start=(k == 0), stop=(k == KC - 1))
        P.act.activation(out=XH["r"][:, 0:n], in_=ps_stat[:, 0:n], func=AF.Sqrt, bias=eps_t, scale=1.0)
        P.dve.reciprocal(out=XH["r"][:, 0:n], in_=XH["r"][:, 0:n])
        for k in range(KC):
            for si, row in enumerate(tk.rows):
                sl = slice(si * tk.L, (si + 1) * tk.L)
                t_ = tmk.next()
                P.dve.scalar_tensor_tensor(out=t_[:, sl], in0=XH["x"][:, k, sl], scalar=A[:, k, row:row + 1], in1=XH["r"][:, sl], op0=ALU.mult, op1=ALU.mult)
                P.act.activation(out=XH["h"][:, k, sl], in_=t_[:, sl], func=AF.Identity, bias=Bsh[:, k, row:row + 1], scale=1.0)

    def load_x(ti, tk, first_layer):
        for si in range(tk.nseg):
            P.sp.dma_start(XH["x"][:, :, si * tk.L:(si + 1) * tk.L], x_dram(ti, tk, si, first_layer))

    def store_x(ti, tk, final):
        for si in range(tk.nseg):
            P.sp.dma_start(y_dram(ti, tk, si), XH["x"][:, :, si * tk.L:(si + 1) * tk.L], final=final)

    for l in range(NLAYERS):

        if DO_MIX and l % 2 == 0:
            i = l // 2
            P.barrier()
            with P.scope():
                Pm_b, BO96_b, BO64_b, Ib = load_consts([])
                cst = P.sbuf("cst", [128, NCST]); P.sp.dma_start(cst, cst_d)

                def CS(name):
                    o, w = CST[name]
                    return cst[:, o:o + w]
                ones1 = P.sbuf("ones1", [128, 64], BF16); P.dve.memset(ones1, 1.0)
                ones128 = P.sbuf("ones128", [128, 128], BF16); P.dve.memset(ones128, 1.0 / 128)
                win = P.sbuf("win", [128, KC, 2208], BF16)
                for k in range(KC):
                    P.pool.dma_start(win[:, k, :], win_e[i][k * 128:(k + 1) * 128, :])
                wuq = P.sbuf("wuq", [128, 2, 768], BF16)
                for k in range(2):
                    P.pool.dma_start(wuq[:, k, :], wuq_d[i][k * 128:(k + 1) * 128, :])
                wukv = P.sbuf("wukv", [128, 1024], BF16); P.pool.dma_start(wukv, wukv_d[i])
                wk_c = P.sbuf("wk_c", [128, 8, 64], BF16); wv_c = P.sbuf("wv_c", [128, 8, 64], BF16)
                wukv3 = wukv.rearrange("p (h c) -> p h c", h=8)
                P.dve.tensor_copy(out=wk_c, in_=wukv3[:, :, 0:64]); P.dve.tensor_copy(out=wv_c, in_=wukv3[:, :, 64:128])
                PGq = P.sbuf("PGq", [128, 128], BF16)
                P.dve.tensor_scalar(out=PGq[0:96, 0:96], in0=C2("Pm")[0:96, 0:96], scalar1=V(f"gq{i}")[0:96, :], scalar2=None, op0=ALU.mult)
                PGk = P.sbuf("PGk", [128, 128], BF16)
                P.dve.memset(PGk, 0.0)
                Ctab = CS("C")
                P.dve.tensor_scalar(out=PGk[64:96, 0:96], in0=C2("Pm")[64:96, 0:96], scalar1=V(f"gkr{i}")[64:96, :], scalar2=None, op0=ALU.mult)
                Stab = CS("S")
                NKT = SEQ // 128
                KT = P.sbuf("KT", [96, 8, SEQ], BF16)
                Vt = P.sbuf("Vt", [128, NKT, 8, 64], BF16)
                mixT = P.sbuf("mixT", [128, 4, TT], BF16)
                ckv_f = P.sbuf("ckv_f", [128, TT]); ckv_b = P.sbuf("ckv_b", [128, TT], BF16)
                kpe_f = P.sbuf("kpe_f", [128, TT]); kpe_b = P.sbuf("kpe_b", [128, TT], BF16)
                qlat_b = P.sbuf("qlat_b", [128, 2, TT], BF16)
                sqa = Rot([P.sbuf(f"sqa{j}", [128, TT], BF16) for j in range(3)])
                tfa = Rot([P.sbuf(f"tfa{j}", [128, TT]) for j in range(6)])
                tfb = Rot([P.sbuf(f"tfb{j}", [128, TT]) for j in range(8)])
                qgb = Rot([P.sbuf(f"qgb{j}", [128, TT], BF16) for j in range(3)])
                Qf = P.sbuf("Qf", [96, 8, TT], BF16)
                PT = Rot([P.sbuf(f"PT{j}", [128, TT], BF16) for j in range(6)])
                rsum = Rot([P.sbuf(f"rsum{j}", [128, TT]) for j in range(2)])
                pastb = P.sbuf("pastb", [128, PAST], BF16); kpast = P.sbuf("kpast", [96, PAST], BF16)
                pa = Rot(psb[0:3]); pb = Rot(psb[3:5]); po = Rot(psb[5:7])

                def rstd_from(ps_view, out_view, r0=0, r1=128):
                    P.act.activation(out=out_view, in_=ps_view, func=AF.Sqrt, bias=eps_t[r0:r1, :], scale=1.0)
                    P.dve.reciprocal(out=out_view, in_=out_view)

                def knope_v(src_b, n, kbase, r0=0):
                    for hp in range(4):
                        kp = pa.next()
                        P.pe.matmul(out=kp[:, 0:n], lhsT=wk_c[:, 2 * hp:2 * hp + 2, :].rearrange("p h c -> p (h c)"), rhs=src_b[:, 0:n], start=True, stop=True)
                        sq = sqa.next()
                        P.act.activation(out=sq[:, 0:n], in_=kp[:, 0:n], func=AF.Square)
                        mp = pb.next()
                        P.pe.matmul(out=mp[:, 0:n], lhsT=BO64_b, rhs=sq[:, 0:n], start=True, stop=True)
                        rs = tfa.next()
                        rstd_from(mp[:, 0:n], rs[:, 0:n])
                        t = tfb.next()
                        P.dve.scalar_tensor_tensor(out=t[:, 0:n], in0=kp[:, 0:n], scalar=V(f"gk{i}"), in1=rs[:, 0:n], op0=ALU.mult, op1=ALU.mult)
                        P.pool.tensor_copy(out=KT[0:64, 2 * hp, kbase:kbase + n], in_=t[0:64, 0:n])
                        P.pool.tensor_copy(out=KT[0:64, 2 * hp + 1, kbase:kbase + n], in_=t[64:128, 0:n])
                    for j0 in range(0, n, 128):
                        m = min(128, n - j0)
                        vp_ = pa.next()
                        P.pe.matmul(out=vp_[0:m, 0:512], lhsT=src_b[:, j0:j0 + m], rhs=wv_c.rearrange("p h c -> p (h c)"), start=True, stop=True)
                        P.act.activation(out=Vt[0:m, (kbase + j0) // 128, :, :].rearrange("p h c -> p (h c)"), in_=vp_[0:m, 0:512], func=AF.Copy)

                def rope_norm(src_ps, r0, r1, n, CG, PG, pos0, out_view):
                    sq = sqa.next()
                    P.act.activation(out=sq[0:96, 0:n], in_=src_ps[0:96, 0:n], func=AF.Square)
                    mp = pb.next()
                    P.pe.matmul(out=mp[0:96, 0:n], lhsT=BO96_b[0:96, 0:96], rhs=sq[0:96, 0:n], start=True, stop=True)
                    rs = tfa.next()
                    rstd_from(mp[r0:r1, 0:n], rs[r0:r1, 0:n], r0, r1)
                    qg = qgb.next()
                    if r0 > 0:
                        P.pool.memset(qg[0:r0, 0:n], 0.0)
                    P.dve.tensor_copy(out=qg[r0:r1, 0:n], in_=src_ps[r0:r1, 0:n])
                    rp = pb.next()
                    P.pe.matmul(out=rp[0:96, 0:n], lhsT=PG[0:96, 0:96], rhs=qg[0:96, 0:n], start=True, stop=True)
                    t1 = tfb.next()
                    P.dve.scalar_tensor_tensor(out=t1[r0:r1, 0:n], in0=src_ps[r0:r1, 0:n], scalar=CG[r0:r1, :], in1=Ctab[r0:r1, pos0:pos0 + n], op0=ALU.mult, op1=ALU.mult)
                    t2 = tfb.next()
                    P.dve.tensor_tensor(out=t2[r0:r1, 0:n], in0=rp[r0:r1, 0:n], in1=Stab[r0:r1, pos0:pos0 + n], op=ALU.mult)
                    P.pool.tensor_tensor(out=t1[r0:r1, 0:n], in0=t1[r0:r1, 0:n], in1=t2[r0:r1, 0:n], op=ALU.add)
                    P.pool.tensor_tensor(out=out_view, in0=t1[r0:r1, 0:n], in1=rs[r0:r1, 0:n], op=ALU.mult)

                for ti, tk in enumerate(tiles):
                    use_buf(ti % 2)
                    n, L, ns = tk.TT, tk.L, tk.nseg
                    load_x(ti, tk, l == 0)
                    modulate(tk, modA_m[l], mod[l][:, 0:8, :])
                    pos0 = (SEQ if tk.sample else tk.t0)

                    def proj(c0, m):
                        ps = pa.next()
                        for k in range(KC):
                            P.pe.matmul(out=ps[0:m, 0:n], lhsT=win[:, k, c0:c0 + m], rhs=XH["h"][:, k, 0:n], start=(k == 0), stop=(k == KC - 1))
                        return ps
                    qps = [proj(0, 128), proj(128, 128)]
                    mp = pb.next()
                    for c in range(2):
                        sq = sqa.next()
                        P.act.activation(out=sq[:, 0:n], in_=qps[c][:, 0:n], func=AF.Square)
                        P.pe.matmul(out=mp[:, 0:n], lhsT=ones128, rhs=sq[:, 0:n], start=(c == 0), stop=(c == 1))
                    rs = tfa.next()
                    P.act.activation(out=rs[:, 0:n], in_=mp[:, 0:n], func=AF.Sqrt, bias=eps_t, scale=0.5)
                    P.dve.reciprocal(out=rs[:, 0:n], in_=rs[:, 0:n])
                    for c in range(2):
                        P.dve.scalar_tensor_tensor(out=qlat_b[:, c, 0:n], in0=qps[c][:, 0:n], scalar=V(f"gqlat{i}", c, 1), in1=rs[:, 0:n], op0=ALU.mult, op1=ALU.mult)
                    kvp = proj(256, 128)
                    sq = sqa.next()
                    P.act.activation(out=sq[:, 0:n], in_=kvp[:, 0:n], func=AF.Square)
                    mp = pb.next()
                    P.pe.matmul(out=mp[:, 0:n], lhsT=ones128, rhs=sq[:, 0:n], start=True, stop=True)
                    rs = tfa.next()
                    rstd_from(mp[:, 0:n], rs[:, 0:n])
                    P.dve.scalar_tensor_tensor(out=ckv_f[:, 0:n], in0=kvp[:, 0:n], scalar=V(f"gkvlat{i}"), in1=rs[:, 0:n], op0=ALU.mult, op1=ALU.mult)
                    P.act.activation(out=ckv_b[:, 0:n], in_=ckv_f[:, 0:n], func=AF.Copy)
                    krp = proj(320, 96)
                    rope_norm(krp, 64, 96, n, V(f"gkr{i}"), PGk, pos0, kpe_f[64:96, 0:n])
                    P.act.activation(out=kpe_b[64:96, 0:n], in_=kpe_f[64:96, 0:n], func=AF.Copy)
                    for si in range(ns):
                        sidx = tk.seqs[si]
                        sl = slice(si * L, (si + 1) * L)
                        if tk.sample:
                            P.sp.dma_start(View(o_ckv_s[i, sidx], Res("o")), ckv_f[:, sl], final=True)
                            P.sp.dma_start(View(o_kpe_s[i, sidx], Res("o")), kpe_f[64:96, sl], final=True)
                        else:
                            P.sp.dma_start(View(o_ckv_p[i, sidx][:, tk.t0:tk.t0 + L], Res("o")), ckv_f[:, sl], final=True)
                            P.sp.dma_start(View(o_kpe_p[i, sidx][:, tk.t0:tk.t0 + L], Res("o")), kpe_f[64:96, sl], final=True)
                    for h in range(8):
                        qp = pa.next()
                        for c in range(2):
                            P.pe.matmul(out=qp[0:96, 0:n], lhsT=wuq[:, c, h * 96:(h + 1) * 96], rhs=qlat_b[:, c, 0:n], start=(c == 0), stop=(c == 1))
                        rope_norm(qp, 0, 96, n, V(f"gq{i}"), PGq, pos0, Qf[0:96, h, 0:n])
                    for si in range(ns):
                        sidx = tk.seqs[si]
                        qsl = slice(si * L, (si + 1) * L)
                        if tk.sample:
                            P.pool.dma_start(pastb, ckv_past[i, sidx])
                            P.pool.dma_start(kpast[64:96, :], kpe_past[i, sidx])
                            for j0 in range(0, PAST, TT):
                                knope_v(pastb[:, j0:j0 + TT], TT, j0)
                            P.pool.tensor_copy(out=KT[64:96, :, 0:PAST], in_=kpast[64:96, :].bc3(1, 8))
                            kb = PAST
                        else:
                            kb = tk.t0
                        knope_v(ckv_b[:, qsl], L, kb)
                        P.pool.tensor_copy(out=KT[64:96, :, kb:kb + L], in_=kpe_b[64:96, qsl].bc3(1, 8))
                        kts = []
                        if tk.sample:
                            kts = [(j0, 128, 0, False) for j0 in range(0, PAST, 128)] + [(PAST, L, 0, False)]
                        else:
                            kts = [(j0, 128, 0, False) for j0 in range(0, tk.t0, 128)]
                            for d in range(L // 128):
                                kts.append((tk.t0 + d * 128, 128, d * 128, True))
                        for h in range(8):
                            par = h % 2
                            op_ = po.next()
                            orow = slice(par * 64, par * 64 + 64); srow = slice((1 - par) * 64, (1 - par) * 64 + 64)
                            for kidx, (k0, ksz, q0, msk) in enumerate(kts):
                                nq = L - q0
                                sp_ = pa.next()
                                P.pe.matmul(out=sp_[0:ksz, 0:nq], lhsT=KT[0:96, h, k0:k0 + ksz], rhs=Qf[0:96, h, si * L + q0:(si + 1) * L], start=True, stop=True)
                                pt = PT.next()
                                P.act.activation(out=pt[0:ksz, 0:nq], in_=sp_[0:ksz, 0:nq], func=AF.Exp, scale=float(96 ** -0.5))
                                if msk:
                                    P.pool.memset(pt[64:128, 0:64], 0.0)
                                first = kidx == 0
                                last = kidx == len(kts) - 1
                                P.pe.matmul(out=op_[orow, q0:L], lhsT=Vt[0:ksz, k0 // 128, h, :], rhs=pt[0:ksz, 0:nq], start=first, stop=last, skip_group_check=True)
                                P.pe.matmul(out=op_[srow, q0:L], lhsT=ones1[0:ksz, :], rhs=pt[0:ksz, 0:nq], start=first, stop=last, skip_group_check=True)
                            rsm = rsum.next()
                            P.dve.reciprocal(out=rsm[srow, 0:L], in_=op_[srow, 0:L])
                            P.dve.tensor_tensor(out=mixT[orow, h // 2, qsl], in0=op_[orow, 0:L], in1=rsm[srow, 0:L], op=ALU.mult)
                    P.sp.dma_start(View(omla_d[ti][:, :, 0:n], omla_res[ti]), mixT[:, 0:4, 0:n])
            with P.scope():
                Pm_b, BO96_b, BO64_b, Ib = load_consts(["MSI64", "MSI32", "R64", "R32"])
                win_r = P.sbuf("win_r", [128, KC, 1792], BF16)
                for k in range(KC):
                    P.pool.dma_start(win_r[:, k, :], win_e[i][k * 128:(k + 1) * 128, 416:2208])
                w2 = P.sbuf("w2", [128, 512], BF16); P.pool.dma_start(w2[0:64, :], w2_d[i])
                wa2 = P.sbuf("wa2", [128, 512], BF16); P.pool.dma_start(wa2[64:128, :], a2_d[i])
                g2 = P.sbuf("g2", [128, 512], BF16); P.pool.dma_start(g2, g2_d[i])
                wout = P.sbuf("wout", [128, KC, D], BF16)
                for k in range(KC):
                    P.pool.dma_start(wout[:, k, :], wout_e[i][k * 128:(k + 1) * 128, :])
                omka = P.sbuf("omka", [128, 4])
                P.dve.tensor_scalar(out=omka, in0=V(f"ka{i}"), scalar1=-1.0, scalar2=1.0, op0=ALU.mult, op1=ALU.add)
                lneps = P.sbuf("lneps", [128, 1]); P.dve.memset(lneps, 64e-5)
                shift_p = P.sbuf("shift_p", [128, 14, NB]); shift_s = P.sbuf("shift_s", [128, 14, NB])
                P.dve.memset(shift_p, 0.0); P.sp.dma_start(shift_s, shift_in[i])
                Hf = {}; Hb = {}
                for grp in ("p", "s"):
                    for sq_ in range(NB):
                        for hp in range(4):
                            Hf[grp, sq_, hp] = P.sbuf(f"Hf{grp}{sq_}{hp}", [128, 64])
                            Hb[grp, sq_, hp] = P.sbuf(f"Hb{grp}{sq_}{hp}", [128, 64], BF16)
                            if grp == "p":
                                P.pool.memset(Hf[grp, sq_, hp], 0.0)
                            else:
                                P.sp.dma_start(Hf[grp, sq_, hp], wkv_in[i, sq_, hp])
                            P.act.activation(out=Hb[grp, sq_, hp], in_=Hf[grp, sq_, hp], func=AF.Copy)
                rwb = Rot([P.sbuf(f"rwb{j}", [128, TT + NB]) for j in range(2)]); xm_t = P.sbuf("xm_t", [128, 14, TT])
                g_t = P.sbuf("g_t", [128, 4, TT])
                eG_t = P.sbuf("eG_t", [128, 4, TT])
                bonus_t = P.sbuf("bonus_t", [128, 4, TT])
                yT_t = P.sbuf("yT_t", [128, 4, TT])
                AR = P.sbuf("AR", [128, 4, 2 * TT], BF16); BK = P.sbuf("BK", [128, 4, 2 * TT], BF16)
                v_b = P.sbuf("v_b", [128, 4, TT], BF16)
                th_b = P.sbuf("th_b", [128, TT], BF16); da_b = P.sbuf("da_b", [128, TT], BF16); sg_b = P.sbuf("sg_b", [128, TT], BF16)
                mixT = P.sbuf("mixTB", [128, KC, TT], BF16)
                tf = Rot([P.sbuf(f"tf{j}", [128, TT]) for j in range(6)])
                tg = [P.sbuf(f"tg{j}", [128, TT]) for j in range(8)]
                tb = Rot([P.sbuf(f"tb{j}", [128, TT], BF16) for j in range(3)])
                pa = Rot(psb[0:7])
                def mk(name, shape, dt):
                    return [P.sbuf(f"{name}{hp}", shape, dt) for hp in range(4)]
                PB2 = []
                for par_ in range(2):
                    PB2.append(dict(tokm=mk(f"tokm{par_}_", [128, 192], BF16), NBs=mk(f"NBs{par_}_", [128, 256], BF16), NKs=mk(f"NKs{par_}_", [128, 256], BF16),
                                    Nm=[mk(f"Nm{par_}_{m}_", [128, 128], BF16) for m in range(6)],
                                    Xf=mk(f"Xf{par_}_", [128, 64], F32), Xb=mk(f"Xb{par_}_", [128, 64], BF16), Yb=mk(f"Yb{par_}_", [128, 64], BF16),
                                    tH=mk(f"tH{par_}_", [128, 64], F32)))
                Lp = [mk(f"Lp{m}_", [128, 128], BF16) for m in range(5)]

                for ti, tk in enumerate(tiles):
                    use_buf(ti % 2)
                    n, L, ns = tk.TT, tk.L, tk.nseg
                    grp = "s" if tk.sample else "p"
                    Cc = 32 if tk.sample else 64
                    nch = n // Cc
                    C2c = 2 * Cc
                    NM = 5 if tk.sample else 6
                    MSI = C2(f"MSI{Cc}"); Rm = C2(f"R{Cc}")
                    shiftst = shift_s if tk.sample else shift_p
                    load_x(ti, tk, l == 0)
                    modulate(tk, modA_m[l], mod[l][:, 0:8, :])
                    for j in range(14):
                        rwj = rwb.next()
                        rw3 = rwj[:, 0:ns * (L + 1)].rearrange("p (s t) -> p s t", s=ns)
                        for si in range(ns):
                            P.pool.tensor_copy(out=rw3[:, si, 0:1], in_=shiftst[:, j, tk.seqs[si]:tk.seqs[si] + 1])
                        ps = pa.next()
                        for k in range(KC):
                            P.pe.matmul(out=ps[:, 0:n], lhsT=win_r[:, k, j * 128:(j + 1) * 128], rhs=XH["h"][:, k, 0:n], start=(k == 0), stop=(k == KC - 1))
                        P.act.activation(out=rw3[:, :, 1:L + 1], in_=ps[:, 0:n].rearrange("p (s t) -> p s t", s=ns), func=AF.Copy)
                        for si in range(ns):
                            P.pool.tensor_copy(out=shiftst[:, j, tk.seqs[si]:tk.seqs[si] + 1], in_=rw3[:, si, L:L + 1])
                        d = tf.next()
                        d3 = d[:, 0:n].rearrange("p (s t) -> p s t", s=ns)
                        P.pool.tensor_tensor(out=d3, in0=rw3[:, :, 0:L], in1=rw3[:, :, 1:L + 1], op=ALU.subtract)
                        P.dve.scalar_tensor_tensor(out=xm_t[:, j, 0:n].rearrange("p (s t) -> p s t", s=ns), in0=d3, scalar=V(f"mu{i}", j, 1),
                                                   in1=rw3[:, :, 1:L + 1], op0=ALU.mult, op1=ALU.add)
                    P.act.activation(out=th_b[0:64, 0:n], in_=xm_t[0:64, 12, 0:n], func=AF.Tanh)
                    P.act.activation(out=da_b[64:128, 0:n], in_=xm_t[64:128, 12, 0:n], func=AF.Copy)
                    P.act.activation(out=sg_b[:, 0:n], in_=xm_t[:, 13, 0:n], func=AF.Sigmoid)
                    for c in range(4):
                        cs_ = slice(c * 128, (c + 1) * 128)
                        r_c, k_c, v_c = xm_t[:, c, 0:n], xm_t[:, 4 + c, 0:n], xm_t[:, 8 + c, 0:n]
                        ps = pa.next()
                        P.pe.matmul(out=ps[:, 0:n], lhsT=w2[0:64, cs_], rhs=th_b[0:64, 0:n], start=True, stop=True)
                        t0_ = tf.next()
                        P.act.activation(out=t0_[:, 0:n], in_=ps[:, 0:n], func=AF.Sigmoid, bias=V(f"w0{i}", c, 1), scale=1.0)
                        P.pool.tensor_scalar(out=tg[0][:, 0:n], in0=t0_[:, 0:n], scalar1=-0.6065306597126334, scalar2=None, op0=ALU.mult)
                        ps = pa.next()
                        P.pe.matmul(out=ps[:, 0:n], lhsT=wa2[64:128, cs_], rhs=da_b[64:128, 0:n], start=True, stop=True)
                        P.act.activation(out=tg[1][:, 0:n], in_=ps[:, 0:n], func=AF.Sigmoid, bias=V(f"a0{i}", c, 1), scale=1.0)
                        ps = pa.next()
                        P.pe.matmul(out=ps[:, 0:n], lhsT=g2[:, cs_], rhs=sg_b[:, 0:n], start=True, stop=True)
                        P.act.activation(out=g_t[:, c, 0:n], in_=ps[:, 0:n], func=AF.Copy)
                        t1 = tf.next()
                        P.pool.tensor_scalar(out=t1[:, 0:n], in0=k_c, scalar1=V(f"kk{i}", c, 1), scalar2=None, op0=ALU.mult)
                        sq = tb.next()
                        P.act.activation(out=sq[:, 0:n], in_=t1[:, 0:n], func=AF.Square)
                        ps = pa.next()
                        P.pe.matmul(out=ps[:, 0:n], lhsT=BO64_b, rhs=sq[:, 0:n], start=True, stop=True)
                        rs = tf.next()
                        P.act.activation(out=rs[:, 0:n], in_=ps[:, 0:n], func=AF.Sqrt, bias=eps_t, scale=64.0)
                        P.dve.reciprocal(out=rs[:, 0:n], in_=rs[:, 0:n])
                        P.dve.tensor_tensor(out=tg[2][:, 0:n], in0=t1[:, 0:n], in1=rs[:, 0:n], op=ALU.mult)
                        t2 = tf.next()
                        P.pool.tensor_scalar(out=t2[:, 0:n], in0=tg[1][:, 0:n], scalar1=V(f"ka{i}", c, 1), scalar2=omka[:, c:c + 1], op0=ALU.mult, op1=ALU.add)
                        P.dve.tensor_tensor(out=tg[3][:, 0:n], in0=k_c, in1=t2[:, 0:n], op=ALU.mult)
                        rkb = tb.next()
                        P.dve.scalar_tensor_tensor(out=rkb[:, 0:n], in0=r_c, scalar=V(f"rk{i}", c, 1), in1=tg[3][:, 0:n], op0=ALU.mult, op1=ALU.mult)
                        ps = pa.next()
                        P.pe.matmul(out=ps[:, 0:n], lhsT=BO64_b, rhs=rkb[:, 0:n], start=True, stop=True)
                        P.dve.scalar_tensor_tensor(out=bonus_t[:, c, 0:n], in0=ps[:, 0:n], scalar=64.0, in1=v_c, op0=ALU.mult, op1=ALU.mult)
                        P.dve.tensor_tensor_scan(out=tg[4][:, 0:n], data0=Rm[:, 0:n], data1=tg[0][:, 0:n], initial=0.0, op0=ALU.mult, op1=ALU.add)
                        P.act.activation(out=eG_t[:, c, 0:n], in_=tg[4][:, 0:n], func=AF.Exp)
                        enG = tg[5]
                        P.act.activation(out=enG[:, 0:n], in_=tg[4][:, 0:n], func=AF.Exp, scale=-1.0)
                        gm = tg[6]
                        P.pool.tensor_tensor(out=gm[:, 0:n], in0=tg[4][:, 0:n], in1=tg[0][:, 0:n], op=ALU.subtract)
                        P.act.activation(out=gm[:, 0:n], in_=gm[:, 0:n], func=AF.Exp)
                        AR4 = AR[:, c, 0:2 * n].rearrange("p (q a t) -> p q a t", a=2, t=Cc)
                        BK4 = BK[:, c, 0:2 * n].rearrange("p (q a t) -> p q a t", a=2, t=Cc)
                        v3 = lambda vv: vv.rearrange("p (q t) -> p q t", t=Cc)
                        P.dve.scalar_tensor_tensor(out=AR4[:, :, 0, :], in0=v3(tg[2][:, 0:n]), scalar=-1.0, in1=v3(gm[:, 0:n]), op0=ALU.mult, op1=ALU.mult)
                        P.pool.tensor_tensor(out=AR4[:, :, 1, :], in0=v3(r_c), in1=v3(eG_t[:, c, 0:n]), op=ALU.mult)
                        bt = tg[7]
                        P.pool.tensor_tensor(out=bt[:, 0:n], in0=tg[2][:, 0:n], in1=tg[1][:, 0:n], op=ALU.mult)
                        P.dve.tensor_tensor(out=BK4[:, :, 0, :], in0=v3(bt[:, 0:n]), in1=v3(enG[:, 0:n]), op=ALU.mult)
                        P.pool.tensor_tensor(out=BK4[:, :, 1, :], in0=v3(tg[3][:, 0:n]), in1=v3(enG[:, 0:n]), op=ALU.mult)
                        P.act.activation(out=v_b[:, c, 0:n], in_=v_c, func=AF.Copy)
                    for ch in range(nch):
                        seq = tk.seqs[ch] if tk.sample else tk.seqs[0]
                        HF = [Hf[grp, seq, hp] for hp in range(4)]; HB = [Hb[grp, seq, hp] for hp in range(4)]
                        pb_ = PB2[ch % 2]
                        tokm, NBs, NKs, Nm, Xf, Xb, Yb, tH = (pb_[k_] for k_ in ("tokm", "NBs", "NKs", "Nm", "Xf", "Xb", "Yb", "tH"))
                        ARc = lambda hp: AR[:, hp, 0:2 * n].rearrange("p (q a t) -> p q a t", a=2, t=Cc)[:, ch]
                        BKc = lambda hp: BK[:, hp, 0:2 * n].rearrange("p (q a t) -> p q a t", a=2, t=Cc)[:, ch]
                        rows = [slice(0, 64), slice(64, 128)]
                        orow = [slice(0, Cc), slice(Cc, C2c)]
                        tsl = slice(ch * Cc, (ch + 1) * Cc)
                        for hp in range(4):
                            ps = pa.next()
                            for hh in range(2):
                                P.pe.matmul(out=ps[orow[hh], 0:64], lhsT=BKc(hp)[rows[hh], 0, :], rhs=Ib[rows[hh], rows[hh]], start=True, stop=True)
                                P.pe.matmul(out=ps[orow[hh], 64:128], lhsT=BKc(hp)[rows[hh], 1, :], rhs=Ib[rows[hh], rows[hh]], start=True, stop=True)
                                P.pe.matmul(out=ps[orow[hh], 128:192], lhsT=v_b[rows[hh], hp, tsl], rhs=Ib[rows[hh], rows[hh]], start=True, stop=True)
                            P.act.activation(out=tokm[hp][0:C2c, :], in_=ps[0:C2c, 0:192], func=AF.Copy)
                        for hp in range(4):
                            for which, dst in ((0, NBs), (1, NKs)):
                                ps = pa.next()
                                for hh in range(2):
                                    for a_ in range(2):
                                        P.pe.matmul(out=ps[orow[hh], a_ * C2c + hh * Cc:a_ * C2c + (hh + 1) * Cc], lhsT=BKc(hp)[rows[hh], which, :],
                                                    rhs=ARc(hp)[rows[hh], a_, :], start=True, stop=True)
                                P.dve.tensor_tensor(out=dst[hp][0:C2c, 0:2 * C2c], in0=ps[0:C2c, 0:2 * C2c], in1=MSI[0:C2c, :], op=ALU.mult)
                        for hp in range(4):
                            P.pool.tensor_copy(out=Nm[0][hp][0:C2c, 0:C2c], in_=NBs[hp][0:C2c, 0:C2c])
                            ps = pa.next()
                            P.pe.matmul(out=ps[0:C2c, 0:C2c], lhsT=NBs[hp][0:C2c, 0:C2c], rhs=Ib[0:C2c, 0:C2c], start=True, stop=True)
                            P.act.activation(out=Lp[0][hp][0:C2c, 0:C2c], in_=ps[0:C2c, 0:C2c], func=AF.Copy)
                        for m in range(NM - 1):
                            for hp in range(4):
                                ps = pa.next()
                                P.pe.matmul(out=ps[0:C2c, 0:C2c], lhsT=Lp[m][hp][0:C2c, 0:C2c], rhs=Nm[m][hp][0:C2c, 0:C2c], start=True, stop=True)
                                P.dve.tensor_copy(out=Nm[m + 1][hp][0:C2c, 0:C2c], in_=ps[0:C2c, 0:C2c])
                                if m < NM - 2:
                                    ps = pa.next()
                                    P.pe.matmul(out=ps[0:C2c, 0:C2c], lhsT=Nm[m][hp][0:C2c, 0:C2c], rhs=Lp[m][hp][0:C2c, 0:C2c], start=True, stop=True)
                                    P.act.activation(out=Lp[m + 1][hp][0:C2c, 0:C2c], in_=ps[0:C2c, 0:C2c], func=AF.Copy)
                        for hp in range(4):
                            ps = pa.next()
                            P.pe.matmul(out=ps[0:C2c, 0:64], lhsT=NKs[hp][0:C2c, 0:C2c], rhs=tokm[hp][0:C2c, 128:192], start=True, stop=False, skip_group_check=True)
                            for hh in range(2):
                                P.pe.matmul(out=ps[orow[hh], 0:64], lhsT=ARc(hp)[rows[hh], 0, :], rhs=HB[hp][rows[hh], :], start=False, stop=(hh == 1), skip_group_check=True)
                            P.act.activation(out=Xf[hp][0:C2c, :], in_=ps[0:C2c, 0:64], func=AF.Copy)
                            P.dve.tensor_copy(out=Xb[hp][0:C2c, :], in_=Xf[hp][0:C2c, :])
                        for m in range(NM):
                            for hp in range(4):
                                ps = pa.next()
                                P.pe.matmul(out=ps[0:C2c, 0:64], lhsT=Nm[m][hp][0:C2c, 0:C2c], rhs=Xb[hp][0:C2c, :], start=True, stop=True)
                                P.dve.tensor_tensor(out=Xf[hp][0:C2c, :], in0=ps[0:C2c, 0:64], in1=Xf[hp][0:C2c, :], op=ALU.add)
                                P.act.activation(out=Xb[hp][0:C2c, :], in_=Xf[hp][0:C2c, :], func=AF.Copy)
                        for hp in range(4):
                            ps = pa.next()
                            P.pe.matmul(out=ps[0:C2c, 0:64], lhsT=NBs[hp][0:C2c, C2c:2 * C2c], rhs=Xb[hp][0:C2c, :], start=True, stop=False, skip_group_check=True)
                            P.pe.matmul(out=ps[0:C2c, 0:64], lhsT=NKs[hp][0:C2c, C2c:2 * C2c], rhs=tokm[hp][0:C2c, 128:192], start=False, stop=False, skip_group_check=True)
                            for hh in range(2):
                                P.pe.matmul(out=ps[orow[hh], 0:64], lhsT=ARc(hp)[rows[hh], 1, :], rhs=HB[hp][rows[hh], :], start=False, stop=(hh == 1), skip_group_check=True)
                            P.act.activation(out=Yb[hp][0:C2c, :], in_=ps[0:C2c, 0:64], func=AF.Copy)
                        for hp in range(4):
                            ps = pa.next()
                            for hh in range(2):
                                P.pe.matmul(out=ps[rows[hh], 0:Cc], lhsT=Yb[hp][orow[hh], :], rhs=Ib[orow[hh], orow[hh]], start=True, stop=True)
                            P.dve.tensor_copy(out=yT_t[:, hp, tsl], in_=ps[:, 0:Cc])
                        for hp in range(4):
                            ps = pa.next()
                            for hh in range(2):
                                P.pe.matmul(out=ps[rows[hh], 0:64], lhsT=tokm[hp][orow[hh], 0:64], rhs=Xb[hp][orow[hh], :], start=True, stop=False, skip_group_check=True)
                                P.pe.matmul(out=ps[rows[hh], 0:64], lhsT=tokm[hp][orow[hh], 64:128], rhs=tokm[hp][orow[hh], 128:192], start=False, stop=True, skip_group_check=True)
                            P.dve.tensor_tensor(out=tH[hp], in0=ps[:, 0:64], in1=HF[hp], op=ALU.add)
                            P.pool.tensor_scalar(out=HF[hp], in0=tH[hp], scalar1=eG_t[:, hp, ch * Cc + Cc - 1:ch * Cc + Cc], scalar2=None, op0=ALU.mult)
                            P.act.activation(out=HB[hp], in_=HF[hp], func=AF.Copy)
                    for hp in range(4):
                        yb = tb.next(); sq = tb.next()
                        P.act.activation(out=yb[:, 0:n], in_=yT_t[:, hp, 0:n], func=AF.Copy)
                        P.act.activation(out=sq[:, 0:n], in_=yT_t[:, hp, 0:n], func=AF.Square)
                        pm_ = pa.next(); pe_ = pa.next()
                        P.pe.matmul(out=pm_[:, 0:n], lhsT=BO64_b, rhs=yb[:, 0:n], start=True, stop=True)
                        P.pe.matmul(out=pe_[:, 0:n], lhsT=BO64_b, rhs=sq[:, 0:n], start=True, stop=True)
                        ms = tf.next(); m2 = tf.next(); var = tf.next()
                        P.act.activation(out=ms[:, 0:n], in_=pm_[:, 0:n], func=AF.Copy)
                        P.pool.tensor_tensor(out=m2[:, 0:n], in0=ms[:, 0:n], in1=ms[:, 0:n], op=ALU.mult)
                        P.dve.tensor_tensor(out=var[:, 0:n], in0=pe_[:, 0:n], in1=m2[:, 0:n], op=ALU.subtract)
                        P.dve.tensor_scalar(out=var[:, 0:n], in0=var[:, 0:n], scalar1=0.0, scalar2=None, op0=ALU.max)
                        P.act.activation(out=var[:, 0:n], in_=var[:, 0:n], func=AF.Sqrt, bias=lneps, scale=1.0)
                        P.dve.reciprocal(out=var[:, 0:n], in_=var[:, 0:n])
                        yc = tf.next()
                        P.pool.tensor_tensor(out=yc[:, 0:n], in0=yT_t[:, hp, 0:n], in1=ms[:, 0:n], op=ALU.subtract)
                        P.dve.tensor_tensor(out=yc[:, 0:n], in0=yc[:, 0:n], in1=var[:, 0:n], op=ALU.mult)
                        P.pool.tensor_scalar(out=yc[:, 0:n], in0=yc[:, 0:n], scalar1=V(f"lng{i}", hp, 1), scalar2=V(f"lnb{i}", hp, 1), op0=ALU.mult, op1=ALU.add)
                        P.dve.tensor_tensor(out=yc[:, 0:n], in0=yc[:, 0:n], in1=bonus_t[:, hp, 0:n], op=ALU.add)
                        P.dve.tensor_tensor(out=mixT[:, 4 + hp, 0:n], in0=yc[:, 0:n], in1=g_t[:, hp, 0:n], op=ALU.mult)
                    P.sp.dma_start(mixT[:, 0:4, 0:n], View(omla_d[ti][:, :, 0:n], omla_res[ti]))
                    for oc in range(KC):
                        d_ps = pa.next()
                        for c in range(KC):
                            P.pe.matmul(out=d_ps[:, 0:n], lhsT=wout[:, c, oc * 128:(oc + 1) * 128], rhs=mixT[:, c, 0:n], start=(c == 0), stop=(c == KC - 1))
                        for si, row in enumerate(tk.rows):
                            sl = slice(si * L, (si + 1) * L)
                            P.dve.scalar_tensor_tensor(out=XH["x"][:, oc, sl], in0=d_ps[:, sl], scalar=mod[l][:, 16 + oc, row:row + 1],
                                                       in1=XH["x"][:, oc, sl], op0=ALU.mult, op1=ALU.add)
                    store_x(ti, tk, False)
                P.sp.dma_start(View(o_shift_p[i], Res("o")), shift_p, final=True)
                P.sp.dma_start(View(o_shift_s[i], Res("o")), shift_s, final=True)
                for sq_ in range(NB):
                    for hp in range(4):
                        P.sp.dma_start(View(o_wkv_p[i, sq_, hp], Res("o")), Hf["p", sq_, hp], final=True)
                        P.sp.dma_start(View(o_wkv_s[i, sq_, hp], Res("o")), Hf["s", sq_, hp], final=True)
        if DO_MIX and l % 2 == 1:
            i = l // 2
            P.barrier()
            with P.scope():
                Pm_b, BO96_b, BO64_b, Ib = load_consts([n_ for n_ in CST2 if not n_.startswith("MSI6") and not n_.startswith("MSI3") and n_ not in ("R64", "R32")])
                wino = P.sbuf("wino", [128, KC, 2568], BF16)
                for k in range(KC):
                    P.pool.dma_start(wino[:, k, :], win_o[i][k * 128:(k + 1) * 128, :])
                poolw = P.sbuf("poolw", [128, 4, 128], BF16)
                for gi in range(4):
                    P.pool.dma_start(poolw[:, gi, :], poolw_d[i, gi])
                wout = P.sbuf("wouto", [128, KC, D], BF16)
                for k in range(KC):
                    P.pool.dma_start(wout[:, k, :], wout_o[i][k * 128:(k + 1) * 128, :])
                one_t = P.sbuf("one_t", [128, 1]); P.dve.memset(one_t, 1.0)
                negA = P.sbuf("negA", [128, 1])
                P.act.activation(out=negA, in_=V(f"alog{i}"), func=AF.Exp)
                P.dve.tensor_scalar(out=negA, in0=negA, scalar1=-1.0, scalar2=None, op0=ALU.mult)
                ones128 = P.sbuf("ones128o", [128, 128], BF16); P.dve.memset(ones128, 1.0 / 128)
                phist = {"p": P.sbuf("phist_p", [128, 4, NB, 15]), "s": P.sbuf("phist_s", [128, 4, NB, 15])}
                chist = {"p": P.sbuf("chist_p", [128, 12, NB, 3]), "s": P.sbuf("chist_s", [128, 12, NB, 3])}
                P.dve.memset(phist["p"], 0.0); P.dve.memset(chist["p"], 0.0)
                P.sp.dma_start(phist["s"], pool_in[i]); P.sp.dma_start(chist["s"], gconv_in[i])
                Sf = {}; Sb = {}
                for grp in ("p", "s"):
                    for sq_ in range(NB):
                        for h in range(4):
                            Sf[grp, sq_, h] = P.sbuf(f"Sf{grp}{sq_}{h}", [128, 128])
                            if grp == "p":
                                P.pool.memset(Sf[grp, sq_, h], 0.0)
                            else:
                                P.sp.dma_start(Sf[grp, sq_, h], gdn_in[i, sq_, h])
                Sbc = [P.sbuf(f"Sbc{h}", [128, 128], BF16) for h in range(4)]
                mixT = P.sbuf("mixTo", [128, KC, TT], BF16)
                zs_t = P.sbuf("zs_t", [128, 4, TT]); oT_t = P.sbuf("oT_t", [128, 4, TT])
                KQ = P.sbuf("KQ", [128, 4, 2 * TT], BF16); k_b = P.sbuf("k_b", [128, 4, TT], BF16); v_b = P.sbuf("v_bo", [128, 4, TT], BF16)
                sig8 = P.sbuf("sig8", [8, TT]); g8 = P.sbuf("g8", [8, TT])
                ubs = Rot([P.sbuf(f"ub{j}", [128, TT + 15 * NB]) for j in range(2)])
                sAB = [P.sbuf(f"sAB{j}", [128, TT + 15 * NB]) for j in range(2)]
                cbs = Rot([P.sbuf(f"cbf{j}", [128, TT + 3 * NB]) for j in range(2)])
                tf = Rot([P.sbuf(f"tfo{j}", [128, TT]) for j in range(6)])
                tb = Rot([P.sbuf(f"tbo{j}", [128, TT], BF16) for j in range(3)])
                pa = Rot(psb[0:7])

                def mk(name, shape, dt):
                    return [P.sbuf(f"{name}{hp}", shape, dt) for hp in range(2)]
                def mk2(name, shape, dt):
                    return [[P.sbuf(f"{name}{par_}_{hp}", shape, dt) for hp in range(2)] for par_ in range(2)]
                G2 = dict(cols_s=mk2("cols", [128, 3], F32), ghl=mk2("ghl", [128, 2], BF16), TGh=mk2("TGh", [128, 128], BF16), TGl=mk2("TGl", [128, 128], BF16))
                cbo = {}
                for nm in ("OH8", "ONB64", "NONB64", "TRI64", "STRI64", "BLK64_0", "BLK64_1", "ONB32", "NONB32", "TRI32", "STRI32", "BLK32_0", "BLK32_1",
                           "SEL8_0", "SEL8_1", "SEL8_2", "SEL8_3"):
                    cbo[nm] = P.sbuf("cbo_" + nm, [128, CST2[nm][1]], BF16)
                    P.dve.tensor_copy(out=cbo[nm], in_=C2(nm))
                sig8b = P.sbuf("sig8b", [8, TT], BF16); g8h = P.sbuf("g8h", [8, TT], BF16); g8l = P.sbuf("g8l", [8, TT], BF16); g8r = P.sbuf("g8r", [8, TT])
                G2.update(dict(dmat=mk2("dmat", [128, 128], F32), EM=mk2("EM", [128, 256], F32), eGcol=mk2("eGcol", [128, 1], F32),
                               wcol=mk2("wcol", [128, 1], F32), egc0=mk2("egc0_", [128, 1], F32), egc1=mk2("egc1_", [128, 1], F32),
                               NQs=mk2("NQs", [128, 256], BF16), Nm0=mk2("oNm0_", [128, 128], BF16), Lp0=mk2("oLp0_", [128, 128], BF16),
                               KW=mk2("KW", [128, 128], BF16), tks=mk2("tks", [128, 128], F32), Xf=mk2("oXf", [128, 128], F32), Xb=mk2("oXb", [128, 128], BF16),
                               tqs=mk2("tqs", [128, 128], F32), o_b=mk2("o_b", [128, 128], BF16),
                               Tm=mk2("Tm", [128, 128], BF16), TTm=mk2("TTm", [128, 128], BF16), Wb=mk2("Wb", [128, 128], BF16), Ub=mk2("Ub", [128, 128], BF16)))

                def proj(c0, m, n):
                    ps = pa.next()
                    for k in range(KC):
                        P.pe.matmul(out=ps[0:m, 0:n], lhsT=wino[:, k, c0:c0 + m], rhs=XH["h"][:, k, 0:n], start=(k == 0), stop=(k == KC - 1))
                    return ps

                for ti, tk in enumerate(tiles):
                    use_buf(ti % 2)
                    n, L, ns = tk.TT, tk.L, tk.nseg
                    grp = "s" if tk.sample else "p"
                    Cc = 32 if tk.sample else 64
                    nch = n // Cc
                    C2c = 2 * Cc
                    NM = 5 if tk.sample else 6
                    s0 = tk.seqs[0]
                    load_x(ti, tk, False)
                    modulate(tk, modA_m[l], mod[l][:, 0:8, :])
                    if ODD_STAGE < 9:
                        P.dve.memset(mixT[:, :, 0:n], 0.0); P.dve.memset(oT_t[:, :, 0:n], 0.0)
                    for gi, w in enumerate((2, 4, 8, 16) if ODD_STAGE >= 1 else ()):
                        ps = proj(gi * 128, 128, n)
                        ub = ubs.next()
                        u3 = ub[:, 0:ns * (L + 15)].rearrange("p (s t) -> p s t", s=ns)
                        P.pool.tensor_copy(out=u3[:, :, 0:15], in_=phist[grp][:, gi, s0:s0 + ns, :])
                        P.act.activation(out=u3[:, :, 15:15 + L], in_=ps[:, 0:n].rearrange("p (s t) -> p s t", s=ns), func=AF.Copy)
                        P.pool.tensor_copy(out=phist[grp][:, gi, s0:s0 + ns, :], in_=u3[:, :, L:L + 15])
                        cur = u3
                        sh = 1
                        for lev in range(gi + 1):
                            nxt = sAB[lev % 2][:, 0:ns * (L + 15)].rearrange("p (s t) -> p s t", s=ns)
                            lo = 2 * sh - 1
                            eng = P.dve if lev % 2 == 0 else P.pool
                            eng.tensor_tensor(out=nxt[:, :, lo:L + 15], in0=cur[:, :, lo:L + 15], in1=cur[:, :, lo - sh:L + 15 - sh], op=ALU.add)
                            cur = nxt
                            sh *= 2
                        df = tb.next()
                        d3 = df[:, 0:n].rearrange("p (s t) -> p s t", s=ns)
                        P.dve.scalar_tensor_tensor(out=d3, in0=cur[:, :, 15:15 + L], scalar=1.0 / w, in1=u3[:, :, 15:15 + L], op0=ALU.mult, op1=ALU.subtract)
                        if tk.first and not tk.sample:
                            t_ = tf.next()
                            P.dve.tensor_tensor(out=t_[:, 0:15], in0=cur[:, 0, 15:30], in1=C2("ICT", gi * 15, 15), op=ALU.mult)
                            P.dve.tensor_tensor(out=df[:, 0:15], in0=t_[:, 0:15], in1=u3[:, 0, 15:30], op=ALU.subtract)
                        ps = pa.next()
                        P.pe.matmul(out=ps[:, 0:n], lhsT=poolw[:, gi, :], rhs=df[:, 0:n], start=True, stop=True)
                        P.dve.tensor_scalar(out=mixT[:, gi, 0:n], in0=ps[:, 0:n], scalar1=V(f"pscale{i}", gi, 1), scalar2=None, op0=ALU.mult)
                    if ODD_STAGE < 2:
                        continue_ = True
                    ps = proj(2560, 8, n)
                    P.act.activation(out=sig8[:, 0:n], in_=ps[0:8, 0:n], func=AF.Sigmoid)
                    e8 = tf.next()
                    P.act.activation(out=e8[0:8, 0:n], in_=ps[0:8, 0:n], func=AF.Exp, bias=V(f"dtb{i}")[0:8, :], scale=1.0)
                    P.act.activation(out=e8[0:8, 0:n], in_=e8[0:8, 0:n], func=AF.Ln, bias=one_t[0:8, :], scale=1.0)
                    P.dve.tensor_scalar(out=g8[:, 0:n], in0=e8[0:8, 0:n], scalar1=negA[0:8, :], scalar2=None, op0=ALU.mult)
                    P.act.activation(out=sig8b[:, 0:n], in_=sig8[:, 0:n], func=AF.Copy)
                    P.act.activation(out=g8h[:, 0:n], in_=g8[:, 0:n], func=AF.Copy)
                    P.dve.tensor_tensor(out=g8r[:, 0:n], in0=g8[:, 0:n], in1=g8h[:, 0:n], op=ALU.subtract)
                    P.act.activation(out=g8l[:, 0:n], in_=g8r[:, 0:n], func=AF.Copy)
                    for j in range(12):
                        ps = proj(512 + j * 128, 128, n)
                        cbf = cbs.next()
                        c3 = cbf[:, 0:ns * (L + 3)].rearrange("p (s t) -> p s t", s=ns)
                        P.pool.tensor_copy(out=c3[:, :, 0:3], in_=chist[grp][:, j, s0:s0 + ns, :])
                        P.act.activation(out=c3[:, :, 3:3 + L], in_=ps[:, 0:n].rearrange("p (s t) -> p s t", s=ns), func=AF.Copy)
                        P.pool.tensor_copy(out=chist[grp][:, j, s0:s0 + ns, :], in_=c3[:, :, L:L + 3])
                        acc = tf.next()
                        a3 = acc[:, 0:n].rearrange("p (s t) -> p s t", s=ns)
                        P.act.activation(out=a3, in_=c3[:, :, 0:L], func=AF.Copy, scale=V(f"gcw{i}_0", j, 1))
                        for t in range(1, 4):
                            P.dve.scalar_tensor_tensor(out=a3, in0=c3[:, :, t:t + L], scalar=V(f"gcw{i}_{t}", j, 1), in1=a3, op0=ALU.mult, op1=ALU.add)
                        sl_ = tf.next()
                        P.act.activation(out=sl_[:, 0:n], in_=acc[:, 0:n], func=AF.Silu)
                        h = j % 4
                        if j < 8:
                            sq = tb.next()
                            P.act.activation(out=sq[:, 0:n], in_=sl_[:, 0:n], func=AF.Square)
                            ps2 = pa.next()
                            P.pe.matmul(out=ps2[:, 0:n], lhsT=ones128, rhs=sq[:, 0:n], start=True, stop=True)
                            rs = tf.next()
                            P.act.activation(out=rs[:, 0:n], in_=ps2[:, 0:n], func=AF.Sqrt, bias=eps_t, scale=128.0)
                            P.dve.reciprocal(out=rs[:, 0:n], in_=rs[:, 0:n])
                            if j < 4:
                                KQ4 = KQ[:, h, 0:2 * n].rearrange("p (q a t) -> p q a t", a=2, t=Cc)
                                P.dve.scalar_tensor_tensor(out=KQ4[:, :, 1, :], in0=sl_[:, 0:n].rearrange("p (q t) -> p q t", t=Cc), scalar=float(128 ** -0.5),
                                                           in1=rs[:, 0:n].rearrange("p (q t) -> p q t", t=Cc), op0=ALU.mult, op1=ALU.mult)
                            else:
                                P.dve.tensor_tensor(out=k_b[:, h, 0:n], in0=sl_[:, 0:n], in1=rs[:, 0:n], op=ALU.mult)
                                psb_ = pa.next()
                                P.pe.matmul(out=psb_[:, 0:n], lhsT=cbo[f"SEL8_{h}"][0:8, :], rhs=sig8b[0:8, 0:n], start=True, stop=True)
                                KQ4 = KQ[:, h, 0:2 * n].rearrange("p (q a t) -> p q a t", a=2, t=Cc)
                                P.dve.tensor_tensor(out=KQ4[:, :, 0, :], in0=psb_[:, 0:n].rearrange("p (q t) -> p q t", t=Cc),
                                                    in1=k_b[:, h, 0:n].rearrange("p (q t) -> p q t", t=Cc), op=ALU.mult)
                        else:
                            P.act.activation(out=v_b[:, h, 0:n], in_=sl_[:, 0:n], func=AF.Copy)
                    for h in range(4):
                        ps = proj(2048 + h * 128, 128, n)
                        P.act.activation(out=zs_t[:, h, 0:n], in_=ps[:, 0:n], func=AF.Silu)
                    MSIN = C2(f"MSIN{Cc}"); TRI = C2(f"TRI{Cc}")
                    orow = [slice(0, Cc), slice(Cc, C2c)]
                    for ch in range(nch if ODD_STAGE >= 3 else 0):
                        seq = tk.seqs[ch] if tk.sample else tk.seqs[0]
                        tsl = slice(ch * Cc, (ch + 1) * Cc)
                        KQc = lambda h: KQ[:, h, 0:2 * n].rearrange("p (q a t) -> p q a t", a=2, t=Cc)[:, ch]
                        if tk.sample or (tk.first and ch == 0):
                            for h_ in range(4):
                                P.act.activation(out=Sbc[h_], in_=Sf[grp, seq, h_], func=AF.Copy)
                        gq_ = {k_: v_[ch % 2] for k_, v_ in G2.items()}
                        cols_s, ghl, TGh, TGl, dmat, EM, eGcol, wcol = (gq_[k_] for k_ in ("cols_s", "ghl", "TGh", "TGl", "dmat", "EM", "eGcol", "wcol"))
                        egc = [gq_["egc0"], gq_["egc1"]]
                        NQs, KW, tks, Xf, Xb, tqs, o_b, Tm, TTm, Wb, Ub = (gq_[k_] for k_ in ("NQs", "KW", "tks", "Xf", "Xb", "tqs", "o_b", "Tm", "TTm", "Wb", "Ub"))
                        Nm = [gq_["Nm0"]]; Lp = [gq_["Lp0"]]
                        for hp in range(2 if SUB >= 1 else 0):
                            ps = pa.next()
                            for hh in range(2):
                                h = 2 * hp + hh
                                P.pe.matmul(out=ps[orow[hh], 0:1], lhsT=sig8b[0:8, tsl], rhs=cbo["OH8"][0:8, h:h + 1], start=True, stop=True)
                                P.pe.matmul(out=ps[orow[hh], 1:2], lhsT=g8h[0:8, tsl], rhs=cbo["OH8"][0:8, 4 + h:5 + h], start=True, stop=True)
                                P.pe.matmul(out=ps[orow[hh], 2:3], lhsT=g8l[0:8, tsl], rhs=cbo["OH8"][0:8, 4 + h:5 + h], start=True, stop=True)
                            P.act.activation(out=cols_s[hp][0:C2c, 0:3], in_=ps[0:C2c, 0:3], func=AF.Copy)
                            P.dve.tensor_copy(out=ghl[hp][0:C2c, :], in_=cols_s[hp][0:C2c, 1:3])
                            P.pool.tensor_scalar(out=TGh[hp][0:C2c, 0:C2c], in0=TRI[0:C2c, :], scalar1=cols_s[hp][0:C2c, 1:2], scalar2=None, op0=ALU.mult)
                            P.pool.tensor_scalar(out=TGl[hp][0:C2c, 0:C2c], in0=TRI[0:C2c, :], scalar1=cols_s[hp][0:C2c, 2:3], scalar2=None, op0=ALU.mult)
                            ps = pa.next()
                            P.pe.matmul(out=ps[0:C2c, 0:C2c], lhsT=cbo[f"ONB{Cc}"][0:C2c, 0:C2c], rhs=TGh[hp][0:C2c, 0:C2c], start=True, stop=False)
                            P.pe.matmul(out=ps[0:C2c, 0:C2c], lhsT=cbo[f"ONB{Cc}"][0:C2c, 0:C2c], rhs=TGl[hp][0:C2c, 0:C2c], start=False, stop=False)
                            P.pe.matmul(out=ps[0:C2c, 0:C2c], lhsT=TGh[hp][0:C2c, 0:C2c], rhs=cbo[f"NONB{Cc}"][0:C2c, 0:C2c], start=False, stop=False)
                            P.pe.matmul(out=ps[0:C2c, 0:C2c], lhsT=TGl[hp][0:C2c, 0:C2c], rhs=cbo[f"NONB{Cc}"][0:C2c, 0:C2c], start=False, stop=True)
                            P.dve.tensor_scalar(out=dmat[hp][0:C2c, 0:C2c], in0=ps[0:C2c, 0:C2c], scalar1=0.0, scalar2=None, op0=ALU.min)
                            P.act.activation(out=dmat[hp][0:C2c, 0:C2c], in_=dmat[hp][0:C2c, 0:C2c], func=AF.Exp)
                            P.dve.tensor_tensor(out=EM[hp][0:C2c, 0:2 * C2c].rearrange("p (a t) -> p a t", a=2), in0=MSIN[0:C2c, :].rearrange("p (a t) -> p a t", a=2),
                                                in1=dmat[hp][0:C2c, 0:C2c].bc3(1, 2), op=ALU.mult)
                            ps = pa.next()
                            for q_ in range(2):
                                P.pe.matmul(out=ps[0:C2c, 0:1], lhsT=cbo[f"TRI{Cc}"][0:C2c, 0:C2c], rhs=ghl[hp][0:C2c, q_:q_ + 1], start=(q_ == 0), stop=(q_ == 1))
                            ps2 = pa.next()
                            for q_ in range(2):
                                P.pe.matmul(out=ps2[0:C2c, 0:1], lhsT=cbo[f"STRI{Cc}"][0:C2c, 0:C2c], rhs=ghl[hp][0:C2c, q_:q_ + 1], start=(q_ == 0), stop=(q_ == 1))
                            P.act.activation(out=eGcol[hp][0:C2c, :], in_=ps[0:C2c, 0:1], func=AF.Exp)
                            P.act.activation(out=wcol[hp][0:C2c, :], in_=ps2[0:C2c, 0:1], func=AF.Exp)
                            for hh in range(2):
                                ps = pa.next()
                                for q_ in range(2):
                                    P.pe.matmul(out=ps[:, 0:1], lhsT=cbo[f"BLK{Cc}_{hh}"][0:C2c, :], rhs=ghl[hp][0:C2c, q_:q_ + 1], start=(q_ == 0), stop=(q_ == 1))
                                P.act.activation(out=egc[hh][hp], in_=ps[:, 0:1], func=AF.Exp)
                        for hp in range(2 if SUB >= 2 else 0):
                            ps = pa.next()
                            for hh in range(2):
                                h = 2 * hp + hh
                                for a_ in range(2):
                                    P.pe.matmul(out=ps[orow[hh], a_ * C2c + hh * Cc:a_ * C2c + (hh + 1) * Cc], lhsT=k_b[:, h, tsl], rhs=KQc(h)[:, a_, :], start=True, stop=True)
                            P.dve.tensor_tensor(out=NQs[hp][0:C2c, 0:2 * C2c], in0=ps[0:C2c, 0:2 * C2c], in1=EM[hp][0:C2c, 0:2 * C2c], op=ALU.mult)
                        for hp in range(2 if SUB >= 3 else 0):
                            P.pool.tensor_copy(out=Nm[0][hp][0:C2c, 0:C2c], in_=NQs[hp][0:C2c, 0:C2c])
                            ps = pa.next()
                            P.pe.matmul(out=ps[0:C2c, 0:C2c], lhsT=NQs[hp][0:C2c, 0:C2c], rhs=Ib[0:C2c, 0:C2c], start=True, stop=True)
                            P.act.activation(out=Lp[0][hp][0:C2c, 0:C2c], in_=ps[0:C2c, 0:C2c], func=AF.Copy)
                        LV = NM
                        for hp in range(2):
                            tmp_ = tks[hp]
                            P.dve.tensor_tensor(out=tmp_[0:C2c, 0:C2c], in0=Lp[0][hp][0:C2c, 0:C2c], in1=C2(f"MK{Cc}_0")[0:C2c, :], op=ALU.mult)
                            P.dve.tensor_tensor(out=Tm[hp][0:C2c, 0:C2c], in0=tmp_[0:C2c, 0:C2c], in1=C2("I128")[0:C2c, 0:C2c], op=ALU.add)
                            ps = pa.next()
                            P.pe.matmul(out=ps[0:C2c, 0:C2c], lhsT=Tm[hp][0:C2c, 0:C2c], rhs=Ib[0:C2c, 0:C2c], start=True, stop=True)
                            P.act.activation(out=TTm[hp][0:C2c, 0:C2c], in_=ps[0:C2c, 0:C2c], func=AF.Copy)
                        for lv in range(1, LV):
                            for hp in range(2):
                                ps = pa.next()
                                P.pe.matmul(out=ps[0:C2c, 0:C2c], lhsT=Nm[0][hp][0:C2c, 0:C2c], rhs=Tm[hp][0:C2c, 0:C2c], start=True, stop=True)
                                P.act.activation(out=Wb[hp][0:C2c, 0:C2c], in_=ps[0:C2c, 0:C2c], func=AF.Copy)
                                ps = pa.next()
                                P.pe.matmul(out=ps[0:C2c, 0:C2c], lhsT=TTm[hp][0:C2c, 0:C2c], rhs=Wb[hp][0:C2c, 0:C2c], start=True, stop=True)
                                tmp_ = tks[hp]
                                P.dve.tensor_tensor(out=tmp_[0:C2c, 0:C2c], in0=ps[0:C2c, 0:C2c], in1=C2(f"MK{Cc}_{lv}")[0:C2c, :], op=ALU.mult)
                                P.dve.tensor_tensor(out=Tm[hp][0:C2c, 0:C2c], in0=tmp_[0:C2c, 0:C2c], in1=Tm[hp][0:C2c, 0:C2c], op=ALU.add)
                                ps = pa.next()
                                P.pe.matmul(out=ps[0:C2c, 0:C2c], lhsT=Tm[hp][0:C2c, 0:C2c], rhs=Ib[0:C2c, 0:C2c], start=True, stop=True)
                                P.act.activation(out=TTm[hp][0:C2c, 0:C2c], in_=ps[0:C2c, 0:C2c], func=AF.Copy)
                        for hp in range(2 if SUB >= 5 else 0):
                            pT = pa.next()
                            for hh in range(2):
                                h = 2 * hp + hh
                                P.pe.matmul(out=pT[orow[hh], 0:128], lhsT=v_b[:, h, tsl], rhs=Ib, start=True, stop=True)
                                P.pe.matmul(out=pT[orow[hh], 128:256], lhsT=k_b[:, h, tsl], rhs=Ib, start=True, stop=True)
                            P.dve.tensor_scalar(out=KW[hp][0:C2c, :], in0=pT[0:C2c, 128:256], scalar1=wcol[hp][0:C2c, :], scalar2=None, op0=ALU.mult)
                            pK = pa.next()
                            for hh in range(2):
                                h = 2 * hp + hh
                                P.pe.matmul(out=pK[orow[hh], 0:128], lhsT=k_b[:, h, tsl], rhs=Sbc[h], start=True, stop=True)
                            P.dve.tensor_scalar(out=tks[hp][0:C2c, :], in0=pK[0:C2c, 0:128], scalar1=eGcol[hp][0:C2c, :], scalar2=None, op0=ALU.mult)
                            P.dve.tensor_tensor(out=Xf[hp][0:C2c, :], in0=pT[0:C2c, 0:128], in1=tks[hp][0:C2c, :], op=ALU.subtract)
                            P.pool.tensor_scalar(out=Xf[hp][0:C2c, :], in0=Xf[hp][0:C2c, :], scalar1=cols_s[hp][0:C2c, 0:1], scalar2=None, op0=ALU.mult)
                            P.act.activation(out=Xb[hp][0:C2c, :], in_=Xf[hp][0:C2c, :], func=AF.Copy)
                        for hp in range(2):
                            ps = pa.next()
                            P.pe.matmul(out=ps[0:C2c, 0:128], lhsT=TTm[hp][0:C2c, 0:C2c], rhs=Xb[hp][0:C2c, :], start=True, stop=True)
                            P.act.activation(out=Ub[hp][0:C2c, :], in_=ps[0:C2c, 0:128], func=AF.Copy)
                        for hp in range(2 if SUB >= 7 else 0):
                            pQ = pa.next()
                            for hh in range(2):
                                h = 2 * hp + hh
                                P.pe.matmul(out=pQ[orow[hh], 0:128], lhsT=KQc(h)[:, 1, :], rhs=Sbc[h], start=True, stop=True)
                            P.dve.tensor_scalar(out=tqs[hp][0:C2c, :], in0=pQ[0:C2c, 0:128], scalar1=eGcol[hp][0:C2c, :], scalar2=None, op0=ALU.mult)
                            pO = pa.next()
                            P.pe.matmul(out=pO[0:C2c, 0:128], lhsT=NQs[hp][0:C2c, C2c:2 * C2c], rhs=Ub[hp][0:C2c, :], start=True, stop=True)
                            P.dve.tensor_tensor(out=o_b[hp][0:C2c, :], in0=pO[0:C2c, 0:128], in1=tqs[hp][0:C2c, :], op=ALU.add)
                            if SUB >= 8:
                                for hh in range(2):
                                    pOT = pa.next()
                                    P.pe.matmul(out=pOT[:, 0:Cc], lhsT=o_b[hp][orow[hh], :], rhs=Ib[orow[hh], orow[hh]], start=True, stop=True)
                                    P.act.activation(out=oT_t[:, 2 * hp + hh, tsl], in_=pOT[:, 0:Cc], func=AF.Copy)
                            for hh in range(2 if SUB >= 9 else 0):
                                h = 2 * hp + hh
                                pS = pa.next()
                                P.pe.matmul(out=pS[:, 0:128], lhsT=KW[hp][orow[hh], :], rhs=Ub[hp][orow[hh], :], start=True, stop=True)
                                P.dve.scalar_tensor_tensor(out=Sf[grp, seq, h], in0=Sf[grp, seq, h], scalar=egc[hh][hp], in1=pS[:, 0:128], op0=ALU.mult, op1=ALU.add)
                                P.act.activation(out=Sbc[h], in_=Sf[grp, seq, h], func=AF.Copy)
                    for h in range(4):
                        sq = tb.next()
                        P.act.activation(out=sq[:, 0:n], in_=oT_t[:, h, 0:n], func=AF.Square)
                        ps = pa.next()
                        P.pe.matmul(out=ps[:, 0:n], lhsT=ones128, rhs=sq[:, 0:n], start=True, stop=True)
                        rs = tf.next()
                        P.act.activation(out=rs[:, 0:n], in_=ps[:, 0:n], func=AF.Sqrt, bias=eps_t, scale=1.0)
                        P.dve.reciprocal(out=rs[:, 0:n], in_=rs[:, 0:n])
                        t_ = tf.next()
                        P.dve.scalar_tensor_tensor(out=t_[:, 0:n], in0=oT_t[:, h, 0:n], scalar=V(f"gog{i}"), in1=rs[:, 0:n], op0=ALU.mult, op1=ALU.mult)
                        P.pool.tensor_tensor(out=mixT[:, 4 + h, 0:n], in0=t_[:, 0:n], in1=zs_t[:, h, 0:n], op=ALU.mult)
                    for oc in range(KC):
                        d_ps = pa.next()
                        for c in range(KC):
                            P.pe.matmul(out=d_ps[:, 0:n], lhsT=wout[:, c, oc * 128:(oc + 1) * 128], rhs=mixT[:, c, 0:n], start=(c == 0), stop=(c == KC - 1))
                        for si, row in enumerate(tk.rows):
                            sl = slice(si * L, (si + 1) * L)
                            P.dve.scalar_tensor_tensor(out=XH["x"][:, oc, sl], in0=d_ps[:, sl], scalar=mod[l][:, 16 + oc, row:row + 1],
                                                       in1=XH["x"][:, oc, sl], op0=ALU.mult, op1=ALU.add)
                    store_x(ti, tk, False)
                for grp, (op_, oc_, og_) in (("p", (o_pool_p, o_gconv_p, o_gdn_p)), ("s", (o_pool_s, o_gconv_s, o_gdn_s))):
                    P.sp.dma_start(View(op_[i], Res("o")), phist[grp], final=True)
                    P.sp.dma_start(View(oc_[i], Res("o")), chist[grp], final=True)
                    for sq_ in range(NB):
                        for h in range(4):
                            P.sp.dma_start(View(og_[i, sq_, h], Res("o")), Sf[grp, sq_, h], final=True)
        P.barrier()
        with P.scope():
            wg = P.sbuf("wg", [128, KC, DFF], BF16); wu = P.sbuf("wu", [128, KC, DFF], BF16)
            wd = P.sbuf("wd", [128, FC, D], BF16)
            for k in range(KC):
                P.pool.dma_start(wg[:, k, :], wg_d[l][k * 128:(k + 1) * 128, :])
                P.pool.dma_start(wu[:, k, :], wu_d[l][k * 128:(k + 1) * 128, :])
            for c in range(FC):
                P.pool.dma_start(wd[:, c, :], wd_d[l][c * 128:(c + 1) * 128, :])
            hist_p = P.sbuf("hist_p", [128, FC, NB, 2]); hist_s = P.sbuf("hist_s", [128, FC, NB, 2])
            P.dve.memset(hist_p, 0.0)
            P.sp.dma_start(hist_s, fh_d[l])
            act_t = P.sbuf("act_t", [128, FC, TTF], BF16)
            gbufs = Rot([P.sbuf(f"gbuf{i}", [128, TTF + 2 * NB]) for i in range(2)])
            accs = Rot([P.sbuf(f"acc{i}", [128, TTF]) for i in range(2)])
            gps = Rot(psb[0:3]); ups = Rot(psb[3:6]); dps = Rot(psb[0:6])
            first_layer = (l == 0 and not DO_MIX) or False
            for ti, tk in enumerate(ftiles):
                use_buf("full" if MK_NT % 2 == 0 else 0)
                n, L, ns = tk.TT, tk.L, tk.nseg
                load_x(ti, tk, l == 0 and not DO_MIX)
                modulate(tk, modA_f[l], mod[l][:, 24:32, :])
                hist = hist_s if tk.sample else hist_p
                s0 = tk.seqs[0]
                for c in range(FC):
                    g_ps = gps.next(); u_ps = ups.next()
                    for k in range(KC):
                        P.pe.matmul(out=g_ps[:, 0:n], lhsT=wg[:, k, c * 128:(c + 1) * 128], rhs=XH["h"][:, k, 0:n],
                                    start=(k == 0), stop=(k == KC - 1))
                    for k in range(KC):
                        P.pe.matmul(out=u_ps[:, 0:n], lhsT=wu[:, k, c * 128:(c + 1) * 128], rhs=XH["h"][:, k, 0:n],
                                    start=(k == 0), stop=(k == KC - 1))
                    gb = gbufs.next()
                    gb3 = gb[:, 0:ns * (L + 2)].rearrange("p (s t) -> p s t", s=ns)
                    P.pool.tensor_copy(out=gb3[:, :, 0:2], in_=hist[:, c, s0:s0 + ns, :])
                    P.act.activation(out=gb3[:, :, 2:L + 2], in_=g_ps[:, 0:n].rearrange("p (s t) -> p s t", s=ns), func=AF.Copy)
                    P.pool.tensor_copy(out=hist[:, c, s0:s0 + ns, :], in_=gb3[:, :, L:L + 2])
                    acc = accs.next()
                    a3 = acc[:, 0:n].rearrange("p (s t) -> p s t", s=ns)
                    P.act.activation(out=a3, in_=gb3[:, :, 0:L], func=AF.Copy, scale=V(f"fcw{l}_0", c, 1))
                    P.dve.scalar_tensor_tensor(out=a3, in0=gb3[:, :, 1:L + 1], scalar=V(f"fcw{l}_1", c, 1), in1=a3, op0=ALU.mult, op1=ALU.add)
                    P.dve.scalar_tensor_tensor(out=a3, in0=gb3[:, :, 2:L + 2], scalar=V(f"fcw{l}_2", c, 1), in1=a3, op0=ALU.mult, op1=ALU.add)
                    P.act.activation(out=acc[:, 0:n], in_=acc[:, 0:n], func=AF.Silu)
                    P.dve.tensor_tensor(out=act_t[:, c, 0:n], in0=u_ps[:, 0:n], in1=acc[:, 0:n], op=ALU.mult)
                for oc in range(KC):
                    d_ps = dps.next()
                    for c in range(FC):
                        P.pe.matmul(out=d_ps[:, 0:n], lhsT=wd[:, c, oc * 128:(oc + 1) * 128], rhs=act_t[:, c, 0:n],
                                    start=(c == 0), stop=(c == FC - 1))
                    for si, row in enumerate(tk.rows):
                        sl = slice(si * L, (si + 1) * L)
                        P.dve.scalar_tensor_tensor(out=XH["x"][:, oc, sl], in0=d_ps[:, sl], scalar=mod[l][:, 40 + oc, row:row + 1],
                                                   in1=XH["x"][:, oc, sl], op0=ALU.mult, op1=ALU.add)
                store_x(ti, tk, l == NLAYERS - 1)
            P.sp.dma_start(View(o_ffn_p[l], Res("o")), hist_p, final=True)
            P.sp.dma_start(View(o_ffn_s[l], Res("o")), hist_s, final=True)
    stats = P.finish()
    return nc, stats


_CACHE = {}


def kernel(**inputs):
    f32 = np.float32
    inp = {k: np.asarray(v) for k, v in inputs.items()}
    if "nc" not in _CACHE:
        _CACHE["nc"], _CACHE["stats"] = build_program()
    nc = _CACHE["nc"]
    vl = vec_layout(inp)
    vecs = vl.array()
    cst_np = const_array(); cst2_np = const_array2()
    B = inp["x_prompt"].shape[0]
    in_maps = []
    fh_all = inp["state_ffn_conv"]
    for c in range(NCORES):
        bs = slice(c * NB, (c + 1) * NB)
        m = {}
        m["xp"] = np.ascontiguousarray(inp["x_prompt"][bs].transpose(0, 2, 1).reshape(NB, KC, 128, SEQ), dtype=f32)
        m["xs"] = np.ascontiguousarray(inp["x_sample"][bs].transpose(0, 2, 1).reshape(NB, KC, 128, DSEQ), dtype=f32)
        cc = np.concatenate([inp["c_prompt"][bs], inp["c_sample"][bs]], axis=0)
        m["cT"] = np.ascontiguousarray(cc.T.reshape(KC, 128, 8).transpose(1, 0, 2), dtype=f32)
        m["vecs"] = vecs
        m["ada_w"] = inp["ada_w"]; m["ffn_w_gate"] = inp["ffn_w_gate"]; m["ffn_w_up"] = inp["ffn_w_up"]
        m["ffn_w_down"] = inp["ffn_w_down"]
        m["ffn_hist"] = np.ascontiguousarray(fh_all[:, bs].reshape(DEPTH, NB, 2, FC, 128).transpose(0, 4, 3, 1, 2), dtype=f32)
        m["consts"] = cst_np; m["consts2"] = cst2_np
        for k in ("rwkv_w2", "rwkv_a2", "rwkv_g2"):
            m[k] = inp[k]
        m["shift_in"] = np.ascontiguousarray(inp["state_rwkv_shift"][:, bs].reshape(2, NB, 14, 128).transpose(0, 3, 2, 1), dtype=f32)
        m["wkv_in"] = np.ascontiguousarray(inp["state_rwkv_wkv"][:, bs].reshape(2, NB, 4, 2, 64, 64).transpose(0, 1, 2, 3, 5, 4).reshape(2, NB, 4, 128, 64), dtype=f32)
        for k in ("even_w_in", "mla_w_uq", "mla_w_ukv", "even_w_out"):
            m[k] = inp[k]
        m["ckv_past"] = np.ascontiguousarray(inp["cache_mla_ckv"][:, bs].transpose(0, 1, 3, 2), dtype=f32)
        m["kpe_past"] = np.ascontiguousarray(inp["cache_mla_kpe"][:, bs].transpose(0, 1, 3, 2), dtype=f32)
        for k in ("odd_w_in", "pool_w", "odd_w_out"):
            m[k] = inp[k]
        m["pool_in"] = np.ascontiguousarray(inp["state_pool"][:, bs].reshape(2, NB, 15, 4, 128).transpose(0, 4, 3, 1, 2), dtype=f32)
        m["gconv_in"] = np.ascontiguousarray(inp["state_gdn_conv"][:, bs].reshape(2, NB, 3, 12, 128).transpose(0, 4, 3, 1, 2), dtype=f32)
        m["gdn_in"] = np.ascontiguousarray(inp["state_gdn"][:, bs], dtype=f32)
        in_maps.append(m)
    res = run_bass_kernel_spmd(nc, in_maps[:MK_CORES], core_ids=list(range(MK_CORES)))
    R = list(res.results) + [res.results[0]] * (NCORES - MK_CORES)

    def cat(name, fn):
        return np.concatenate([fn(np.asarray(r[name])) for r in R], axis=0)

    y_prompt = cat("yp", lambda a: a.reshape(NB, D, SEQ).transpose(0, 2, 1))
    y_sample = cat("ys", lambda a: a.reshape(NB, D, DSEQ).transpose(0, 2, 1))
    ffn_fix = lambda a: a.transpose(0, 3, 4, 2, 1).reshape(DEPTH, NB, 2, DFF)
    p_ffn = np.concatenate([ffn_fix(np.asarray(r["o_ffn_p"])) for r in R], axis=1)
    s_ffn = np.concatenate([ffn_fix(np.asarray(r["o_ffn_s"])) for r in R], axis=1)
    _CACHE["raw"] = R
    tr = lambda name: np.concatenate([np.asarray(r[name]).transpose(0, 1, 3, 2) for r in R], axis=1)
    p_ckv, p_kpe, s_ckv, s_kpe = tr("o_ckv_p"), tr("o_kpe_p"), tr("o_ckv_s"), tr("o_kpe_s")
    shf = lambda name: np.concatenate([np.asarray(r[name]).transpose(0, 3, 2, 1).reshape(2, NB, 1792) for r in R], axis=1)
    p_shift, s_shift = shf("o_shift_p"), shf("o_shift_s")
    wkvf = lambda name: np.concatenate([np.asarray(r[name]).reshape(2, NB, 4, 2, 64, 64).transpose(0, 1, 2, 3, 5, 4).reshape(2, NB, 8, 64, 64) for r in R], axis=1)
    p_wkv, s_wkv = wkvf("o_wkv_p"), wkvf("o_wkv_s")
    poolf = lambda name: np.concatenate([np.asarray(r[name]).transpose(0, 3, 4, 2, 1).reshape(2, NB, 15, 512) for r in R], axis=1)
    gcf = lambda name: np.concatenate([np.asarray(r[name]).transpose(0, 3, 4, 2, 1).reshape(2, NB, 3, 1536) for r in R], axis=1)
    gdf = lambda name: np.concatenate([np.asarray(r[name]) for r in R], axis=1)
    p_pool, s_pool, p_gc, s_gc, p_gdn, s_gdn = poolf("o_pool_p"), poolf("o_pool_s"), gcf("o_gconv_p"), gcf("o_gconv_s"), gdf("o_gdn_p"), gdf("o_gdn_s")
    z = lambda *s: np.zeros(s, f32)
    NE, NO = 2, 2
    outs = (np.ascontiguousarray(y_prompt, f32), np.ascontiguousarray(y_sample, f32),
            np.ascontiguousarray(p_ckv, f32), np.ascontiguousarray(p_kpe, f32), np.ascontiguousarray(p_shift, f32), np.ascontiguousarray(p_wkv, f32),
            np.ascontiguousarray(p_pool, f32), np.ascontiguousarray(p_gc, f32), np.ascontiguousarray(p_gdn, f32), np.ascontiguousarray(p_ffn, f32),
            np.ascontiguousarray(s_ckv, f32), np.ascontiguousarray(s_kpe, f32), np.ascontiguousarray(s_shift, f32), np.ascontiguousarray(s_wkv, f32),
            np.ascontiguousarray(s_pool, f32), np.ascontiguousarray(s_gc, f32), np.ascontiguousarray(s_gdn, f32), np.ascontiguousarray(s_ffn, f32))
    return outs
```

```python
from contextlib import ExitStack
import os
import numpy as np
import concourse.bass as bass
import concourse.mybir as mybir
from concourse.bass_utils import run_bass_kernel_spmd

F32 = mybir.dt.float32
BF16 = mybir.dt.bfloat16
I32 = mybir.dt.int32
ALU = mybir.AluOpType
AF = mybir.ActivationFunctionType
AX = mybir.AxisListType

ENGS = ("pe", "act", "dve", "pool", "sp")
N_DMA_SEMS = 8


class Res:
    __slots__ = ("name", "excl", "last_w", "readers")

    def __init__(self, name, excl=False):
        self.name = name
        self.excl = excl
        self.last_w = None
        self.readers = {}

    def add_reader(self, op):
        self.readers[id(op)] = op


class View:
    __slots__ = ("ap", "res")

    def __init__(self, ap, res):
        self.ap = ap
        self.res = res if isinstance(res, (list, tuple)) else [res]

    def __getitem__(self, idx):
        return View(self.ap[idx], self.res)

    def rearrange(self, *a, **k):
        return View(self.ap.rearrange(*a, **k), self.res)

    def bitcast(self, dt):
        return View(self.ap.bitcast(dt), self.res)

    @property
    def shape(self):
        return self.ap.shape

    def bc3(self, axis, n):
        sh = list(self.ap.shape)
        sh.insert(axis, n)
        return View(self.ap.unsqueeze(axis).to_broadcast(sh), self.res)


class Tile(View):
    pass


class Op:
    __slots__ = ("eng", "fn", "deps", "is_dma", "signal", "cnt", "dsem", "dval", "idx", "tag", "epoch", "odeps", "dur", "succ", "fin", "nun")

    def __init__(self, eng, fn, is_dma, tag=""):
        self.eng = eng
        self.fn = fn
        self.deps = []
        self.is_dma = is_dma
        self.signal = False
        self.cnt = 0
        self.dsem = None
        self.dval = 0
        self.idx = -1
        self.tag = tag
        self.epoch = 0
        self.odeps = []
        self.dur = 0.3
        self.succ = []
        self.fin = None
        self.nun = 0


class Prog:
    def __init__(self, nc):
        self.nc = nc
        self.es = ExitStack()
        self.ops = []
        self.final_ops = []
        self.n_res = 0
        self.bar_op = None
        self.bar_idx = 0
        self.epoch = 0

    def sbuf(self, name, shape, dt=F32):
        self.n_res += 1
        name = f"sb_{name}_{self.n_res}"
        t = self.es.enter_context(self.nc.sbuf_tensor(name, list(shape), dt))
        return Tile(t[:], Res(name))

    def psum(self, name, shape, dt=F32):
        self.n_res += 1
        name = f"ps_{name}_{self.n_res}"
        t = self.es.enter_context(self.nc.psum_tensor(name, list(shape), dt))
        return Tile(t[:], Res(name, excl=True))

    def dram(self, name, shape, dt=F32, kind="Internal"):
        t = self.nc.dram_tensor(name, list(shape), dt, kind=kind)
        return Tile(t.ap(), Res(name))

    def scope(self):
        prog = self

        class _S:
            def __enter__(s2):
                s2.saved = prog.es
                prog.es = ExitStack()
                return s2

            def __exit__(s2, *a):
                prog.es.close()
                prog.es = s2.saved
                prog.barrier()
                return False
        return _S()

    def barrier(self):
        last = {}
        dmas = []
        for op in self.ops[self.bar_idx:]:
            if op.is_dma:
                dmas.append(op)
            else:
                last[op.eng] = op
        op = Op("sp", lambda e: e.nop(), False, "barrier")
        op.deps = list(last.values()) + dmas + ([self.bar_op] if self.bar_op else [])
        self.epoch += 1
        op.epoch = self.epoch
        op.idx = len(self.ops)
        self.ops.append(op)
        self.bar_op = op
        self.bar_idx = len(self.ops)

    def _record(self, op, reads, writes):
        deps = []
        for r in reads:
            if r.excl:
                writes = list(writes) + [r]
                continue
            if r.last_w is not None:
                deps.append(r.last_w)
            r.add_reader(op)
        for w in writes:
            if w.last_w is not None:
                deps.append(w.last_w)
            deps.extend(w.readers.values())
            w.last_w = op
            w.readers = {}
        seen = set()
        if self.bar_op is not None:
            deps.append(self.bar_op)
        op.epoch = self.epoch
        for d in deps:
            if d is op or id(d) in seen:
                continue
            if d.idx < self.bar_idx and d is not self.bar_op:
                continue
            op.odeps.append(d)
            seen.add(id(d))
            if (not d.is_dma) and d.eng == "pe" and op.eng == "pe" and not op.is_dma:
                continue
            op.deps.append(d)
        op.idx = len(self.ops)
        self.ops.append(op)
        return op

    def call(self, eng, meth, *args, tag="", **kw):
        if eng == "pool" and meth == "tensor_scalar" and kw.get("scalar2", 0) is None:
            kw = dict(kw); kw["scalar2"] = 0.0; kw["op1"] = ALU.add
        reads, writes = [], []
        a2 = list(args)
        for i, a in enumerate(a2):
            if isinstance(a, View):
                (writes if (i == 0 and "out" not in kw) else reads).extend(a.res)
                a2[i] = a.ap
        k2 = dict(kw)
        for k, a in kw.items():
            if isinstance(a, View):
                (writes if k in ("out", "accum_out") else reads).extend(a.res)
                k2[k] = a.ap

        def fn(e, a2=a2, k2=k2, meth=meth):
            return getattr(e, meth)(*a2, **k2)

        op = Op(eng, fn, False, tag or meth)
        oap = k2.get("out", a2[0] if a2 else None)
        free = 1
        try:
            for d_ in oap.shape[1:]:
                free *= int(d_)
        except Exception:
            free = 64
        if eng == "pe":
            op.dur = 0.11 + free / 2400.0
        elif eng == "act":
            op.dur = 0.2 + free / 1200.0
        elif eng == "dve":
            op.dur = (0.1 + free * 0.0065) if meth == "reciprocal" else (0.08 + free / 960.0)
        else:
            op.dur = 0.25 + free / 500.0
        return self._record(op, reads, writes)

    def dma(self, eng, out, in_, final=False, **kw):
        reads = list(in_.res) if isinstance(in_, View) else []
        writes = list(out.res) if isinstance(out, View) else []
        oap = out.ap if isinstance(out, View) else out
        iap = in_.ap if isinstance(in_, View) else in_

        def fn(e, oap=oap, iap=iap, kw=kw):
            return e.dma_start(out=oap, in_=iap, **kw)

        op_ = Op(eng, fn, True, "dma")
        nbytes = 4
        try:
            for d_ in oap.shape:
                nbytes *= int(d_)
        except Exception:
            nbytes = 65536
        op_.dur = 2.0 + nbytes / 150000.0
        op = self._record(op_, reads, writes)
        if final:
            self.final_ops.append(op)
        return op

    def __getattr__(self, name):
        if name in ENGS:
            return _EngProxy(self, name)
        raise AttributeError(name)

    def schedule(self):
        import heapq
        W = int(os.environ.get("MK_WIN", "32"))
        XLAT, SLAT = 2.0, 0.15
        ops = self.ops
        per_eng = {e: [] for e in ENGS}
        epochs = {}
        for op in ops:
            epochs.setdefault(op.epoch, []).append(op)
        prev_sched = None
        for ep in sorted(epochs):
            eops = epochs[ep]
            bar = eops[0] if eops[0].tag == "barrier" else None
            if bar is not None and prev_sched is not None:
                last = {}
                dmas = []
                for e in ENGS:
                    for o in prev_sched[e]:
                        if o.is_dma:
                            dmas.append(o)
                        else:
                            last[e] = o
                bar.deps = list(last.values()) + dmas
                bar.odeps = []
            inset = set(id(o) for o in eops)
            for o in eops:
                o.succ = []
                o.fin = None
            for o in eops:
                o.odeps = [d for d in o.odeps if id(d) in inset]
                o.nun = len(o.odeps)
                for d in o.odeps:
                    d.succ.append(o)
            queues = {e: [o for o in eops if o.eng == e] for e in ENGS}
            heads = {e: 0 for e in ENGS}
            done = {e: [] for e in ENGS}
            free = {e: 0.0 for e in ENGS}
            sched = {e: [] for e in ENGS}
            scheduled = set()
            remaining = len(eops)

            def candidate(e):
                q = queues[e]
                i = heads[e]
                best = None
                cnt_ = 0
                n_ = len(q)
                while i < n_ and cnt_ < W:
                    o = q[i]
                    i += 1
                    if o.fin is not None:
                        continue
                    cnt_ += 1
                    if o.nun > 0:
                        continue
                    r = 0.0
                    for d in o.odeps:
                        t = d.fin + (SLAT if (d.eng == e and not d.is_dma) else XLAT)
                        if t > r:
                            r = t
                    st = r if r > free[e] else free[e]
                    if best is None or st < best[0] - 1e-9:
                        best = (st, o)
                        if st <= free[e] + 1e-9:
                            break
                return best
            cand = {e: candidate(e) for e in ENGS}
            while remaining > 0:
                be = None
                for e in ENGS:
                    c = cand[e]
                    if c is not None and (be is None or c[0] < cand[be][0]):
                        be = e
                assert be is not None, "scheduler deadlock"
                st, o = cand[be]
                issue = o.dur
                if o.is_dma:
                    issue = 1.0 if o.eng == "pool" else 0.3
                o.fin = st + o.dur
                free[be] = st + issue
                sched[be].append(o)
                remaining -= 1
                q = queues[be]
                while heads[be] < len(q) and q[heads[be]].fin is not None:
                    heads[be] += 1
                dirty = {be}
                for s_ in o.succ:
                    s_.nun -= 1
                    if s_.nun == 0:
                        dirty.add(s_.eng)
                for e in dirty:
                    cand[e] = candidate(e)
            for e in ENGS:
                per_eng[e].extend(sched[e])
            prev_sched = sched
            self.sim_time = getattr(self, "sim_time", 0.0) + max(free.values())
        return per_eng

    def finish(self):
        nc = self.nc
        ops = self.ops
        do_sched = int(os.environ.get("MK_SCHED", "1"))
        per_eng_s = self.schedule() if do_sched else None
        for op in ops:
            for d in op.deps:
                d.signal = True
        for op in self.final_ops:
            op.signal = True
        cnt = {}
        dcount = {e: 0 for e in ENGS}
        per_eng = {e: [] for e in ENGS}
        if per_eng_s is None:
            for op in ops:
                per_eng[op.eng].append(op)
        else:
            per_eng = per_eng_s
        for op in [o for e in ENGS for o in per_eng[e]]:
            if op.is_dma:
                k = dcount[op.eng]
                dcount[op.eng] += 1
                op.dsem = (op.eng, k % N_DMA_SEMS)
                op.dval = 16 * (k // N_DMA_SEMS + 1)
            elif op.signal:
                ke = (op.epoch, op.eng)
                cnt[ke] = cnt.get(ke, 0) + 1
                op.cnt = cnt[ke]
        es = self.es
        assert max(cnt.values()) < 60000, max(cnt.values())
        esem = {ke: es.enter_context(nc.semaphore(f"s_{ke[1]}_{ke[0]}")) for ke in cnt}
        dsem = {}
        for e in ENGS:
            if dcount[e]:
                for i in range(min(N_DMA_SEMS, dcount[e])):
                    dsem[(e, i)] = es.enter_context(nc.semaphore(f"d_{e}_{i}"))
        block = es.enter_context(nc.Block())
        engmap = {"pe": "tensor", "act": "scalar", "dve": "vector", "pool": "gpsimd", "sp": "sync"}
        final_ops = self.final_ops

        def emit(e_name):
            def body(eng):
                known = {}
                prev_dma = {}
                my_ops = per_eng[e_name]
                for op in my_ops:
                    waits = {}
                    for d in op.deps:
                        if d.is_dma:
                            key, val = ("d",) + d.dsem, d.dval
                        else:
                            key, val = ("e", d.epoch, d.eng), d.cnt
                        if waits.get(key, 0) < val:
                            waits[key] = val
                    if op.is_dma:
                        if op.dval > 16:
                            key = ("d",) + op.dsem
                            if waits.get(key, 0) < op.dval - 16:
                                waits[key] = op.dval - 16
                    for key, val in waits.items():
                        if known.get(key, 0) >= val:
                            continue
                        known[key] = val
                        sem = dsem[key[1:]] if key[0] == "d" else esem[key[1:]]
                        eng.wait_ge(sem, val)
                    ins = op.fn(eng)
                    if op.is_dma:
                        ins.then_inc(dsem[op.dsem], 16)
                    elif op.signal:
                        ins.then_inc(esem[(op.epoch, e_name)], 1)
                if e_name == "sp":
                    waits = {}
                    for op in final_ops:
                        key = ("d",) + op.dsem
                        waits[key] = max(waits.get(key, 0), op.dval)
                    for key, val in waits.items():
                        if known.get(key, 0) < val:
                            eng.wait_ge(dsem[key[1:]], val)
            return body

        for e in ENGS:
            if per_eng[e] or e == "sp":
                getattr(block, engmap[e])(emit(e))
        self.es.close()
        return {e: (len(per_eng[e]), max([v for k, v in cnt.items() if k[1] == e] + [0]), dcount[e]) for e in ENGS}, len(esem)


class _EngProxy:
    def __init__(self, prog, eng):
        self.prog = prog
        self.eng = eng

    def __getattr__(self, meth):
        if meth == "dma_start":
            def f(out, in_, **kw):
                return self.prog.dma(self.eng, out, in_, **kw)
            return f

        def f(*a, **kw):
            return self.prog.call(self.eng, meth, *a, **kw)
        return f


NCORES = 8
D = 1024; KC = 8; DFF = 2816; FC = 22; DEPTH = 4
SEQ = 2048; DSEQ = 32; PAST = 1024; NB = 4
TT = 256
EPS = 1e-6
NLAYERS = int(os.environ.get("MK_LAYERS", "4"))
DO_MIX = int(os.environ.get("MK_MIX", "1"))
MK_NT = int(os.environ.get("MK_NT", str(SEQ // TT)))
MK_NSEQ = int(os.environ.get("MK_NSEQ", "4"))
MK_CORES = int(os.environ.get("MK_CORES", "8"))
ODD_STAGE = int(os.environ.get("MK_ODD", "9"))
SUB = int(os.environ.get("MK_SUB", "9"))


class VecPack:
    def __init__(self):
        self.cols = []
        self.off = {}
        self.n = 0

    def put(self, name, arr):
        arr = np.ascontiguousarray(arr, dtype=np.float32)
        assert arr.shape[0] == 128
        self.off[name] = (self.n, arr.shape[1])
        self.cols.append(arr)
        self.n += arr.shape[1]

    def put_feat(self, name, v):
        v = np.asarray(v, dtype=np.float32)
        self.put(name, v.reshape(-1, 128).T)

    def array(self):
        return np.concatenate(self.cols, axis=1)


def vec_layout(inputs=None):
    z = lambda *s: np.zeros(s, np.float32)
    g = (lambda k: np.asarray(inputs[k], np.float32)) if inputs is not None else None
    vp = VecPack()
    for l in range(DEPTH):
        vp.put_feat(f"gmix{l}", g("norm_mix_g")[l] if g else z(D))
        vp.put_feat(f"gffn{l}", g("norm_ffn_g")[l] if g else z(D))
        vp.put_feat(f"adab{l}", g("ada_b")[l] if g else z(6 * D))
        for i in range(3):
            vp.put_feat(f"fcw{l}_{i}", g("ffn_conv_w")[l, i] if g else z(DFF))
    for i in range(2):
        vp.put_feat(f"gqlat{i}", g("mla_g_qlat")[i] if g else z(256))
        vp.put_feat(f"gkvlat{i}", g("mla_g_kvlat")[i] if g else z(128))
        gq = z(128, 1); gk = z(128, 1); gkr = z(128, 1)
        if g:
            gq[0:64, 0] = g("mla_g_qn")[i]; gq[64:96, 0] = g("mla_g_qr")[i]
            gk[0:64, 0] = g("mla_g_kn")[i]; gk[64:128, 0] = g("mla_g_kn")[i]
            gkr[64:96, 0] = g("mla_g_kr")[i]
        vp.put(f"gq{i}", gq); vp.put(f"gk{i}", gk); vp.put(f"gkr{i}", gkr)
        for nm, key, nn in (("mu", "rwkv_mu", 1792), ("w0", "rwkv_w0", 512), ("a0", "rwkv_a0", 512), ("kk", "rwkv_k_k", 512),
                            ("ka", "rwkv_k_a", 512), ("lng", "rwkv_lnx_g", 512), ("lnb", "rwkv_lnx_b", 512)):
            vp.put_feat(f"{nm}{i}", g(key)[i] if g else z(nn))
        vp.put_feat(f"rk{i}", g("rwkv_r_k")[i].reshape(-1) if g else z(512))
    for i in range(2):
        vp.put_feat(f"pscale{i}", g("pool_scale")[i] if g else z(512))
        for t in range(4):
            vp.put_feat(f"gcw{i}_{t}", g("gdn_conv_w")[i, t] if g else z(1536))
        vp.put_feat(f"gog{i}", g("gdn_o_g")[i] if g else z(128))
        dtb = z(128, 1); alog = z(128, 1)
        if g:
            dtb[4:8, 0] = g("gdn_dt_bias")[i]; alog[4:8, 0] = g("gdn_a_log")[i]
        vp.put(f"dtb{i}", dtb); vp.put(f"alog{i}", alog)
    return vp


NPOS = SEQ + NB * DSEQ
CST = {}


def const_array():
    CST.clear()
    cols = []
    off = 0

    def put(name, a):
        nonlocal off
        a = np.ascontiguousarray(a, np.float32)
        CST[name] = (off, a.shape[1]); cols.append(a); off += a.shape[1]
    pos = np.concatenate([np.arange(SEQ)] + [PAST + np.arange(DSEQ)] * NB).astype(np.float32)
    inv = (1.0 / (10000.0 ** (np.arange(0, 32, 2, dtype=np.float32) / 32))).astype(np.float32)
    ang = pos[None, :] * inv[:, None]
    C = np.zeros((128, NPOS), np.float32); S = np.zeros((128, NPOS), np.float32)
    C[0:64] = 1.0
    C[64:80] = np.cos(ang); C[80:96] = np.cos(ang)
    S[64:80] = np.sin(ang); S[80:96] = np.sin(ang)
    put("C", C); put("S", S)
    return np.concatenate(cols, axis=1)


CST2 = {}


def const_array2():
    CST2.clear()
    cols = []
    off = 0

    def put(name, a):
        nonlocal off
        a = np.ascontiguousarray(a, np.float32)
        assert a.shape[0] == 128
        CST2[name] = (off, a.shape[1]); cols.append(a); off += a.shape[1]
    Pm = np.zeros((128, 128), np.float32)
    for i in range(16):
        Pm[80 + i, 64 + i] = -1.0
        Pm[64 + i, 80 + i] = 1.0
    put("Pm", Pm)
    BO96 = np.zeros((128, 128), np.float32); BO96[0:64, 0:64] = 1 / 64; BO96[64:96, 64:96] = 1 / 32
    put("BO96", BO96)
    BO64 = np.zeros((128, 128), np.float32); BO64[0:64, 0:64] = 1 / 64; BO64[64:128, 64:128] = 1 / 64
    put("BO64", BO64)
    put("I128", np.eye(128, dtype=np.float32))
    for Cc in (64, 32):
        n2 = 2 * Cc
        blk = (np.arange(n2)[:, None] // Cc) == (np.arange(n2)[None, :] // Cc)
        jj = np.arange(n2)[:, None] % Cc; ii = np.arange(n2)[None, :] % Cc
        MS = (blk & (ii > jj)).astype(np.float32); MI = (blk & (ii >= jj)).astype(np.float32)
        m = np.zeros((128, 2 * n2), np.float32); m[0:n2, 0:n2] = MS; m[0:n2, n2:2 * n2] = MI
        put(f"MSI{Cc}", m)
        mneg = np.zeros((128, 2 * n2), np.float32); mneg[0:n2, 0:n2] = -MS; mneg[0:n2, n2:2 * n2] = MI
        put(f"MSIN{Cc}", mneg)
        tri = np.zeros((128, n2), np.float32); tri[0:n2] = (blk & (jj <= ii)).astype(np.float32)
        put(f"TRI{Cc}", tri)
        ob = np.zeros((128, n2), np.float32); ob[0:n2] = blk.astype(np.float32)
        put(f"ONB{Cc}", ob)
        put(f"NONB{Cc}", -ob)
        lv = 0
        while (1 << lv) < Cc:
            half = 1 << lv; full = 2 * half
            I_ = np.arange(n2)[:, None]; J_ = np.arange(n2)[None, :]
            mk_ = ((I_ // full) == (J_ // full)) & ((I_ % full) >= half) & ((J_ % full) < half)
            mm_ = np.zeros((128, n2), np.float32); mm_[0:n2] = mk_.astype(np.float32)
            put(f"MK{Cc}_{lv}", mm_)
            lv += 1
        stri = np.zeros((128, n2), np.float32); stri[0:n2] = (blk & (jj > ii)).astype(np.float32)
        put(f"STRI{Cc}", stri)
        for hh in range(2):
            sl = np.zeros((128, 128), np.float32); sl[hh * Cc:(hh + 1) * Cc, :] = 1.0
            put(f"BLK{Cc}_{hh}", sl)
        r = np.ones((128, NB * Cc), np.float32); r[:, ::Cc] = 0.0
        put(f"R{Cc}", r)
    ict = np.zeros((128, 4 * 15), np.float32)
    for gi, w in enumerate((2, 4, 8, 16)):
        ict[:, gi * 15:(gi + 1) * 15] = 1.0 / np.minimum(w, np.arange(15) + 1)
    put("ICT", ict)
    oh = np.zeros((128, 8), np.float32); oh[0:8, 0:8] = np.eye(8)
    put("OH8", oh)
    for h in range(4):
        sl = np.zeros((128, 128), np.float32); sl[h, :] = 1.0
        put(f"SEL8_{h}", sl)
    return np.concatenate(cols, axis=1)


class TokTile:
    def __init__(self, nseg, L, rows, seqs, t0, first, last, sample):
        self.nseg, self.L, self.rows, self.seqs, self.t0 = nseg, L, rows, seqs, t0
        self.TT = nseg * L
        self.first, self.last, self.sample = first, last, sample
        self.res = None


def make_tiles():
    tiles = []
    for s in range(MK_NSEQ):
        nt = MK_NT
        for j in range(nt):
            tiles.append(TokTile(1, TT, [s], [s], j * TT, j == 0, j == nt - 1, False))
    tiles.append(TokTile(NB, DSEQ, [4, 5, 6, 7], [0, 1, 2, 3], 0, True, True, True))
    return tiles


def build_program():
    nc = bass.Bass("TRN2", target_bir_lowering=False)
    P = Prog(nc)
    vl = vec_layout(None)
    NV = vl.n

    def din(name, shape, dt=F32):
        return nc.dram_tensor(name, list(shape), dt, kind="ExternalInput").ap()

    def dout(name, shape, dt=F32):
        return nc.dram_tensor(name, list(shape), dt, kind="ExternalOutput").ap()

    xp = din("xp", [NB, KC, 128, SEQ]); xs = din("xs", [NB, KC, 128, DSEQ])
    yp = dout("yp", [NB, KC, 128, SEQ]); ys = dout("ys", [NB, KC, 128, DSEQ])
    cT_d = din("cT", [128, KC, 8])
    vecs_d = din("vecs", [128, NV])
    ada_w = din("ada_w", [DEPTH, D, 6 * D])
    wg_d = din("ffn_w_gate", [DEPTH, D, DFF]); wu_d = din("ffn_w_up", [DEPTH, D, DFF])
    wd_d = din("ffn_w_down", [DEPTH, DFF, D])
    fh_d = din("ffn_hist", [DEPTH, 128, FC, NB, 2])
    o_ffn_p = dout("o_ffn_p", [DEPTH, 128, FC, NB, 2]); o_ffn_s = dout("o_ffn_s", [DEPTH, 128, FC, NB, 2])

    cst_np = const_array(); cst2_np = const_array2()
    NCST = cst_np.shape[1]
    cst_d = din("consts", [128, NCST])
    win_e = din("even_w_in", [2, D, 2208]); wuq_d = din("mla_w_uq", [2, 256, 768]); wukv_d = din("mla_w_ukv", [2, 128, 1024])
    wout_e = din("even_w_out", [2, D, D])
    ckv_past = din("ckv_past", [2, NB, 128, PAST]); kpe_past = din("kpe_past", [2, NB, 32, PAST])
    o_ckv_p = dout("o_ckv_p", [2, NB, 128, SEQ]); o_kpe_p = dout("o_kpe_p", [2, NB, 32, SEQ])
    o_ckv_s = dout("o_ckv_s", [2, NB, 128, DSEQ]); o_kpe_s = dout("o_kpe_s", [2, NB, 32, DSEQ])
    cst2_np = const_array2()
    NCST2 = cst2_np.shape[1]
    cst2_d = din("consts2", [128, NCST2])
    w2_d = din("rwkv_w2", [2, 64, 512]); a2_d = din("rwkv_a2", [2, 64, 512]); g2_d = din("rwkv_g2", [2, 128, 512])
    shift_in = din("shift_in", [2, 128, 14, NB]); wkv_in = din("wkv_in", [2, NB, 4, 128, 64])
    o_shift_p = dout("o_shift_p", [2, 128, 14, NB]); o_shift_s = dout("o_shift_s", [2, 128, 14, NB])
    o_wkv_p = dout("o_wkv_p", [2, NB, 4, 128, 64]); o_wkv_s = dout("o_wkv_s", [2, NB, 4, 128, 64])
    win_o = din("odd_w_in", [2, D, 2568]); poolw_d = din("pool_w", [2, 4, 128, 128]); wout_o = din("odd_w_out", [2, D, D])
    pool_in = din("pool_in", [2, 128, 4, NB, 15]); gconv_in = din("gconv_in", [2, 128, 12, NB, 3]); gdn_in = din("gdn_in", [2, NB, 4, 128, 128])
    o_pool_p = dout("o_pool_p", [2, 128, 4, NB, 15]); o_pool_s = dout("o_pool_s", [2, 128, 4, NB, 15])
    o_gconv_p = dout("o_gconv_p", [2, 128, 12, NB, 3]); o_gconv_s = dout("o_gconv_s", [2, 128, 12, NB, 3])
    o_gdn_p = dout("o_gdn_p", [2, NB, 4, 128, 128]); o_gdn_s = dout("o_gdn_s", [2, NB, 4, 128, 128])
    tiles = make_tiles()
    TTF = 512
    omla_d = nc.dram_tensor("omla_scr", [len(tiles), 128, 4, TT], BF16, kind="Internal").ap()
    omla_res = [Res(f"omla{t}") for t in range(len(tiles))]
    xres = [Res(f"x{t}") for t in range(len(tiles))]
    for t_, tk_ in enumerate(tiles):
        tk_.res = [xres[t_]]
    ftiles = []
    if MK_NT % 2 == 0:
        for t_ in range(0, len(tiles) - 1, 2):
            a_, b_ = tiles[t_], tiles[t_ + 1]
            ft = TokTile(1, 2 * TT, a_.rows, a_.seqs, a_.t0, a_.first, b_.last, False)
            ft.res = [xres[t_], xres[t_ + 1]]
            ftiles.append(ft)
        ftiles.append(tiles[-1])
    else:
        ftiles = tiles

    def x_dram(ti, tk, seg, src_first):
        s = tk.seqs[seg]
        if tk.sample:
            base = xs if src_first else ys
            ap = base[s].rearrange("k p t -> p k t")
        else:
            base = xp if src_first else yp
            ap = base[s][:, :, tk.t0:tk.t0 + tk.L].rearrange("k p t -> p k t")
        return View(ap, tk.res)

    def y_dram(ti, tk, seg):
        s = tk.seqs[seg]
        if tk.sample:
            ap = ys[s].rearrange("k p t -> p k t")
        else:
            ap = yp[s][:, :, tk.t0:tk.t0 + tk.L].rearrange("k p t -> p k t")
        return View(ap, tk.res)

    vecs = P.sbuf("vecs", [128, NV])
    P.sp.dma_start(vecs, vecs_d)

    def V(name, c0=0, n=None):
        o, w = vl.off[name]
        n = w - c0 if n is None else n
        return vecs[:, o + c0:o + c0 + n]

    CH = {}

    def load_consts(names):
        names = ["Pm", "BO96", "BO64", "I128"] + [n_ for n_ in names if n_ not in ("Pm", "BO96", "BO64", "I128")]
        offs = {}
        tot = 0
        for n_ in names:
            offs[n_] = (tot, CST2[n_][1]); tot += CST2[n_][1]
        t_ = P.sbuf("cst2", [128, tot])
        for n_ in names:
            o_, w_ = CST2[n_]
            P.sp.dma_start(t_[:, offs[n_][0]:offs[n_][0] + w_], cst2_d[:, o_:o_ + w_])
        CH["cst2"] = t_; CH["offs"] = offs
        cb_ = P.sbuf("cb", [128, 4 * 128], BF16)
        for j, nm in enumerate(("Pm", "BO96", "BO64", "I128")):
            P.dve.tensor_copy(out=cb_[:, j * 128:(j + 1) * 128], in_=C2(nm))
        return cb_[:, 0:128], cb_[:, 128:256], cb_[:, 256:384], cb_[:, 384:512]

    def C2(name, c0=0, n=None):
        o, w = CH["offs"][name]
        n = w - c0 if n is None else n
        return CH["cst2"][:, o + c0:o + c0 + n]
    ones_bf = P.sbuf("ones_bf", [128, 128], BF16)
    P.dve.memset(ones_bf, 1.0 / D)
    eps_t = P.sbuf("eps_t", [128, 1]); P.dve.memset(eps_t, EPS)
    psb = [P.psum(f"psb{i}", [128, 512]) for i in range(8)]
    for pt_ in psb:
        P.dve.memset(pt_, 0.0)

    class Rot:
        def __init__(self, items):
            self.items, self.i = items, 0

        def next(self):
            it = self.items[self.i % len(self.items)]
            self.i += 1
            return it

    cT = P.sbuf("cT", [128, KC, 8]); P.sp.dma_start(cT, cT_d)
    cs = P.sbuf("cs", [128, KC, 8], BF16)
    P.act.activation(out=cs, in_=cT, func=AF.Silu)
    mod = [P.sbuf(f"mod{l}", [128, 48, 8]) for l in range(DEPTH)]
    modA_m = [P.sbuf(f"modAm{l}", [128, KC, 8]) for l in range(DEPTH)]
    modA_f = [P.sbuf(f"modAf{l}", [128, KC, 8]) for l in range(DEPTH)]
    with P.scope():
        wa_bufs = Rot([P.sbuf(f"wa{i}", [128, KC, 1024], BF16) for i in range(2)])
        for l in range(NLAYERS):
            pm = psb[l % 2]
            for fg in range(6):
                wa = wa_bufs.next()
                P.pool.dma_start(wa, ada_w[l][:, fg * 1024:(fg + 1) * 1024].rearrange("(k p) f -> p k f", p=128))
                for cc in range(8):
                    c = fg * 8 + cc
                    for k in range(KC):
                        P.pe.matmul(out=pm[:, c * 8:(c + 1) * 8], lhsT=wa[:, k, cc * 128:(cc + 1) * 128],
                                    rhs=cs[:, k, :], start=(k == 0), stop=(k == KC - 1))
            P.dve.tensor_tensor(out=mod[l], in0=pm[:, 0:384].rearrange("p (c b) -> p c b", b=8),
                                in1=V(f"adab{l}").bc3(2, 8), op=ALU.add)
            for (A, gname, c0) in ((modA_m[l], f"gmix{l}", 8), (modA_f[l], f"gffn{l}", 32)):
                P.dve.tensor_scalar(out=A, in0=mod[l][:, c0:c0 + 8, :], scalar1=1.0, scalar2=None, op0=ALU.add)
                P.dve.tensor_tensor(out=A, in0=A, in1=V(gname).bc3(2, 8), op=ALU.mult)

    x_raw = P.sbuf("x_t", [128, KC, TTF])
    h_raw = P.sbuf("h_t", [128, KC, TTF], BF16)
    r_raw = P.sbuf("rstd_t", [128, TTF])
    XB = {}
    for nm_, raw_ in (("x", x_raw), ("h", h_raw), ("r", r_raw)):
        rr_ = [Res(nm_ + "_h0"), Res(nm_ + "_h1")]
        if nm_ == "r":
            XB[nm_] = {"full": View(raw_.ap, rr_), 0: View(raw_.ap[:, 0:TT], rr_[0]), 1: View(raw_.ap[:, TT:2 * TT], rr_[1])}
        else:
            XB[nm_] = {"full": View(raw_.ap, rr_), 0: View(raw_.ap[:, :, 0:TT], rr_[0]), 1: View(raw_.ap[:, :, TT:2 * TT], rr_[1])}
    XH = {}

    def use_buf(j):
        for nm_ in ("x", "h", "r"):
            XH[nm_] = XB[nm_][j]
    use_buf("full")
    sqk = Rot([P.sbuf(f"sqk{j}", [128, TTF], BF16) for j in range(2)])
    tmk = Rot([P.sbuf(f"tmk{j}", [128, TTF]) for j in range(2)])
    ps_stat = psb[7]

    def modulate(tk, A, Bsh):
        n = tk.TT
        for k in range(KC):
            sq = sqk.next()
            P.act.activation(out=sq[:, 0:n], in_=XH["x"][:, k, 0:n], func=AF.Square)
            P.pe.matmul(out=ps_stat[:, 0:n], lhsT=ones_bf, rhs=sq[:, 0:n], start=(k == 0), stop=(k == KC - 1))
        P.act.activation(out=XH["r"][:, 0:n], in_=ps_stat[:, 0:n], func=AF.Sqrt, bias=eps_t, scale=1.0)
        P.dve.reciprocal(out=XH["r"][:, 0:n], in_=XH["r"][:, 0:n])
        for k in range(KC):
            for si, row in enumerate(tk.rows):
                sl = slice(si * tk.L, (si + 1) * tk.L)
                t_ = tmk.next()
                P.dve.scalar_tensor_tensor(out=t_[:, sl], in0=XH["x"][:, k, sl], scalar=A[:, k, row:row + 1], in1=XH["r"][:, sl], op0=ALU.mult, op1=ALU.mult)
                P.act.activation(out=XH["h"][:, k, sl], in_=t_[:, sl], func=AF.Identity, bias=Bsh[:, k, row:row + 1], scale=1.0)

    def load_x(ti, tk, first_layer):
        for si in range(tk.nseg):
            P.sp.dma_start(XH["x"][:, :, si * tk.L:(si + 1) * tk.L], x_dram(ti, tk, si, first_layer))

    def store_x(ti, tk, final):
        for si in range(tk.nseg):
            P.sp.dma_start(y_dram(ti, tk, si), XH["x"][:, :, si * tk.L:(si + 1) * tk.L], final=final)

    for l in range(NLAYERS):

        if DO_MIX and l % 2 == 0:
            i = l // 2
            P.barrier()
            with P.scope():
                Pm_b, BO96_b, BO64_b, Ib = load_consts([])
                cst = P.sbuf("cst", [128, NCST]); P.sp.dma_start(cst, cst_d)

                def CS(name):
                    o, w = CST[name]
                    return cst[:, o:o + w]
                ones1 = P.sbuf("ones1", [128, 64], BF16); P.dve.memset(ones1, 1.0)
                ones128 = P.sbuf("ones128", [128, 128], BF16); P.dve.memset(ones128, 1.0 / 128)
                win = P.sbuf("win", [128, KC, 2208], BF16)
                for k in range(KC):
                    P.pool.dma_start(win[:, k, :], win_e[i][k * 128:(k + 1) * 128, :])
                wuq = P.sbuf("wuq", [128, 2, 768], BF16)
                for k in range(2):
                    P.pool.dma_start(wuq[:, k, :], wuq_d[i][k * 128:(k + 1) * 128, :])
                wukv = P.sbuf("wukv", [128, 1024], BF16); P.pool.dma_start(wukv, wukv_d[i])
                wk_c = P.sbuf("wk_c", [128, 8, 64], BF16); wv_c = P.sbuf("wv_c", [128, 8, 64], BF16)
                wukv3 = wukv.rearrange("p (h c) -> p h c", h=8)
                P.dve.tensor_copy(out=wk_c, in_=wukv3[:, :, 0:64]); P.dve.tensor_copy(out=wv_c, in_=wukv3[:, :, 64:128])
                PGq = P.sbuf("PGq", [128, 128], BF16)
                P.dve.tensor_scalar(out=PGq[0:96, 0:96], in0=C2("Pm")[0:96, 0:96], scalar1=V(f"gq{i}")[0:96, :], scalar2=None, op0=ALU.mult)
                PGk = P.sbuf("PGk", [128, 128], BF16)
                P.dve.memset(PGk, 0.0)
                Ctab = CS("C")
                P.dve.tensor_scalar(out=PGk[64:96, 0:96], in0=C2("Pm")[64:96, 0:96], scalar1=V(f"gkr{i}")[64:96, :], scalar2=None, op0=ALU.mult)
                Stab = CS("S")
                NKT = SEQ // 128
                KT = P.sbuf("KT", [96, 8, SEQ], BF16)
                Vt = P.sbuf("Vt", [128, NKT, 8, 64], BF16)
                mixT = P.sbuf("mixT", [128, 4, TT], BF16)
                ckv_f = P.sbuf("ckv_f", [128, TT]); ckv_b = P.sbuf("ckv_b", [128, TT], BF16)
                kpe_f = P.sbuf("kpe_f", [128, TT]); kpe_b = P.sbuf("kpe_b", [128, TT], BF16)
                qlat_b = P.sbuf("qlat_b", [128, 2, TT], BF16)
                sqa = Rot([P.sbuf(f"sqa{j}", [128, TT], BF16) for j in range(3)])
                tfa = Rot([P.sbuf(f"tfa{j}", [128, TT]) for j in range(6)])
                tfb = Rot([P.sbuf(f"tfb{j}", [128, TT]) for j in range(8)])
                qgb = Rot([P.sbuf(f"qgb{j}", [128, TT], BF16) for j in range(3)])
                Qf = P.sbuf("Qf", [96, 8, TT], BF16)
                PT = Rot([P.sbuf(f"PT{j}", [128, TT], BF16) for j in range(6)])
                rsum = Rot([P.sbuf(f"rsum{j}", [128, TT]) for j in range(2)])
                pastb = P.sbuf("pastb", [128, PAST], BF16); kpast = P.sbuf("kpast", [96, PAST], BF16)
                pa = Rot(psb[0:3]); pb = Rot(psb[3:5]); po = Rot(psb[5:7])

                def rstd_from(ps_view, out_view, r0=0, r1=128):
                    P.act.activation(out=out_view, in_=ps_view, func=AF.Sqrt, bias=eps_t[r0:r1, :], scale=1.0)
                    P.dve.reciprocal(out=out_view, in_=out_view)

                def knope_v(src_b, n, kbase, r0=0):
                    for hp in range(4):
                        kp = pa.next()
                        P.pe.matmul(out=kp[:, 0:n], lhsT=wk_c[:, 2 * hp:2 * hp + 2, :].rearrange("p h c -> p (h c)"), rhs=src_b[:, 0:n], start=True, stop=True)
                        sq = sqa.next()
                        P.act.activation(out=sq[:, 0:n], in_=kp[:, 0:n], func=AF.Square)
                        mp = pb.next()
                        P.pe.matmul(out=mp[:, 0:n], lhsT=BO64_b, rhs=sq[:, 0:n], start=True, stop=True)
                        rs = tfa.next()
                        rstd_from(mp[:, 0:n], rs[:, 0:n])
                        t = tfb.next()
                        P.dve.scalar_tensor_tensor(out=t[:, 0:n], in0=kp[:, 0:n], scalar=V(f"gk{i}"), in1=rs[:, 0:n], op0=ALU.mult, op1=ALU.mult)
                        P.pool.tensor_copy(out=KT[0:64, 2 * hp, kbase:kbase + n], in_=t[0:64, 0:n])
                        P.pool.tensor_copy(out=KT[0:64, 2 * hp + 1, kbase:kbase + n], in_=t[64:128, 0:n])
                    for j0 in range(0, n, 128):
                        m = min(128, n - j0)
                        vp_ = pa.next()
                        P.pe.matmul(out=vp_[0:m, 0:512], lhsT=src_b[:, j0:j0 + m], rhs=wv_c.rearrange("p h c -> p (h c)"), start=True, stop=True)
                        P.act.activation(out=Vt[0:m, (kbase + j0) // 128, :, :].rearrange("p h c -> p (h c)"), in_=vp_[0:m, 0:512], func=AF.Copy)

                def rope_norm(src_ps, r0, r1, n, CG, PG, pos0, out_view):
                    sq = sqa.next()
                    P.act.activation(out=sq[0:96, 0:n], in_=src_ps[0:96, 0:n], func=AF.Square)
                    mp = pb.next()
                    P.pe.matmul(out=mp[0:96, 0:n], lhsT=BO96_b[0:96, 0:96], rhs=sq[0:96, 0:n], start=True, stop=True)
                    rs = tfa.next()
                    rstd_from(mp[r0:r1, 0:n], rs[r0:r1, 0:n], r0, r1)
                    qg = qgb.next()
                    if r0 > 0:
                        P.pool.memset(qg[0:r0, 0:n], 0.0)
                    P.dve.tensor_copy(out=qg[r0:r1, 0:n], in_=src_ps[r0:r1, 0:n])
                    rp = pb.next()
                    P.pe.matmul(out=rp[0:96, 0:n], lhsT=PG[0:96, 0:96], rhs=qg[0:96, 0:n], start=True, stop=True)
                    t1 = tfb.next()
                    P.dve.scalar_tensor_tensor(out=t1[r0:r1, 0:n], in0=src_ps[r0:r1, 0:n], scalar=CG[r0:r1, :], in1=Ctab[r0:r1, pos0:pos0 + n], op0=ALU.mult, op1=ALU.mult)
                    t2 = tfb.next()
                    P.dve.tensor_tensor(out=t2[r0:r1, 0:n], in0=rp[r0:r1, 0:n], in1=Stab[r0:r1, pos0:pos0 + n], op=ALU.mult)
                    P.pool.tensor_tensor(out=t1[r0:r1, 0:n], in0=t1[r0:r1, 0:n], in1=t2[r0:r1, 0:n], op=ALU.add)
                    P.pool.tensor_tensor(out=out_view, in0=t1[r0:r1, 0:n], in1=rs[r0:r1, 0:n], op=ALU.mult)

                for ti, tk in enumerate(tiles):
                    use_buf(ti % 2)
                    n, L, ns = tk.TT, tk.L, tk.nseg
                    load_x(ti, tk, l == 0)
                    modulate(tk, modA_m[l], mod[l][:, 0:8, :])
                    pos0 = (SEQ if tk.sample else tk.t0)

                    def proj(c0, m):
                        ps = pa.next()
                        for k in range(KC):
                            P.pe.matmul(out=ps[0:m, 0:n], lhsT=win[:, k, c0:c0 + m], rhs=XH["h"][:, k, 0:n], start=(k == 0), stop=(k == KC - 1))
                        return ps
                    qps = [proj(0, 128), proj(128, 128)]
                    mp = pb.next()
                    for c in range(2):
                        sq = sqa.next()
                        P.act.activation(out=sq[:, 0:n], in_=qps[c][:, 0:n], func=AF.Square)
                        P.pe.matmul(out=mp[:, 0:n], lhsT=ones128, rhs=sq[:, 0:n], start=(c == 0), stop=(c == 1))
                    rs = tfa.next()
                    P.act.activation(out=rs[:, 0:n], in_=mp[:, 0:n], func=AF.Sqrt, bias=eps_t, scale=0.5)
                    P.dve.reciprocal(out=rs[:, 0:n], in_=rs[:, 0:n])
                    for c in range(2):
                        P.dve.scalar_tensor_tensor(out=qlat_b[:, c, 0:n], in0=qps[c][:, 0:n], scalar=V(f"gqlat{i}", c, 1), in1=rs[:, 0:n], op0=ALU.mult, op1=ALU.mult)
                    kvp = proj(256, 128)
                    sq = sqa.next()
                    P.act.activation(out=sq[:, 0:n], in_=kvp[:, 0:n], func=AF.Square)
                    mp = pb.next()
                    P.pe.matmul(out=mp[:, 0:n], lhsT=ones128, rhs=sq[:, 0:n], start=True, stop=True)
                    rs = tfa.next()
                    rstd_from(mp[:, 0:n], rs[:, 0:n])
                    P.dve.scalar_tensor_tensor(out=ckv_f[:, 0:n], in0=kvp[:, 0:n], scalar=V(f"gkvlat{i}"), in1=rs[:, 0:n], op0=ALU.mult, op1=ALU.mult)
                    P.act.activation(out=ckv_b[:, 0:n], in_=ckv_f[:, 0:n], func=AF.Copy)
                    krp = proj(320, 96)
                    rope_norm(krp, 64, 96, n, V(f"gkr{i}"), PGk, pos0, kpe_f[64:96, 0:n])
                    P.act.activation(out=kpe_b[64:96, 0:n], in_=kpe_f[64:96, 0:n], func=AF.Copy)
                    for si in range(ns):
                        sidx = tk.seqs[si]
                        sl = slice(si * L, (si + 1) * L)
                        if tk.sample:
                            P.sp.dma_start(View(o_ckv_s[i, sidx], Res("o")), ckv_f[:, sl], final=True)
                            P.sp.dma_start(View(o_kpe_s[i, sidx], Res("o")), kpe_f[64:96, sl], final=True)
                        else:
                            P.sp.dma_start(View(o_ckv_p[i, sidx][:, tk.t0:tk.t0 + L], Res("o")), ckv_f[:, sl], final=True)
                            P.sp.dma_start(View(o_kpe_p[i, sidx][:, tk.t0:tk.t0 + L], Res("o")), kpe_f[64:96, sl], final=True)
                    for h in range(8):
                        qp = pa.next()
                        for c in range(2):
                            P.pe.matmul(out=qp[0:96, 0:n], lhsT=wuq[:, c, h * 96:(h + 1) * 96], rhs=qlat_b[:, c, 0:n], start=(c == 0), stop=(c == 1))
                        rope_norm(qp, 0, 96, n, V(f"gq{i}"), PGq, pos0, Qf[0:96, h, 0:n])
                    for si in range(ns):
                        sidx = tk.seqs[si]
                        qsl = slice(si * L, (si + 1) * L)
                        if tk.sample:
                            P.pool.dma_start(pastb, ckv_past[i, sidx])
                            P.pool.dma_start(kpast[64:96, :], kpe_past[i, sidx])
                            for j0 in range(0, PAST, TT):
                                knope_v(pastb[:, j0:j0 + TT], TT, j0)
                            P.pool.tensor_copy(out=KT[64:96, :, 0:PAST], in_=kpast[64:96, :].bc3(1, 8))
                            kb = PAST
                        else:
                            kb = tk.t0
                        knope_v(ckv_b[:, qsl], L, kb)
                        P.pool.tensor_copy(out=KT[64:96, :, kb:kb + L], in_=kpe_b[64:96, qsl].bc3(1, 8))
                        kts = []
                        if tk.sample:
                            kts = [(j0, 128, 0, False) for j0 in range(0, PAST, 128)] + [(PAST, L, 0, False)]
                        else:
                            kts = [(j0, 128, 0, False) for j0 in range(0, tk.t0, 128)]
                            for d in range(L // 128):
                                kts.append((tk.t0 + d * 128, 128, d * 128, True))
                        for h in range(8):
                            par = h % 2
                            op_ = po.next()
                            orow = slice(par * 64, par * 64 + 64); srow = slice((1 - par) * 64, (1 - par) * 64 + 64)
                            for kidx, (k0, ksz, q0, msk) in enumerate(kts):
                                nq = L - q0
                                sp_ = pa.next()
                                P.pe.matmul(out=sp_[0:ksz, 0:nq], lhsT=KT[0:96, h, k0:k0 + ksz], rhs=Qf[0:96, h, si * L + q0:(si + 1) * L], start=True, stop=True)
                                pt = PT.next()
                                P.act.activation(out=pt[0:ksz, 0:nq], in_=sp_[0:ksz, 0:nq], func=AF.Exp, scale=float(96 ** -0.5))
                                if msk:
                                    P.pool.memset(pt[64:128, 0:64], 0.0)
                                first = kidx == 0
                                last = kidx == len(kts) - 1
                                P.pe.matmul(out=op_[orow, q0:L], lhsT=Vt[0:ksz, k0 // 128, h, :], rhs=pt[0:ksz, 0:nq], start=first, stop=last, skip_group_check=True)
                                P.pe.matmul(out=op_[srow, q0:L], lhsT=ones1[0:ksz, :], rhs=pt[0:ksz, 0:nq], start=first, stop=last, skip_group_check=True)
                            rsm = rsum.next()
                            P.dve.reciprocal(out=rsm[srow, 0:L], in_=op_[srow, 0:L])
                            P.dve.tensor_tensor(out=mixT[orow, h // 2, qsl], in0=op_[orow, 0:L], in1=rsm[srow, 0:L], op=ALU.mult)
                    P.sp.dma_start(View(omla_d[ti][:, :, 0:n], omla_res[ti]), mixT[:, 0:4, 0:n])
            with P.scope():
                Pm_b, BO96_b, BO64_b, Ib = load_consts(["MSI64", "MSI32", "R64", "R32"])
                win_r = P.sbuf("win_r", [128, KC, 1792], BF16)
                for k in range(KC):
                    P.pool.dma_start(win_r[:, k, :], win_e[i][k * 128:(k + 1) * 128, 416:2208])
                w2 = P.sbuf("w2", [128, 512], BF16); P.pool.dma_start(w2[0:64, :], w2_d[i])
                wa2 = P.sbuf("wa2", [128, 512], BF16); P.pool.dma_start(wa2[64:128, :], a2_d[i])
                g2 = P.sbuf("g2", [128, 512], BF16); P.pool.dma_start(g2, g2_d[i])
                wout = P.sbuf("wout", [128, KC, D], BF16)
                for k in range(KC):
                    P.pool.dma_start(wout[:, k, :], wout_e[i][k * 128:(k + 1) * 128, :])
                omka = P.sbuf("omka", [128, 4])
                P.dve.tensor_scalar(out=omka, in0=V(f"ka{i}"), scalar1=-1.0, scalar2=1.0, op0=ALU.mult, op1=ALU.add)
                lneps = P.sbuf("lneps", [128, 1]); P.dve.memset(lneps, 64e-5)
                shift_p = P.sbuf("shift_p", [128, 14, NB]); shift_s = P.sbuf("shift_s", [128, 14, NB])
                P.dve.memset(shift_p, 0.0); P.sp.dma_start(shift_s, shift_in[i])
                Hf = {}; Hb = {}
                for grp in ("p", "s"):
                    for sq_ in range(NB):
                        for hp in range(4):
                            Hf[grp, sq_, hp] = P.sbuf(f"Hf{grp}{sq_}{hp}", [128, 64])
                            Hb[grp, sq_, hp] = P.sbuf(f"Hb{grp}{sq_}{hp}", [128, 64], BF16)
                            if grp == "p":
                                P.pool.memset(Hf[grp, sq_, hp], 0.0)
                            else:
                                P.sp.dma_start(Hf[grp, sq_, hp], wkv_in[i, sq_, hp])
                            P.act.activation(out=Hb[grp, sq_, hp], in_=Hf[grp, sq_, hp], func=AF.Copy)
                rwb = Rot([P.sbuf(f"rwb{j}", [128, TT + NB]) for j in range(2)]); xm_t = P.sbuf("xm_t", [128, 14, TT])
                g_t = P.sbuf("g_t", [128, 4, TT])
                eG_t = P.sbuf("eG_t", [128, 4, TT])
                bonus_t = P.sbuf("bonus_t", [128, 4, TT])
                yT_t = P.sbuf("yT_t", [128, 4, TT])
                AR = P.sbuf("AR", [128, 4, 2 * TT], BF16); BK = P.sbuf("BK", [128, 4, 2 * TT], BF16)
                v_b = P.sbuf("v_b", [128, 4, TT], BF16)
                th_b = P.sbuf("th_b", [128, TT], BF16); da_b = P.sbuf("da_b", [128, TT], BF16); sg_b = P.sbuf("sg_b", [128, TT], BF16)
                mixT = P.sbuf("mixTB", [128, KC, TT], BF16)
                tf = Rot([P.sbuf(f"tf{j}", [128, TT]) for j in range(6)])
                tg = [P.sbuf(f"tg{j}", [128, TT]) for j in range(8)]
                tb = Rot([P.sbuf(f"tb{j}", [128, TT], BF16) for j in range(3)])
                pa = Rot(psb[0:7])
                def mk(name, shape, dt):
                    return [P.sbuf(f"{name}{hp}", shape, dt) for hp in range(4)]
                PB2 = []
                for par_ in range(2):
                    PB2.append(dict(tokm=mk(f"tokm{par_}_", [128, 192], BF16), NBs=mk(f"NBs{par_}_", [128, 256], BF16), NKs=mk(f"NKs{par_}_", [128, 256], BF16),
                                    Nm=[mk(f"Nm{par_}_{m}_", [128, 128], BF16) for m in range(6)],
                                    Xf=mk(f"Xf{par_}_", [128, 64], F32), Xb=mk(f"Xb{par_}_", [128, 64], BF16), Yb=mk(f"Yb{par_}_", [128, 64], BF16),
                                    tH=mk(f"tH{par_}_", [128, 64], F32)))
                Lp = [mk(f"Lp{m}_", [128, 128], BF16) for m in range(5)]

                for ti, tk in enumerate(tiles):
                    use_buf(ti % 2)
                    n, L, ns = tk.TT, tk.L, tk.nseg
                    grp = "s" if tk.sample else "p"
                    Cc = 32 if tk.sample else 64
                    nch = n // Cc
                    C2c = 2 * Cc
                    NM = 5 if tk.sample else 6
                    MSI = C2(f"MSI{Cc}"); Rm = C2(f"R{Cc}")
                    shiftst = shift_s if tk.sample else shift_p
                    load_x(ti, tk, l == 0)
                    modulate(tk, modA_m[l], mod[l][:, 0:8, :])
                    for j in range(14):
                        rwj = rwb.next()
                        rw3 = rwj[:, 0:ns * (L + 1)].rearrange("p (s t) -> p s t", s=ns)
                        for si in range(ns):
                            P.pool.tensor_copy(out=rw3[:, si, 0:1], in_=shiftst[:, j, tk.seqs[si]:tk.seqs[si] + 1])
                        ps = pa.next()
                        for k in range(KC):
                            P.pe.matmul(out=ps[:, 0:n], lhsT=win_r[:, k, j * 128:(j + 1) * 128], rhs=XH["h"][:, k, 0:n], start=(k == 0), stop=(k == KC - 1))
                        P.act.activation(out=rw3[:, :, 1:L + 1], in_=ps[:, 0:n].rearrange("p (s t) -> p s t", s=ns), func=AF.Copy)
                        for si in range(ns):
                            P.pool.tensor_copy(out=shiftst[:, j, tk.seqs[si]:tk.seqs[si] + 1], in_=rw3[:, si, L:L + 1])
                        d = tf.next()
                        d3 = d[:, 0:n].rearrange("p (s t) -> p s t", s=ns)
                        P.pool.tensor_tensor(out=d3, in0=rw3[:, :, 0:L], in1=rw3[:, :, 1:L + 1], op=ALU.subtract)
                        P.dve.scalar_tensor_tensor(out=xm_t[:, j, 0:n].rearrange("p (s t) -> p s t", s=ns), in0=d3, scalar=V(f"mu{i}", j, 1),
                                                   in1=rw3[:, :, 1:L + 1], op0=ALU.mult, op1=ALU.add)
                    P.act.activation(out=th_b[0:64, 0:n], in_=xm_t[0:64, 12, 0:n], func=AF.Tanh)
                    P.act.activation(out=da_b[64:128, 0:n], in_=xm_t[64:128, 12, 0:n], func=AF.Copy)
                    P.act.activation(out=sg_b[:, 0:n], in_=xm_t[:, 13, 0:n], func=AF.Sigmoid)
                    for c in range(4):
                        cs_ = slice(c * 128, (c + 1) * 128)
                        r_c, k_c, v_c = xm_t[:, c, 0:n], xm_t[:, 4 + c, 0:n], xm_t[:, 8 + c, 0:n]
                        ps = pa.next()
                        P.pe.matmul(out=ps[:, 0:n], lhsT=w2[0:64, cs_], rhs=th_b[0:64, 0:n], start=True, stop=True)
                        t0_ = tf.next()
                        P.act.activation(out=t0_[:, 0:n], in_=ps[:, 0:n], func=AF.Sigmoid, bias=V(f"w0{i}", c, 1), scale=1.0)
                        P.pool.tensor_scalar(out=tg[0][:, 0:n], in0=t0_[:, 0:n], scalar1=-0.6065306597126334, scalar2=None, op0=ALU.mult)
                        ps = pa.next()
                        P.pe.matmul(out=ps[:, 0:n], lhsT=wa2[64:128, cs_], rhs=da_b[64:128, 0:n], start=True, stop=True)
                        P.act.activation(out=tg[1][:, 0:n], in_=ps[:, 0:n], func=AF.Sigmoid, bias=V(f"a0{i}", c, 1), scale=1.0)
                        ps = pa.next()
                        P.pe.matmul(out=ps[:, 0:n], lhsT=g2[:, cs_], rhs=sg_b[:, 0:n], start=True, stop=True)
                        P.act.activation(out=g_t[:, c, 0:n], in_=ps[:, 0:n], func=AF.Copy)
                        t1 = tf.next()
                        P.pool.tensor_scalar(out=t1[:, 0:n], in0=k_c, scalar1=V(f"kk{i}", c, 1), scalar2=None, op0=ALU.mult)
                        sq = tb.next()
                        P.act.activation(out=sq[:, 0:n], in_=t1[:, 0:n], func=AF.Square)
                        ps = pa.next()
                        P.pe.matmul(out=ps[:, 0:n], lhsT=BO64_b, rhs=sq[:, 0:n], start=True, stop=True)
                        rs = tf.next()
                        P.act.activation(out=rs[:, 0:n], in_=ps[:, 0:n], func=AF.Sqrt, bias=eps_t, scale=64.0)
                        P.dve.reciprocal(out=rs[:, 0:n], in_=rs[:, 0:n])
                        P.dve.tensor_tensor(out=tg[2][:, 0:n], in0=t1[:, 0:n], in1=rs[:, 0:n], op=ALU.mult)
                        t2 = tf.next()
                        P.pool.tensor_scalar(out=t2[:, 0:n], in0=tg[1][:, 0:n], scalar1=V(f"ka{i}", c, 1), scalar2=omka[:, c:c + 1], op0=ALU.mult, op1=ALU.add)
                        P.dve.tensor_tensor(out=tg[3][:, 0:n], in0=k_c, in1=t2[:, 0:n], op=ALU.mult)
                        rkb = tb.next()
                        P.dve.scalar_tensor_tensor(out=rkb[:, 0:n], in0=r_c, scalar=V(f"rk{i}", c, 1), in1=tg[3][:, 0:n], op0=ALU.mult, op1=ALU.mult)
                        ps = pa.next()
                        P.pe.matmul(out=ps[:, 0:n], lhsT=BO64_b, rhs=rkb[:, 0:n], start=True, stop=True)
                        P.dve.scalar_tensor_tensor(out=bonus_t[:, c, 0:n], in0=ps[:, 0:n], scalar=64.0, in1=v_c, op0=ALU.mult, op1=ALU.mult)
                        P.dve.tensor_tensor_scan(out=tg[4][:, 0:n], data0=Rm[:, 0:n], data1=tg[0][:, 0:n], initial=0.0, op0=ALU.mult, op1=ALU.add)
                        P.act.activation(out=eG_t[:, c, 0:n], in_=tg[4][:, 0:n], func=AF.Exp)
                        enG = tg[5]
                        P.act.activation(out=enG[:, 0:n], in_=tg[4][:, 0:n], func=AF.Exp, scale=-1.0)
                        gm = tg[6]
                        P.pool.tensor_tensor(out=gm[:, 0:n], in0=tg[4][:, 0:n], in1=tg[0][:, 0:n], op=ALU.subtract)
                        P.act.activation(out=gm[:, 0:n], in_=gm[:, 0:n], func=AF.Exp)
                        AR4 = AR[:, c, 0:2 * n].rearrange("p (q a t) -> p q a t", a=2, t=Cc)
                        BK4 = BK[:, c, 0:2 * n].rearrange("p (q a t) -> p q a t", a=2, t=Cc)
                        v3 = lambda vv: vv.rearrange("p (q t) -> p q t", t=Cc)
                        P.dve.scalar_tensor_tensor(out=AR4[:, :, 0, :], in0=v3(tg[2][:, 0:n]), scalar=-1.0, in1=v3(gm[:, 0:n]), op0=ALU.mult, op1=ALU.mult)
                        P.pool.tensor_tensor(out=AR4[:, :, 1, :], in0=v3(r_c), in1=v3(eG_t[:, c, 0:n]), op=ALU.mult)
                        bt = tg[7]
                        P.pool.tensor_tensor(out=bt[:, 0:n], in0=tg[2][:, 0:n], in1=tg[1][:, 0:n], op=ALU.mult)
                        P.dve.tensor_tensor(out=BK4[:, :, 0, :], in0=v3(bt[:, 0:n]), in1=v3(enG[:, 0:n]), op=ALU.mult)
                        P.pool.tensor_tensor(out=BK4[:, :, 1, :], in0=v3(tg[3][:, 0:n]), in1=v3(enG[:, 0:n]), op=ALU.mult)
                        P.act.activation(out=v_b[:, c, 0:n], in_=v_c, func=AF.Copy)
                    for ch in range(nch):
                        seq = tk.seqs[ch] if tk.sample else tk.seqs[0]
                        HF = [Hf[grp, seq, hp] for hp in range(4)]; HB = [Hb[grp, seq, hp] for hp in range(4)]
                        pb_ = PB2[ch % 2]
                        tokm, NBs, NKs, Nm, Xf, Xb, Yb, tH = (pb_[k_] for k_ in ("tokm", "NBs", "NKs", "Nm", "Xf", "Xb", "Yb", "tH"))
                        ARc = lambda hp: AR[:, hp, 0:2 * n].rearrange("p (q a t) -> p q a t", a=2, t=Cc)[:, ch]
                        BKc = lambda hp: BK[:, hp, 0:2 * n].rearrange("p (q a t) -> p q a t", a=2, t=Cc)[:, ch]
                        rows = [slice(0, 64), slice(64, 128)]
                        orow = [slice(0, Cc), slice(Cc, C2c)]
                        tsl = slice(ch * Cc, (ch + 1) * Cc)
                        for hp in range(4):
                            ps = pa.next()
                            for hh in range(2):
                                P.pe.matmul(out=ps[orow[hh], 0:64], lhsT=BKc(hp)[rows[hh], 0, :], rhs=Ib[rows[hh], rows[hh]], start=True, stop=True)
                                P.pe.matmul(out=ps[orow[hh], 64:128], lhsT=BKc(hp)[rows[hh], 1, :], rhs=Ib[rows[hh], rows[hh]], start=True, stop=True)
                                P.pe.matmul(out=ps[orow[hh], 128:192], lhsT=v_b[rows[hh], hp, tsl], rhs=Ib[rows[hh], rows[hh]], start=True, stop=True)
                            P.act.activation(out=tokm[hp][0:C2c, :], in_=ps[0:C2c, 0:192], func=AF.Copy)
                        for hp in range(4):
                            for which, dst in ((0, NBs), (1, NKs)):
                                ps = pa.next()
                                for hh in range(2):
                                    for a_ in range(2):
                                        P.pe.matmul(out=ps[orow[hh], a_ * C2c + hh * Cc:a_ * C2c + (hh + 1) * Cc], lhsT=BKc(hp)[rows[hh], which, :],
                                                    rhs=ARc(hp)[rows[hh], a_, :], start=True, stop=True)
                                P.dve.tensor_tensor(out=dst[hp][0:C2c, 0:2 * C2c], in0=ps[0:C2c, 0:2 * C2c], in1=MSI[0:C2c, :], op=ALU.mult)
                        for hp in range(4):
                            P.pool.tensor_copy(out=Nm[0][hp][0:C2c, 0:C2c], in_=NBs[hp][0:C2c, 0:C2c])
                            ps = pa.next()
                            P.pe.matmul(out=ps[0:C2c, 0:C2c], lhsT=NBs[hp][0:C2c, 0:C2c], rhs=Ib[0:C2c, 0:C2c], start=True, stop=True)
                            P.act.activation(out=Lp[0][hp][0:C2c, 0:C2c], in_=ps[0:C2c, 0:C2c], func=AF.Copy)
                        for m in range(NM - 1):
                            for hp in range(4):
                                ps = pa.next()
                                P.pe.matmul(out=ps[0:C2c, 0:C2c], lhsT=Lp[m][hp][0:C2c, 0:C2c], rhs=Nm[m][hp][0:C2c, 0:C2c], start=True, stop=True)
                                P.dve.tensor_copy(out=Nm[m + 1][hp][0:C2c, 0:C2c], in_=ps[0:C2c, 0:C2c])
                                if m < NM - 2:
                                    ps = pa.next()
                                    P.pe.matmul(out=ps[0:C2c, 0:C2c], lhsT=Nm[m][hp][0:C2c, 0:C2c], rhs=Lp[m][hp][0:C2c, 0:C2c], start=True, stop=True)
                                    P.act.activation(out=Lp[m + 1][hp][0:C2c, 0:C2c], in_=ps[0:C2c, 0:C2c], func=AF.Copy)
                        for hp in range(4):
                            ps = pa.next()
                            P.pe.matmul(out=ps[0:C2c, 0:64], lhsT=NKs[hp][0:C2c, 0:C2c], rhs=tokm[hp][0:C2c, 128:192], start=True, stop=False, skip_group_check=True)
                            for hh in range(2):
                                P.pe.matmul(out=ps[orow[hh], 0:64], lhsT=ARc(hp)[rows[hh], 0, :], rhs=HB[hp][rows[hh], :], start=False, stop=(hh == 1), skip_group_check=True)
                            P.act.activation(out=Xf[hp][0:C2c, :], in_=ps[0:C2c, 0:64], func=AF.Copy)
                            P.dve.tensor_copy(out=Xb[hp][0:C2c, :], in_=Xf[hp][0:C2c, :])
                        for m in range(NM):
                            for hp in range(4):
                                ps = pa.next()
                                P.pe.matmul(out=ps[0:C2c, 0:64], lhsT=Nm[m][hp][0:C2c, 0:C2c], rhs=Xb[hp][0:C2c, :], start=True, stop=True)
                                P.dve.tensor_tensor(out=Xf[hp][0:C2c, :], in0=ps[0:C2c, 0:64], in1=Xf[hp][0:C2c, :], op=ALU.add)
                                P.act.activation(out=Xb[hp][0:C2c, :], in_=Xf[hp][0:C2c, :], func=AF.Copy)
                        for hp in range(4):
                            ps = pa.next()
                            P.pe.matmul(out=ps[0:C2c, 0:64], lhsT=NBs[hp][0:C2c, C2c:2 * C2c], rhs=Xb[hp][0:C2c, :], start=True, stop=False, skip_group_check=True)
                            P.pe.matmul(out=ps[0:C2c, 0:64], lhsT=NKs[hp][0:C2c, C2c:2 * C2c], rhs=tokm[hp][0:C2c, 128:192], start=False, stop=False, skip_group_check=True)
                            for hh in range(2):
                                P.pe.matmul(out=ps[orow[hh], 0:64], lhsT=ARc(hp)[rows[hh], 1, :], rhs=HB[hp][rows[hh], :], start=False, stop=(hh == 1), skip_group_check=True)
                            P.act.activation(out=Yb[hp][0:C2c, :], in_=ps[0:C2c, 0:64], func=AF.Copy)
                        for hp in range(4):
                            ps = pa.next()
                            for hh in range(2):
                                P.pe.matmul(out=ps[rows[hh], 0:Cc], lhsT=Yb[hp][orow[hh], :], rhs=Ib[orow[hh], orow[hh]], start=True, stop=True)
                            P.dve.tensor_copy(out=yT_t[:, hp, tsl], in_=ps[:, 0:Cc])
                        for hp in range(4):
                            ps = pa.next()
                            for hh in range(2):
                                P.pe.matmul(out=ps[rows[hh], 0:64], lhsT=tokm[hp][orow[hh], 0:64], rhs=Xb[hp][orow[hh], :], start=True, stop=False, skip_group_check=True)
                                P.pe.matmul(out=ps[rows[hh], 0:64], lhsT=tokm[hp][orow[hh], 64:128], rhs=tokm[hp][orow[hh], 128:192], start=False, stop=True, skip_group_check=True)
                            P.dve.tensor_tensor(out=tH[hp], in0=ps[:, 0:64], in1=HF[hp], op=ALU.add)
                            P.pool.tensor_scalar(out=HF[hp], in0=tH[hp], scalar1=eG_t[:, hp, ch * Cc + Cc - 1:ch * Cc + Cc], scalar2=None, op0=ALU.mult)
                            P.act.activation(out=HB[hp], in_=HF[hp], func=AF.Copy)
                    for hp in range(4):
                        yb = tb.next(); sq = tb.next()
                        P.act.activation(out=yb[:, 0:n], in_=yT_t[:, hp, 0:n], func=AF.Copy)
                        P.act.activation(out=sq[:, 0:n], in_=yT_t[:, hp, 0:n], func=AF.Square)
                        pm_ = pa.next(); pe_ = pa.next()
                        P.pe.matmul(out=pm_[:, 0:n], lhsT=BO64_b, rhs=yb[:, 0:n], start=True, stop=True)
                        P.pe.matmul(out=pe_[:, 0:n], lhsT=BO64_b, rhs=sq[:, 0:n], start=True, stop=True)
                        ms = tf.next(); m2 = tf.next(); var = tf.next()
                        P.act.activation(out=ms[:, 0:n], in_=pm_[:, 0:n], func=AF.Copy)
                        P.pool.tensor_tensor(out=m2[:, 0:n], in0=ms[:, 0:n], in1=ms[:, 0:n], op=ALU.mult)
                        P.dve.tensor_tensor(out=var[:, 0:n], in0=pe_[:, 0:n], in1=m2[:, 0:n], op=ALU.subtract)
                        P.dve.tensor_scalar(out=var[:, 0:n], in0=var[:, 0:n], scalar1=0.0, scalar2=None, op0=ALU.max)
                        P.act.activation(out=var[:, 0:n], in_=var[:, 0:n], func=AF.Sqrt, bias=lneps, scale=1.0)
                        P.dve.reciprocal(out=var[:, 0:n], in_=var[:, 0:n])
                        yc = tf.next()
                        P.pool.tensor_tensor(out=yc[:, 0:n], in0=yT_t[:, hp, 0:n], in1=ms[:, 0:n], op=ALU.subtract)
                        P.dve.tensor_tensor(out=yc[:, 0:n], in0=yc[:, 0:n], in1=var[:, 0:n], op=ALU.mult)
                        P.pool.tensor_scalar(out=yc[:, 0:n], in0=yc[:, 0:n], scalar1=V(f"lng{i}", hp, 1), scalar2=V(f"lnb{i}", hp, 1), op0=ALU.mult, op1=ALU.add)
                        P.dve.tensor_tensor(out=yc[:, 0:n], in0=yc[:, 0:n], in1=bonus_t[:, hp, 0:n], op=ALU.add)
                        P.dve.tensor_tensor(out=mixT[:, 4 + hp, 0:n], in0=yc[:, 0:n], in1=g_t[:, hp, 0:n], op=ALU.mult)
                    P.sp.dma_start(mixT[:, 0:4, 0:n], View(omla_d[ti][:, :, 0:n], omla_res[ti]))
                    for oc in range(KC):
                        d_ps = pa.next()
                        for c in range(KC):
                            P.pe.matmul(out=d_ps[:, 0:n], lhsT=wout[:, c, oc * 128:(oc + 1) * 128], rhs=mixT[:, c, 0:n], start=(c == 0), stop=(c == KC - 1))
                        for si, row in enumerate(tk.rows):
                            sl = slice(si * L, (si + 1) * L)
                            P.dve.scalar_tensor_tensor(out=XH["x"][:, oc, sl], in0=d_ps[:, sl], scalar=mod[l][:, 16 + oc, row:row + 1],
                                                       in1=XH["x"][:, oc, sl], op0=ALU.mult, op1=ALU.add)
                    store_x(ti, tk, False)
                P.sp.dma_start(View(o_shift_p[i], Res("o")), shift_p, final=True)
                P.sp.dma_start(View(o_shift_s[i], Res("o")), shift_s, final=True)
                for sq_ in range(NB):
                    for hp in range(4):
                        P.sp.dma_start(View(o_wkv_p[i, sq_, hp], Res("o")), Hf["p", sq_, hp], final=True)
                        P.sp.dma_start(View(o_wkv_s[i, sq_, hp], Res("o")), Hf["s", sq_, hp], final=True)
        if DO_MIX and l % 2 == 1:
            i = l // 2
            P.barrier()
            with P.scope():
                Pm_b, BO96_b, BO64_b, Ib = load_consts([n_ for n_ in CST2 if not n_.startswith("MSI6") and not n_.startswith("MSI3") and n_ not in ("R64", "R32")])
                wino = P.sbuf("wino", [128, KC, 2568], BF16)
                for k in range(KC):
                    P.pool.dma_start(wino[:, k, :], win_o[i][k * 128:(k + 1) * 128, :])
                poolw = P.sbuf("poolw", [128, 4, 128], BF16)
                for gi in range(4):
                    P.pool.dma_start(poolw[:, gi, :], poolw_d[i, gi])
                wout = P.sbuf("wouto", [128, KC, D], BF16)
                for k in range(KC):
                    P.pool.dma_start(wout[:, k, :], wout_o[i][k * 128:(k + 1) * 128, :])
                one_t = P.sbuf("one_t", [128, 1]); P.dve.memset(one_t, 1.0)
                negA = P.sbuf("negA", [128, 1])
                P.act.activation(out=negA, in_=V(f"alog{i}"), func=AF.Exp)
                P.dve.tensor_scalar(out=negA, in0=negA, scalar1=-1.0, scalar2=None, op0=ALU.mult)
                ones128 = P.sbuf("ones128o", [128, 128], BF16); P.dve.memset(ones128, 1.0 / 128)
                phist = {"p": P.sbuf("phist_p", [128, 4, NB, 15]), "s": P.sbuf("phist_s", [128, 4, NB, 15])}
                chist = {"p": P.sbuf("chist_p", [128, 12, NB, 3]), "s": P.sbuf("chist_s", [128, 12, NB, 3])}
                P.dve.memset(phist["p"], 0.0); P.dve.memset(chist["p"], 0.0)
                P.sp.dma_start(phist["s"], pool_in[i]); P.sp.dma_start(chist["s"], gconv_in[i])
                Sf = {}; Sb = {}
                for grp in ("p", "s"):
                    for sq_ in range(NB):
                        for h in range(4):
                            Sf[grp, sq_, h] = P.sbuf(f"Sf{grp}{sq_}{h}", [128, 128])
                            if grp == "p":
                                P.pool.memset(Sf[grp, sq_, h], 0.0)
                            else:
                                P.sp.dma_start(Sf[grp, sq_, h], gdn_in[i, sq_, h])
                Sbc = [P.sbuf(f"Sbc{h}", [128, 128], BF16) for h in range(4)]
                mixT = P.sbuf("mixTo", [128, KC, TT], BF16)
                zs_t = P.sbuf("zs_t", [128, 4, TT]); oT_t = P.sbuf("oT_t", [128, 4, TT])
                KQ = P.sbuf("KQ", [128, 4, 2 * TT], BF16); k_b = P.sbuf("k_b", [128, 4, TT], BF16); v_b = P.sbuf("v_bo", [128, 4, TT], BF16)
                sig8 = P.sbuf("sig8", [8, TT]); g8 = P.sbuf("g8", [8, TT])
                ubs = Rot([P.sbuf(f"ub{j}", [128, TT + 15 * NB]) for j in range(2)])
                sAB = [P.sbuf(f"sAB{j}", [128, TT + 15 * NB]) for j in range(2)]
                cbs = Rot([P.sbuf(f"cbf{j}", [128, TT + 3 * NB]) for j in range(2)])
                tf = Rot([P.sbuf(f"tfo{j}", [128, TT]) for j in range(6)])
                tb = Rot([P.sbuf(f"tbo{j}", [128, TT], BF16) for j in range(3)])
                pa = Rot(psb[0:7])

                def mk(name, shape, dt):
                    return [P.sbuf(f"{name}{hp}", shape, dt) for hp in range(2)]
                def mk2(name, shape, dt):
                    return [[P.sbuf(f"{name}{par_}_{hp}", shape, dt) for hp in range(2)] for par_ in range(2)]
                G2 = dict(cols_s=mk2("cols", [128, 3], F32), ghl=mk2("ghl", [128, 2], BF16), TGh=mk2("TGh", [128, 128], BF16), TGl=mk2("TGl", [128, 128], BF16))
                cbo = {}
                for nm in ("OH8", "ONB64", "NONB64", "TRI64", "STRI64", "BLK64_0", "BLK64_1", "ONB32", "NONB32", "TRI32", "STRI32", "BLK32_0", "BLK32_1",
                           "SEL8_0", "SEL8_1", "SEL8_2", "SEL8_3"):
                    cbo[nm] = P.sbuf("cbo_" + nm, [128, CST2[nm][1]], BF16)
                    P.dve.tensor_copy(out=cbo[nm], in_=C2(nm))
                sig8b = P.sbuf("sig8b", [8, TT], BF16); g8h = P.sbuf("g8h", [8, TT], BF16); g8l = P.sbuf("g8l", [8, TT], BF16); g8r = P.sbuf("g8r", [8, TT])
                G2.update(dict(dmat=mk2("dmat", [128, 128], F32), EM=mk2("EM", [128, 256], F32), eGcol=mk2("eGcol", [128, 1], F32),
                               wcol=mk2("wcol", [128, 1], F32), egc0=mk2("egc0_", [128, 1], F32), egc1=mk2("egc1_", [128, 1], F32),
                               NQs=mk2("NQs", [128, 256], BF16), Nm0=mk2("oNm0_", [128, 128], BF16), Lp0=mk2("oLp0_", [128, 128], BF16),
                               KW=mk2("KW", [128, 128], BF16), tks=mk2("tks", [128, 128], F32), Xf=mk2("oXf", [128, 128], F32), Xb=mk2("oXb", [128, 128], BF16),
                               tqs=mk2("tqs", [128, 128], F32), o_b=mk2("o_b", [128, 128], BF16),
                               Tm=mk2("Tm", [128, 128], BF16), TTm=mk2("TTm", [128, 128], BF16), Wb=mk2("Wb", [128, 128], BF16), Ub=mk2("Ub", [128, 128], BF16)))

                def proj(c0, m, n):
                    ps = pa.next()
                    for k in range(KC):
                        P.pe.matmul(out=ps[0:m, 0:n], lhsT=wino[:, k, c0:c0 + m], rhs=XH["h"][:, k, 0:n], start=(k == 0), stop=(k == KC - 1))
                    return ps

                for ti, tk in enumerate(tiles):
                    use_buf(ti % 2)
                    n, L, ns = tk.TT, tk.L, tk.nseg
                    grp = "s" if tk.sample else "p"
                    Cc = 32 if tk.sample else 64
                    nch = n // Cc
                    C2c = 2 * Cc
                    NM = 5 if tk.sample else 6
                    s0 = tk.seqs[0]
                    load_x(ti, tk, False)
                    modulate(tk, modA_m[l], mod[l][:, 0:8, :])
                    if ODD_STAGE < 9:
                        P.dve.memset(mixT[:, :, 0:n], 0.0); P.dve.memset(oT_t[:, :, 0:n], 0.0)
                    for gi, w in enumerate((2, 4, 8, 16) if ODD_STAGE >= 1 else ()):
                        ps = proj(gi * 128, 128, n)
                        ub = ubs.next()
                        u3 = ub[:, 0:ns * (L + 15)].rearrange("p (s t) -> p s t", s=ns)
                        P.pool.tensor_copy(out=u3[:, :, 0:15], in_=phist[grp][:, gi, s0:s0 + ns, :])
                        P.act.activation(out=u3[:, :, 15:15 + L], in_=ps[:, 0:n].rearrange("p (s t) -> p s t", s=ns), func=AF.Copy)
                        P.pool.tensor_copy(out=phist[grp][:, gi, s0:s0 + ns, :], in_=u3[:, :, L:L + 15])
                        cur = u3
                        sh = 1
                        for lev in range(gi + 1):
                            nxt = sAB[lev % 2][:, 0:ns * (L + 15)].rearrange("p (s t) -> p s t", s=ns)
                            lo = 2 * sh - 1
                            eng = P.dve if lev % 2 == 0 else P.pool
                            eng.tensor_tensor(out=nxt[:, :, lo:L + 15], in0=cur[:, :, lo:L + 15], in1=cur[:, :, lo - sh:L + 15 - sh], op=ALU.add)
                            cur = nxt
                            sh *= 2
                        df = tb.next()
                        d3 = df[:, 0:n].rearrange("p (s t) -> p s t", s=ns)
                        P.dve.scalar_tensor_tensor(out=d3, in0=cur[:, :, 15:15 + L], scalar=1.0 / w, in1=u3[:, :, 15:15 + L], op0=ALU.mult, op1=ALU.subtract)
                        if tk.first and not tk.sample:
                            t_ = tf.next()
                            P.dve.tensor_tensor(out=t_[:, 0:15], in0=cur[:, 0, 15:30], in1=C2("ICT", gi * 15, 15), op=ALU.mult)
                            P.dve.tensor_tensor(out=df[:, 0:15], in0=t_[:, 0:15], in1=u3[:, 0, 15:30], op=ALU.subtract)
                        ps = pa.next()
                        P.pe.matmul(out=ps[:, 0:n], lhsT=poolw[:, gi, :], rhs=df[:, 0:n], start=True, stop=True)
                        P.dve.tensor_scalar(out=mixT[:, gi, 0:n], in0=ps[:, 0:n], scalar1=V(f"pscale{i}", gi, 1), scalar2=None, op0=ALU.mult)
                    if ODD_STAGE < 2:
                        continue_ = True
                    ps = proj(2560, 8, n)
                    P.act.activation(out=sig8[:, 0:n], in_=ps[0:8, 0:n], func=AF.Sigmoid)
                    e8 = tf.next()
                    P.act.activation(out=e8[0:8, 0:n], in_=ps[0:8, 0:n], func=AF.Exp, bias=V(f"dtb{i}")[0:8, :], scale=1.0)
                    P.act.activation(out=e8[0:8, 0:n], in_=e8[0:8, 0:n], func=AF.Ln, bias=one_t[0:8, :], scale=1.0)
                    P.dve.tensor_scalar(out=g8[:, 0:n], in0=e8[0:8, 0:n], scalar1=negA[0:8, :], scalar2=None, op0=ALU.mult)
                    P.act.activation(out=sig8b[:, 0:n], in_=sig8[:, 0:n], func=AF.Copy)
                    P.act.activation(out=g8h[:, 0:n], in_=g8[:, 0:n], func=AF.Copy)
                    P.dve.tensor_tensor(out=g8r[:, 0:n], in0=g8[:, 0:n], in1=g8h[:, 0:n], op=ALU.subtract)
                    P.act.activation(out=g8l[:, 0:n], in_=g8r[:, 0:n], func=AF.Copy)
                    for j in range(12):
                        ps = proj(512 + j * 128, 128, n)
                        cbf = cbs.next()
                        c3 = cbf[:, 0:ns * (L + 3)].rearrange("p (s t) -> p s t", s=ns)
                        P.pool.tensor_copy(out=c3[:, :, 0:3], in_=chist[grp][:, j, s0:s0 + ns, :])
                        P.act.activation(out=c3[:, :, 3:3 + L], in_=ps[:, 0:n].rearrange("p (s t) -> p s t", s=ns), func=AF.Copy)
                        P.pool.tensor_copy(out=chist[grp][:, j, s0:s0 + ns, :], in_=c3[:, :, L:L + 3])
                        acc = tf.next()
                        a3 = acc[:, 0:n].rearrange("p (s t) -> p s t", s=ns)
                        P.act.activation(out=a3, in_=c3[:, :, 0:L], func=AF.Copy, scale=V(f"gcw{i}_0", j, 1))
                        for t in range(1, 4):
                            P.dve.scalar_tensor_tensor(out=a3, in0=c3[:, :, t:t + L], scalar=V(f"gcw{i}_{t}", j, 1), in1=a3, op0=ALU.mult, op1=ALU.add)
                        sl_ = tf.next()
                        P.act.activation(out=sl_[:, 0:n], in_=acc[:, 0:n], func=AF.Silu)
                        h = j % 4
                        if j < 8:
                            sq = tb.next()
                            P.act.activation(out=sq[:, 0:n], in_=sl_[:, 0:n], func=AF.Square)
                            ps2 = pa.next()
                            P.pe.matmul(out=ps2[:, 0:n], lhsT=ones128, rhs=sq[:, 0:n], start=True, stop=True)
                            rs = tf.next()
                            P.act.activation(out=rs[:, 0:n], in_=ps2[:, 0:n], func=AF.Sqrt, bias=eps_t, scale=128.0)
                            P.dve.reciprocal(out=rs[:, 0:n], in_=rs[:, 0:n])
                            if j < 4:
                                KQ4 = KQ[:, h, 0:2 * n].rearrange("p (q a t) -> p q a t", a=2, t=Cc)
                                P.dve.scalar_tensor_tensor(out=KQ4[:, :, 1, :], in0=sl_[:, 0:n].rearrange("p (q t) -> p q t", t=Cc), scalar=float(128 ** -0.5),
                                                           in1=rs[:, 0:n].rearrange("p (q t) -> p q t", t=Cc), op0=ALU.mult, op1=ALU.mult)
                            else:
                                P.dve.tensor_tensor(out=k_b[:, h, 0:n], in0=sl_[:, 0:n], in1=rs[:, 0:n], op=ALU.mult)
                                psb_ = pa.next()
                                P.pe.matmul(out=psb_[:, 0:n], lhsT=cbo[f"SEL8_{h}"][0:8, :], rhs=sig8b[0:8, 0:n], start=True, stop=True)
                                KQ4 = KQ[:, h, 0:2 * n].rearrange("p (q a t) -> p q a t", a=2, t=Cc)
                                P.dve.tensor_tensor(out=KQ4[:, :, 0, :], in0=psb_[:, 0:n].rearrange("p (q t) -> p q t", t=Cc),
                                                    in1=k_b[:, h, 0:n].rearrange("p (q t) -> p q t", t=Cc), op=ALU.mult)
                        else:
                            P.act.activation(out=v_b[:, h, 0:n], in_=sl_[:, 0:n], func=AF.Copy)
                    for h in range(4):
                        ps = proj(2048 + h * 128, 128, n)
                        P.act.activation(out=zs_t[:, h, 0:n], in_=ps[:, 0:n], func=AF.Silu)
                    MSIN = C2(f"MSIN{Cc}"); TRI = C2(f"TRI{Cc}")
                    orow = [slice(0, Cc), slice(Cc, C2c)]
                    for ch in range(nch if ODD_STAGE >= 3 else 0):
                        seq = tk.seqs[ch] if tk.sample else tk.seqs[0]
                        tsl = slice(ch * Cc, (ch + 1) * Cc)
                        KQc = lambda h: KQ[:, h, 0:2 * n].rearrange("p (q a t) -> p q a t", a=2, t=Cc)[:, ch]
                        if tk.sample or (tk.first and ch == 0):
                            for h_ in range(4):
                                P.act.activation(out=Sbc[h_], in_=Sf[grp, seq, h_], func=AF.Copy)
                        gq_ = {k_: v_[ch % 2] for k_, v_ in G2.items()}
                        cols_s, ghl, TGh, TGl, dmat, EM, eGcol, wcol = (gq_[k_] for k_ in ("cols_s", "ghl", "TGh", "TGl", "dmat", "EM", "eGcol", "wcol"))
                        egc = [gq_["egc0"], gq_["egc1"]]
                        NQs, KW, tks, Xf, Xb, tqs, o_b, Tm, TTm, Wb, Ub = (gq_[k_] for k_ in ("NQs", "KW", "tks", "Xf", "Xb", "tqs", "o_b", "Tm", "TTm", "Wb", "Ub"))
                        Nm = [gq_["Nm0"]]; Lp = [gq_["Lp0"]]
                        for hp in range(2 if SUB >= 1 else 0):
                            ps = pa.next()
                            for hh in range(2):
                                h = 2 * hp + hh
                                P.pe.matmul(out=ps[orow[hh], 0:1], lhsT=sig8b[0:8, tsl], rhs=cbo["OH8"][0:8, h:h + 1], start=True, stop=True)
                                P.pe.matmul(out=ps[orow[hh], 1:2], lhsT=g8h[0:8, tsl], rhs=cbo["OH8"][0:8, 4 + h:5 + h], start=True, stop=True)
                                P.pe.matmul(out=ps[orow[hh], 2:3], lhsT=g8l[0:8, tsl], rhs=cbo["OH8"][0:8, 4 + h:5 + h], start=True, stop=True)
                            P.act.activation(out=cols_s[hp][0:C2c, 0:3], in_=ps[0:C2c, 0:3], func=AF.Copy)
                            P.dve.tensor_copy(out=ghl[hp][0:C2c, :], in_=cols_s[hp][0:C2c, 1:3])
                            P.pool.tensor_scalar(out=TGh[hp][0:C2c, 0:C2c], in0=TRI[0:C2c, :], scalar1=cols_s[hp][0:C2c, 1:2], scalar2=None, op0=ALU.mult)
                            P.pool.tensor_scalar(out=TGl[hp][0:C2c, 0:C2c], in0=TRI[0:C2c, :], scalar1=cols_s[hp][0:C2c, 2:3], scalar2=None, op0=ALU.mult)
                            ps = pa.next()
                            P.pe.matmul(out=ps[0:C2c, 0:C2c], lhsT=cbo[f"ONB{Cc}"][0:C2c, 0:C2c], rhs=TGh[hp][0:C2c, 0:C2c], start=True, stop=False)
                            P.pe.matmul(out=ps[0:C2c, 0:C2c], lhsT=cbo[f"ONB{Cc}"][0:C2c, 0:C2c], rhs=TGl[hp][0:C2c, 0:C2c], start=False, stop=False)
                            P.pe.matmul(out=ps[0:C2c, 0:C2c], lhsT=TGh[hp][0:C2c, 0:C2c], rhs=cbo[f"NONB{Cc}"][0:C2c, 0:C2c], start=False, stop=False)
                            P.pe.matmul(out=ps[0:C2c, 0:C2c], lhsT=TGl[hp][0:C2c, 0:C2c], rhs=cbo[f"NONB{Cc}"][0:C2c, 0:C2c], start=False, stop=True)
                            P.dve.tensor_scalar(out=dmat[hp][0:C2c, 0:C2c], in0=ps[0:C2c, 0:C2c], scalar1=0.0, scalar2=None, op0=ALU.min)
                            P.act.activation(out=dmat[hp][0:C2c, 0:C2c], in_=dmat[hp][0:C2c, 0:C2c], func=AF.Exp)
                            P.dve.tensor_tensor(out=EM[hp][0:C2c, 0:2 * C2c].rearrange("p (a t) -> p a t", a=2), in0=MSIN[0:C2c, :].rearrange("p (a t) -> p a t", a=2),
                                                in1=dmat[hp][0:C2c, 0:C2c].bc3(1, 2), op=ALU.mult)
                            ps = pa.next()
                            for q_ in range(2):
                                P.pe.matmul(out=ps[0:C2c, 0:1], lhsT=cbo[f"TRI{Cc}"][0:C2c, 0:C2c], rhs=ghl[hp][0:C2c, q_:q_ + 1], start=(q_ == 0), stop=(q_ == 1))
                            ps2 = pa.next()
                            for q_ in range(2):
                                P.pe.matmul(out=ps2[0:C2c, 0:1], lhsT=cbo[f"STRI{Cc}"][0:C2c, 0:C2c], rhs=ghl[hp][0:C2c, q_:q_ + 1], start=(q_ == 0), stop=(q_ == 1))
                            P.act.activation(out=eGcol[hp][0:C2c, :], in_=ps[0:C2c, 0:1], func=AF.Exp)
                            P.act.activation(out=wcol[hp][0:C2c, :], in_=ps2[0:C2c, 0:1], func=AF.Exp)
                            for hh in range(2):
                                ps = pa.next()
                                for q_ in range(2):
                                    P.pe.matmul(out=ps[:, 0:1], lhsT=cbo[f"BLK{Cc}_{hh}"][0:C2c, :], rhs=ghl[hp][0:C2c, q_:q_ + 1], start=(q_ == 0), stop=(q_ == 1))
                                P.act.activation(out=egc[hh][hp], in_=ps[:, 0:1], func=AF.Exp)
                        for hp in range(2 if SUB >= 2 else 0):
                            ps = pa.next()
                            for hh in range(2):
                                h = 2 * hp + hh
                                for a_ in range(2):
                                    P.pe.matmul(out=ps[orow[hh], a_ * C2c + hh * Cc:a_ * C2c + (hh + 1) * Cc], lhsT=k_b[:, h, tsl], rhs=KQc(h)[:, a_, :], start=True, stop=True)
                            P.dve.tensor_tensor(out=NQs[hp][0:C2c, 0:2 * C2c], in0=ps[0:C2c, 0:2 * C2c], in1=EM[hp][0:C2c, 0:2 * C2c], op=ALU.mult)
                        for hp in range(2 if SUB >= 3 else 0):
                            P.pool.tensor_copy(out=Nm[0][hp][0:C2c, 0:C2c], in_=NQs[hp][0:C2c, 0:C2c])
                            ps = pa.next()
                            P.pe.matmul(out=ps[0:C2c, 0:C2c], lhsT=NQs[hp][0:C2c, 0:C2c], rhs=Ib[0:C2c, 0:C2c], start=True, stop=True)
                            P.act.activation(out=Lp[0][hp][0:C2c, 0:C2c], in_=ps[0:C2c, 0:C2c], func=AF.Copy)
                        LV = NM
                        for hp in range(2):
                            tmp_ = tks[hp]
                            P.dve.tensor_tensor(out=tmp_[0:C2c, 0:C2c], in0=Lp[0][hp][0:C2c, 0:C2c], in1=C2(f"MK{Cc}_0")[0:C2c, :], op=ALU.mult)
                            P.dve.tensor_tensor(out=Tm[hp][0:C2c, 0:C2c], in0=tmp_[0:C2c, 0:C2c], in1=C2("I128")[0:C2c, 0:C2c], op=ALU.add)
                            ps = pa.next()
                            P.pe.matmul(out=ps[0:C2c, 0:C2c], lhsT=Tm[hp][0:C2c, 0:C2c], rhs=Ib[0:C2c, 0:C2c], start=True, stop=True)
                            P.act.activation(out=TTm[hp][0:C2c, 0:C2c], in_=ps[0:C2c, 0:C2c], func=AF.Copy)
                        for lv in range(1, LV):
                            for hp in range(2):
                                ps = pa.next()
                                P.pe.matmul(out=ps[0:C2c, 0:C2c], lhsT=Nm[0][hp][0:C2c, 0:C2c], rhs=Tm[hp][0:C2c, 0:C2c], start=True, stop=True)
                                P.act.activation(out=Wb[hp][0:C2c, 0:C2c], in_=ps[0:C2c, 0:C2c], func=AF.Copy)
                                ps = pa.next()
                                P.pe.matmul(out=ps[0:C2c, 0:C2c], lhsT=TTm[hp][0:C2c, 0:C2c], rhs=Wb[hp][0:C2c, 0:C2c], start=True, stop=True)
                                tmp_ = tks[hp]
                                P.dve.tensor_tensor(out=tmp_[0:C2c, 0:C2c], in0=ps[0:C2c, 0:C2c], in1=C2(f"MK{Cc}_{lv}")[0:C2c, :], op=ALU.mult)
                                P.dve.tensor_tensor(out=Tm[hp][0:C2c, 0:C2c], in0=tmp_[0:C2c, 0:C2c], in1=Tm[hp][0:C2c, 0:C2c], op=ALU.add)
                                ps = pa.next()
                                P.pe.matmul(out=ps[0:C2c, 0:C2c], lhsT=Tm[hp][0:C2c, 0:C2c], rhs=Ib[0:C2c, 0:C2c], start=True, stop=True)
                                P.act.activation(out=TTm[hp][0:C2c, 0:C2c], in_=ps[0:C2c, 0:C2c], func=AF.Copy)
                        for hp in range(2 if SUB >= 5 else 0):
                            pT = pa.next()
                            for hh in range(2):
                                h = 2 * hp + hh
                                P.pe.matmul(out=pT[orow[hh], 0:128], lhsT=v_b[:, h, tsl], rhs=Ib, start=True, stop=True)
                                P.pe.matmul(out=pT[orow[hh], 128:256], lhsT=k_b[:, h, tsl], rhs=Ib, start=True, stop=True)
                            P.dve.tensor_scalar(out=KW[hp][0:C2c, :], in0=pT[0:C2c, 128:256], scalar1=wcol[hp][0:C2c, :], scalar2=None, op0=ALU.mult)
                            pK = pa.next()
                            for hh in range(2):
                                h = 2 * hp + hh
                                P.pe.matmul(out=pK[orow[hh], 0:128], lhsT=k_b[:, h, tsl], rhs=Sbc[h], start=True, stop=True)
                            P.dve.tensor_scalar(out=tks[hp][0:C2c, :], in0=pK[0:C2c, 0:128], scalar1=eGcol[hp][0:C2c, :], scalar2=None, op0=ALU.mult)
                            P.dve.tensor_tensor(out=Xf[hp][0:C2c, :], in0=pT[0:C2c, 0:128], in1=tks[hp][0:C2c, :], op=ALU.subtract)
                            P.pool.tensor_scalar(out=Xf[hp][0:C2c, :], in0=Xf[hp][0:C2c, :], scalar1=cols_s[hp][0:C2c, 0:1], scalar2=None, op0=ALU.mult)
                            P.act.activation(out=Xb[hp][0:C2c, :], in_=Xf[hp][0:C2c, :], func=AF.Copy)
                        for hp in range(2):
                            ps = pa.next()
                            P.pe.matmul(out=ps[0:C2c, 0:128], lhsT=TTm[hp][0:C2c, 0:C2c], rhs=Xb[hp][0:C2c, :], start=True, stop=True)
                            P.act.activation(out=Ub[hp][0:C2c, :], in_=ps[0:C2c, 0:128], func=AF.Copy)
                        for hp in range(2 if SUB >= 7 else 0):
                            pQ = pa.next()
                            for hh in range(2):
                                h = 2 * hp + hh
                                P.pe.matmul(out=pQ[orow[hh], 0:128], lhsT=KQc(h)[:, 1, :], rhs=Sbc[h], start=True, stop=True)
                            P.dve.tensor_scalar(out=tqs[hp][0:C2c, :], in0=pQ[0:C2c, 0:128], scalar1=eGcol[hp][0:C2c, :], scalar2=None, op0=ALU.mult)
                            pO = pa.next()
                            P.pe.matmul(out=pO[0:C2c, 0:128], lhsT=NQs[hp][0:C2c, C2c:2 * C2c], rhs=Ub[hp][0:C2c, :], start=True, stop=True)
                            P.dve.tensor_tensor(out=o_b[hp][0:C2c, :], in0=pO[0:C2c, 0:128], in1=tqs[hp][0:C2c, :], op=ALU.add)
                            if SUB >= 8:
                                for hh in range(2):
                                    pOT = pa.next()
                                    P.pe.matmul(out=pOT[:, 0:Cc], lhsT=o_b[hp][orow[hh], :], rhs=Ib[orow[hh], orow[hh]], start=True, stop=True)
                                    P.act.activation(out=oT_t[:, 2 * hp + hh, tsl], in_=pOT[:, 0:Cc], func=AF.Copy)
                            for hh in range(2 if SUB >= 9 else 0):
                                h = 2 * hp + hh
                                pS = pa.next()
                                P.pe.matmul(out=pS[:, 0:128], lhsT=KW[hp][orow[hh], :], rhs=Ub[hp][orow[hh], :], start=True, stop=True)
                                P.dve.scalar_tensor_tensor(out=Sf[grp, seq, h], in0=Sf[grp, seq, h], scalar=egc[hh][hp], in1=pS[:, 0:128], op0=ALU.mult, op1=ALU.add)
                                P.act.activation(out=Sbc[h], in_=Sf[grp, seq, h], func=AF.Copy)
                    for h in range(4):
                        sq = tb.next()
                        P.act.activation(out=sq[:, 0:n], in_=oT_t[:, h, 0:n], func=AF.Square)
                        ps = pa.next()
                        P.pe.matmul(out=ps[:, 0:n], lhsT=ones128, rhs=sq[:, 0:n], start=True, stop=True)
                        rs = tf.next()
                        P.act.activation(out=rs[:, 0:n], in_=ps[:, 0:n], func=AF.Sqrt, bias=eps_t, scale=1.0)
                        P.dve.reciprocal(out=rs[:, 0:n], in_=rs[:, 0:n])
                        t_ = tf.next()
                        P.dve.scalar_tensor_tensor(out=t_[:, 0:n], in0=oT_t[:, h, 0:n], scalar=V(f"gog{i}"), in1=rs[:, 0:n], op0=ALU.mult, op1=ALU.mult)
                        P.pool.tensor_tensor(out=mixT[:, 4 + h, 0:n], in0=t_[:, 0:n], in1=zs_t[:, h, 0:n], op=ALU.mult)
                    for oc in range(KC):
                        d_ps = pa.next()
                        for c in range(KC):
                            P.pe.matmul(out=d_ps[:, 0:n], lhsT=wout[:, c, oc * 128:(oc + 1) * 128], rhs=mixT[:, c, 0:n], start=(c == 0), stop=(c == KC - 1))
                        for si, row in enumerate(tk.rows):
                            sl = slice(si * L, (si + 1) * L)
                            P.dve.scalar_tensor_tensor(out=XH["x"][:, oc, sl], in0=d_ps[:, sl], scalar=mod[l][:, 16 + oc, row:row + 1],
                                                       in1=XH["x"][:, oc, sl], op0=ALU.mult, op1=ALU.add)
                    store_x(ti, tk, False)
                for grp, (op_, oc_, og_) in (("p", (o_pool_p, o_gconv_p, o_gdn_p)), ("s", (o_pool_s, o_gconv_s, o_gdn_s))):
                    P.sp.dma_start(View(op_[i], Res("o")), phist[grp], final=True)
                    P.sp.dma_start(View(oc_[i], Res("o")), chist[grp], final=True)
                    for sq_ in range(NB):
                        for h in range(4):
                            P.sp.dma_start(View(og_[i, sq_, h], Res("o")), Sf[grp, sq_, h], final=True)
        P.barrier()
        with P.scope():
            wg = P.sbuf("wg", [128, KC, DFF], BF16); wu = P.sbuf("wu", [128, KC, DFF], BF16)
            wd = P.sbuf("wd", [128, FC, D], BF16)
            for k in range(KC):
                P.pool.dma_start(wg[:, k, :], wg_d[l][k * 128:(k + 1) * 128, :])
                P.pool.dma_start(wu[:, k, :], wu_d[l][k * 128:(k + 1) * 128, :])
            for c in range(FC):
                P.pool.dma_start(wd[:, c, :], wd_d[l][c * 128:(c + 1) * 128, :])
            hist_p = P.sbuf("hist_p", [128, FC, NB, 2]); hist_s = P.sbuf("hist_s", [128, FC, NB, 2])
            P.dve.memset(hist_p, 0.0)
            P.sp.dma_start(hist_s, fh_d[l])
            act_t = P.sbuf("act_t", [128, FC, TTF], BF16)
            gbufs = Rot([P.sbuf(f"gbuf{i}", [128, TTF + 2 * NB]) for i in range(2)])
            accs = Rot([P.sbuf(f"acc{i}", [128, TTF]) for i in range(2)])
            gps = Rot(psb[0:3]); ups = Rot(psb[3:6]); dps = Rot(psb[0:6])
            first_layer = (l == 0 and not DO_MIX) or False
            for ti, tk in enumerate(ftiles):
                use_buf("full" if MK_NT % 2 == 0 else 0)
                n, L, ns = tk.TT, tk.L, tk.nseg
                load_x(ti, tk, l == 0 and not DO_MIX)
                modulate(tk, modA_f[l], mod[l][:, 24:32, :])
                hist = hist_s if tk.sample else hist_p
                s0 = tk.seqs[0]
                for c in range(FC):
                    g_ps = gps.next(); u_ps = ups.next()
                    for k in range(KC):
                        P.pe.matmul(out=g_ps[:, 0:n], lhsT=wg[:, k, c * 128:(c + 1) * 128], rhs=XH["h"][:, k, 0:n],
                                    start=(k == 0), stop=(k == KC - 1))
                    for k in range(KC):
                        P.pe.matmul(out=u_ps[:, 0:n], lhsT=wu[:, k, c * 128:(c + 1) * 128], rhs=XH["h"][:, k, 0:n],
                                    start=(k == 0), stop=(k == KC - 1))
                    gb = gbufs.next()
                    gb3 = gb[:, 0:ns * (L + 2)].rearrange("p (s t) -> p s t", s=ns)
                    P.pool.tensor_copy(out=gb3[:, :, 0:2], in_=hist[:, c, s0:s0 + ns, :])
                    P.act.activation(out=gb3[:, :, 2:L + 2], in_=g_ps[:, 0:n].rearrange("p (s t) -> p s t", s=ns), func=AF.Copy)
                    P.pool.tensor_copy(out=hist[:, c, s0:s0 + ns, :], in_=gb3[:, :, L:L + 2])
                    acc = accs.next()
                    a3 = acc[:, 0:n].rearrange("p (s t) -> p s t", s=ns)
                    P.act.activation(out=a3, in_=gb3[:, :, 0:L], func=AF.Copy, scale=V(f"fcw{l}_0", c, 1))
                    P.dve.scalar_tensor_tensor(out=a3, in0=gb3[:, :, 1:L + 1], scalar=V(f"fcw{l}_1", c, 1), in1=a3, op0=ALU.mult, op1=ALU.add)
                    P.dve.scalar_tensor_tensor(out=a3, in0=gb3[:, :, 2:L + 2], scalar=V(f"fcw{l}_2", c, 1), in1=a3, op0=ALU.mult, op1=ALU.add)
                    P.act.activation(out=acc[:, 0:n], in_=acc[:, 0:n], func=AF.Silu)
                    P.dve.tensor_tensor(out=act_t[:, c, 0:n], in0=u_ps[:, 0:n], in1=acc[:, 0:n], op=ALU.mult)
                for oc in range(KC):
                    d_ps = dps.next()
                    for c in range(FC):
                        P.pe.matmul(out=d_ps[:, 0:n], lhsT=wd[:, c, oc * 128:(oc + 1) * 128], rhs=act_t[:, c, 0:n],
                                    start=(c == 0), stop=(c == FC - 1))
                    for si, row in enumerate(tk.rows):
                        sl = slice(si * L, (si + 1) * L)
                        P.dve.scalar_tensor_tensor(out=XH["x"][:, oc, sl], in0=d_ps[:, sl], scalar=mod[l][:, 40 + oc, row:row + 1],
                                                   in1=XH["x"][:, oc, sl], op0=ALU.mult, op1=ALU.add)
                store_x(ti, tk, l == NLAYERS - 1)
            P.sp.dma_start(View(o_ffn_p[l], Res("o")), hist_p, final=True)
            P.sp.dma_start(View(o_ffn_s[l], Res("o")), hist_s, final=True)
    stats = P.finish()
    return nc, stats


_CACHE = {}


def kernel(**inputs):
    f32 = np.float32
    inp = {k: np.asarray(v) for k, v in inputs.items()}
    if "nc" not in _CACHE:
        _CACHE["nc"], _CACHE["stats"] = build_program()
    nc = _CACHE["nc"]
    vl = vec_layout(inp)
    vecs = vl.array()
    cst_np = const_array(); cst2_np = const_array2()
    B = inp["x_prompt"].shape[0]
    in_maps = []
    fh_all = inp["state_ffn_conv"]
    for c in range(NCORES):
        bs = slice(c * NB, (c + 1) * NB)
        m = {}
        m["xp"] = np.ascontiguousarray(inp["x_prompt"][bs].transpose(0, 2, 1).reshape(NB, KC, 128, SEQ), dtype=f32)
        m["xs"] = np.ascontiguousarray(inp["x_sample"][bs].transpose(0, 2, 1).reshape(NB, KC, 128, DSEQ), dtype=f32)
        cc = np.concatenate([inp["c_prompt"][bs], inp["c_sample"][bs]], axis=0)
        m["cT"] = np.ascontiguousarray(cc.T.reshape(KC, 128, 8).transpose(1, 0, 2), dtype=f32)
        m["vecs"] = vecs
        m["ada_w"] = inp["ada_w"]; m["ffn_w_gate"] = inp["ffn_w_gate"]; m["ffn_w_up"] = inp["ffn_w_up"]
        m["ffn_w_down"] = inp["ffn_w_down"]
        m["ffn_hist"] = np.ascontiguousarray(fh_all[:, bs].reshape(DEPTH, NB, 2, FC, 128).transpose(0, 4, 3, 1, 2), dtype=f32)
        m["consts"] = cst_np; m["consts2"] = cst2_np
        for k in ("rwkv_w2", "rwkv_a2", "rwkv_g2"):
            m[k] = inp[k]
        m["shift_in"] = np.ascontiguousarray(inp["state_rwkv_shift"][:, bs].reshape(2, NB, 14, 128).transpose(0, 3, 2, 1), dtype=f32)
        m["wkv_in"] = np.ascontiguousarray(inp["state_rwkv_wkv"][:, bs].reshape(2, NB, 4, 2, 64, 64).transpose(0, 1, 2, 3, 5, 4).reshape(2, NB, 4, 128, 64), dtype=f32)
        for k in ("even_w_in", "mla_w_uq", "mla_w_ukv", "even_w_out"):
            m[k] = inp[k]
        m["ckv_past"] = np.ascontiguousarray(inp["cache_mla_ckv"][:, bs].transpose(0, 1, 3, 2), dtype=f32)
        m["kpe_past"] = np.ascontiguousarray(inp["cache_mla_kpe"][:, bs].transpose(0, 1, 3, 2), dtype=f32)
        for k in ("odd_w_in", "pool_w", "odd_w_out"):
            m[k] = inp[k]
        m["pool_in"] = np.ascontiguousarray(inp["state_pool"][:, bs].reshape(2, NB, 15, 4, 128).transpose(0, 4, 3, 1, 2), dtype=f32)
        m["gconv_in"] = np.ascontiguousarray(inp["state_gdn_conv"][:, bs].reshape(2, NB, 3, 12, 128).transpose(0, 4, 3, 1, 2), dtype=f32)
        m["gdn_in"] = np.ascontiguousarray(inp["state_gdn"][:, bs], dtype=f32)
        in_maps.append(m)
    res = run_bass_kernel_spmd(nc, in_maps[:MK_CORES], core_ids=list(range(MK_CORES)))
    R = list(res.results) + [res.results[0]] * (NCORES - MK_CORES)

    def cat(name, fn):
        return np.concatenate([fn(np.asarray(r[name])) for r in R], axis=0)

    y_prompt = cat("yp", lambda a: a.reshape(NB, D, SEQ).transpose(0, 2, 1))
    y_sample = cat("ys", lambda a: a.reshape(NB, D, DSEQ).transpose(0, 2, 1))
    ffn_fix = lambda a: a.transpose(0, 3, 4, 2, 1).reshape(DEPTH, NB, 2, DFF)
    p_ffn = np.concatenate([ffn_fix(np.asarray(r["o_ffn_p"])) for r in R], axis=1)
    s_ffn = np.concatenate([ffn_fix(np.asarray(r["o_ffn_s"])) for r in R], axis=1)
    _CACHE["raw"] = R
    tr = lambda name: np.concatenate([np.asarray(r[name]).transpose(0, 1, 3, 2) for r in R], axis=1)
    p_ckv, p_kpe, s_ckv, s_kpe = tr("o_ckv_p"), tr("o_kpe_p"), tr("o_ckv_s"), tr("o_kpe_s")
    shf = lambda name: np.concatenate([np.asarray(r[name]).transpose(0, 3, 2, 1).reshape(2, NB, 1792) for r in R], axis=1)
    p_shift, s_shift = shf("o_shift_p"), shf("o_shift_s")
    wkvf = lambda name: np.concatenate([np.asarray(r[name]).reshape(2, NB, 4, 2, 64, 64).transpose(0, 1, 2, 3, 5, 4).reshape(2, NB, 8, 64, 64) for r in R], axis=1)
    p_wkv, s_wkv = wkvf("o_wkv_p"), wkvf("o_wkv_s")
    poolf = lambda name: np.concatenate([np.asarray(r[name]).transpose(0, 3, 4, 2, 1).reshape(2, NB, 15, 512) for r in R], axis=1)
    gcf = lambda name: np.concatenate([np.asarray(r[name]).transpose(0, 3, 4, 2, 1).reshape(2, NB, 3, 1536) for r in R], axis=1)
    gdf = lambda name: np.concatenate([np.asarray(r[name]) for r in R], axis=1)
    p_pool, s_pool, p_gc, s_gc, p_gdn, s_gdn = poolf("o_pool_p"), poolf("o_pool_s"), gcf("o_gconv_p"), gcf("o_gconv_s"), gdf("o_gdn_p"), gdf("o_gdn_s")
    z = lambda *s: np.zeros(s, f32)
    NE, NO = 2, 2
    outs = (np.ascontiguousarray(y_prompt, f32), np.ascontiguousarray(y_sample, f32),
            np.ascontiguousarray(p_ckv, f32), np.ascontiguousarray(p_kpe, f32), np.ascontiguousarray(p_shift, f32), np.ascontiguousarray(p_wkv, f32),
            np.ascontiguousarray(p_pool, f32), np.ascontiguousarray(p_gc, f32), np.ascontiguousarray(p_gdn, f32), np.ascontiguousarray(p_ffn, f32),
            np.ascontiguousarray(s_ckv, f32), np.ascontiguousarray(s_kpe, f32), np.ascontiguousarray(s_shift, f32), np.ascontiguousarray(s_wkv, f32),
            np.ascontiguousarray(s_pool, f32), np.ascontiguousarray(s_gc, f32), np.ascontiguousarray(s_gdn, f32), np.ascontiguousarray(s_ffn, f32))
    return outs
```
